# Optimizing a Trainium2 kernel written in Bass

```python
import math
import jax
import jax.numpy as jnp
from jax import lax
import numpy as np


D_MODEL = 1024
BATCH = 16
SEQ = 4096
DEPTH = 2

GRID_W = 64
CTX_LEN = 256
A_WIDTH = 512
A_GROUPS = 4
A_GROUP_DIM = A_WIDTH // A_GROUPS
A_CHUNK = 128
B_HEADS = 4
B_HEAD_DIM = 64
B_QK = B_HEADS * 2 * B_HEAD_DIM
B_V = B_HEADS * 2 * B_HEAD_DIM
Q_BLOCK = 128
ROPE_BASE = 10000.0
C_HEADS = 4
C_HEAD_DIM = 128
C_WIDTH = C_HEADS * C_HEAD_DIM
C_CONV = 3
GDN_CHUNK = 64
N_BRANCHES = 3
D_FF = 2816
FFN_CONV = 3
EPS = 1e-6
IN_SPLITS = (A_WIDTH, A_WIDTH, B_QK, B_QK, B_V, C_WIDTH, C_WIDTH, C_WIDTH, C_WIDTH,
             2 * C_HEADS, 2 * C_HEADS, N_BRANCHES * D_MODEL)
D_IN = sum(IN_SPLITS)

kernel_name = 'hybrid_gated_branch_diffusion_block'


def rms_norm(x, g):
    xf = x.astype(jnp.float32)
    y = xf * lax.rsqrt(jnp.mean(xf * xf, axis=-1, keepdims=True) + EPS)
    return (y * g.astype(jnp.float32)).astype(x.dtype)


def l2_normalize(x):
    return x * lax.rsqrt(jnp.sum(x * x, axis=-1, keepdims=True) + EPS)


def modulate(x, shift, scale):
    return x * (1 + scale) + shift


def dwconv_centred(x, w):
    pad = (w.shape[0] - 1) // 2
    return lax.conv_general_dilated(x, w[:, None, :].astype(x.dtype), window_strides=(1,),
                                    padding=[(pad, pad)], dimension_numbers=('NWC', 'WIO', 'NWC'),
                                    feature_group_count=x.shape[-1])


def split_in(p):
    idx = np.cumsum(IN_SPLITS)[:-1].tolist()
    return jnp.split(p, idx, axis=-1)


def axial_angles(rows):
    n_freq = B_HEAD_DIM // 4
    inv_freq = ROPE_BASE ** (-jnp.arange(n_freq, dtype=jnp.float32) / n_freq)
    row = jnp.repeat(jnp.arange(rows, dtype=jnp.float32), GRID_W)
    col = jnp.tile(jnp.arange(GRID_W, dtype=jnp.float32), rows)
    return row[:, None] * inv_freq, col[:, None] * inv_freq


def rope_half(x, ang):
    h = x.shape[-1] // 2
    cos = jnp.cos(ang).astype(x.dtype)
    sin = jnp.sin(ang).astype(x.dtype)
    x1, x2 = x[..., :h], x[..., h:]
    return jnp.concatenate([x1 * cos - x2 * sin, x2 * cos + x1 * sin], axis=-1)


def rope_axial(x, ang_r, ang_c):
    h = x.shape[-1] // 2
    return jnp.concatenate([rope_half(x[..., :h], ang_r), rope_half(x[..., h:], ang_c)], axis=-1)


def spatial_gating(u, v, norm_g, w_s, b_s):
    b, n, _ = v.shape
    u = jax.nn.gelu(u)
    v = rms_norm(jax.nn.gelu(v).reshape(b, n // A_CHUNK, A_CHUNK, A_GROUPS, A_GROUP_DIM), norm_g)
    mixed = jnp.einsum('gij,bnjgc->bnigc', w_s, v) + b_s.T[:, :, None]
    return u * mixed.reshape(b, n, A_WIDTH)


def diff_q(q, g):
    b, n, _ = q.shape
    return rms_norm(q.reshape(b, n, B_HEADS, 2, B_HEAD_DIM), g).transpose(0, 2, 3, 1, 4)


def diff_kv(k, v, g):
    b, n, _ = k.shape
    k = rms_norm(k.reshape(b, n, B_HEADS, 2, B_HEAD_DIM), g).transpose(0, 2, 3, 1, 4)
    v = v.reshape(b, n, B_HEADS, 2 * B_HEAD_DIM).transpose(0, 2, 1, 3)
    return k, v


def diff_softmax_combine(q, k, v, lam):
    s = jnp.einsum('bhmqd,bhmkd->bhmqk', q, k, preferred_element_type=jnp.float32) * B_HEAD_DIM ** -0.5
    p = jax.nn.softmax(s, axis=-1)
    a = p[:, :, 0] - lam * p[:, :, 1]
    return jnp.einsum('bhqk,bhkd->bhqd', a.astype(v.dtype), v)


def diff_attention_blocks(q, k_all, v_all, lam):
    b, h, _, n, hd = q.shape
    nblk = n // Q_BLOCK
    qb = q.reshape(b, h, 2, nblk, Q_BLOCK, hd).transpose(3, 0, 1, 2, 4, 5)
    o = lax.map(lambda qblk: diff_softmax_combine(qblk, k_all, v_all, lam), qb)
    return o.transpose(1, 2, 0, 3, 4).reshape(b, h, n, v_all.shape[-1])


def diff_output(o, subln_g, lam_init):
    b, h, n, dv = o.shape
    o = rms_norm(o, subln_g) * (1 - lam_init)
    return o.transpose(0, 2, 1, 3).reshape(b, n, h * dv)


def gdn_prepare(q, k, v, conv_w):
    b, n, _ = q.shape
    qkv = jax.nn.silu(dwconv_centred(jnp.concatenate([q, k, v], axis=-1), conv_w))
    heads = lambda t: t.reshape(b, n, C_HEADS, C_HEAD_DIM).transpose(0, 2, 1, 3).astype(jnp.float32)
    q, k, v = (heads(t) for t in jnp.split(qkv, 3, axis=-1))
    return l2_normalize(q), l2_normalize(k), v


def gdn_gates(beta_logits, a_logits, a_log, dt_bias):
    b, n, _ = beta_logits.shape
    per_dir = lambda t: t.astype(jnp.float32).reshape(b, n, 2, C_HEADS).transpose(2, 0, 3, 1)
    beta = jax.nn.sigmoid(per_dir(beta_logits))
    g = -jnp.exp(a_log.astype(jnp.float32))[:, None, :, None] * jax.nn.softplus(
        per_dir(a_logits) + dt_bias.astype(jnp.float32)[:, None, :, None])
    return beta, g


def gated_delta_chunked(q, k, v, g, beta, s0):
    b, h, n, dk = q.shape
    dv = v.shape[-1]
    nc = n // GDN_CHUNK
    chunks = lambda t: t.reshape(b, h, nc, GDN_CHUNK, *t.shape[3:])
    q = chunks(q * dk ** -0.5)
    k, v, g, beta = chunks(k), chunks(v), chunks(g), chunks(beta)
    G = jnp.cumsum(g, axis=-1)
    incl = jnp.tri(GDN_CHUNK, dtype=bool)
    strict = jnp.tri(GDN_CHUNK, k=-1, dtype=bool)
    seg = jnp.exp(jnp.where(incl, G[..., :, None] - G[..., None, :], -jnp.inf))
    kb = k * beta[..., None]
    a = jnp.where(strict, jnp.einsum('bhnid,bhnjd->bhnij', kb, k) * seg, 0.0)
    lhs = a + jnp.eye(GDN_CHUNK, dtype=q.dtype)
    u = lax.linalg.triangular_solve(lhs, v * beta[..., None], left_side=True, lower=True, unit_diagonal=True)
    w = lax.linalg.triangular_solve(lhs, kb * jnp.exp(G)[..., None], left_side=True, lower=True,
                                    unit_diagonal=True)
    intra = jnp.einsum('bhnid,bhnjd->bhnij', q, k) * seg
    q_dec = q * jnp.exp(G)[..., None]
    k_tail = k * jnp.exp(G[..., -1:] - G)[..., None]
    g_tot = jnp.exp(G[..., -1])

    def step(state, blk):
        u_c, w_c, qd_c, in_c, kt_c, gt_c = blk
        v_new = u_c - jnp.einsum('bhcd,bhde->bhce', w_c, state)
        o_c = jnp.einsum('bhcd,bhde->bhce', qd_c, state) + jnp.einsum('bhij,bhje->bhie', in_c, v_new)
        state = state * gt_c[..., None, None] + jnp.einsum('bhcd,bhce->bhde', kt_c, v_new)
        return state, o_c

    xs = tuple(jnp.moveaxis(t, 2, 0) for t in (u, w, q_dec, intra, k_tail, g_tot))
    s_fin, o = lax.scan(step, s0, xs)
    return jnp.moveaxis(o, 0, 2).reshape(b, h, n, dv), s_fin


def gdn_bidirectional(q_c, k_c, v_c, beta_c, g_c, q_x, k_x, v_x, beta_x, g_x):
    b, h, _, dk = q_c.shape
    s0 = jnp.zeros((b, h, dk, v_c.shape[-1]), jnp.float32)
    rev = lambda t: jnp.flip(t, axis=2)
    o_cf, s_cf = gated_delta_chunked(q_c, k_c, v_c, g_c[0], beta_c[0], s0)
    o_xf, _ = gated_delta_chunked(q_x, k_x, v_x, g_x[0], beta_x[0], s_cf)
    o_cb, s_cb = gated_delta_chunked(rev(q_c), rev(k_c), rev(v_c), rev(g_c[1]), rev(beta_c[1]), s0)
    o_xb, _ = gated_delta_chunked(rev(q_x), rev(k_x), rev(v_x), rev(g_x[1]), rev(beta_x[1]), s_cb)
    return o_cf + rev(o_cb), o_xf + rev(o_xb)


def gdn_output(o, z, norm_g):
    b, h, n, dv = o.shape
    o = rms_norm(o.transpose(0, 2, 1, 3), norm_g).astype(z.dtype)
    return (o * jax.nn.silu(z.reshape(b, n, h, dv))).reshape(b, n, h * dv)


def merge_branches(gate_logits, ya, yb, yc, w_a_br, w_b_br, w_c_br, w_o):
    b, n, _ = ya.shape
    g = jax.nn.sigmoid(gate_logits).reshape(b, n, N_BRANCHES, D_MODEL)
    y = g[:, :, 0] * (ya @ w_a_br) + g[:, :, 1] * (yb @ w_b_br) + g[:, :, 2] * (yc @ w_c_br)
    return y @ w_o


def conv_ffn(h, w_up, conv_w, w_down):
    gate, val = jnp.split(dwconv_centred(h @ w_up, conv_w), 2, axis=-1)
    return (jax.nn.silu(gate) * val) @ w_down


def hybrid_mixer(h_x, h_c, need_ctx, ang_r, ang_c, lam, lam_init, w_in, sgu_norm_g, sgu_w, sgu_b, w_a_br,
                 q_norm_g, k_norm_g, subln_g, w_b_br, conv_qkv_w, a_log, dt_bias, gdn_norm_g, w_c_br, w_o):
    (u_x, sv_x, bq_x, bk_x, bv_x, dq_x, dk_x, dv_x, dz_x, beta_x, a_x, gate_x) = split_in(h_x @ w_in)
    (u_c, sv_c, bq_c, bk_c, bv_c, dq_c, dk_c, dv_c, dz_c, beta_c, a_c, gate_c) = split_in(h_c @ w_in)
    ya_x = spatial_gating(u_x, sv_x, sgu_norm_g, sgu_w, sgu_b)
    q_x = rope_axial(diff_q(bq_x, q_norm_g), ang_r, ang_c)
    k_x, v_x = diff_kv(bk_x, bv_x, k_norm_g)
    k_x = rope_axial(k_x, ang_r, ang_c)
    k_c, v_c = diff_kv(bk_c, bv_c, k_norm_g)
    k_all = jnp.concatenate([k_c, k_x], axis=3)
    v_all = jnp.concatenate([v_c, v_x], axis=2)
    yb_x = diff_output(diff_attention_blocks(q_x, k_all, v_all, lam), subln_g, lam_init)
    gq_x, gk_x, gv_x = gdn_prepare(dq_x, dk_x, dv_x, conv_qkv_w)
    gq_c, gk_c, gv_c = gdn_prepare(dq_c, dk_c, dv_c, conv_qkv_w)
    be_x, g_x = gdn_gates(beta_x, a_x, a_log, dt_bias)
    be_c, g_c = gdn_gates(beta_c, a_c, a_log, dt_bias)
    o_c, o_x = gdn_bidirectional(gq_c, gk_c, gv_c, be_c, g_c, gq_x, gk_x, gv_x, be_x, g_x)
    yc_x = gdn_output(o_x, dz_x, gdn_norm_g)
    y_x = merge_branches(gate_x, ya_x, yb_x, yc_x, w_a_br, w_b_br, w_c_br, w_o)
    if not need_ctx:
        return y_x, None
    ya_c = spatial_gating(u_c, sv_c, sgu_norm_g, sgu_w, sgu_b)
    yb_c = diff_output(diff_softmax_combine(diff_q(bq_c, q_norm_g), k_c, v_c, lam), subln_g, lam_init)
    yc_c = gdn_output(o_c, dz_c, gdn_norm_g)
    y_c = merge_branches(gate_c, ya_c, yb_c, yc_c, w_a_br, w_b_br, w_c_br, w_o)
    return y_x, y_c


def setup_inputs(seed: int = 0) -> dict:
    key = jax.random.key(seed)
    ks = jax.random.split(key, 32)
    f32 = jnp.float32
    L = DEPTH
    nrm = lambda k, shape, scale: jax.random.normal(k, shape, f32) * scale
    gain = lambda k, shape: 1.0 + 0.1 * jax.random.normal(k, shape, f32)
    dt = jnp.exp(jax.random.uniform(ks[22], (L, 2, C_HEADS), f32, minval=math.log(1e-3), maxval=math.log(1e-1)))
    return {
        'x': nrm(ks[0], (BATCH, SEQ, D_MODEL), 1.0),
        'c': nrm(ks[1], (BATCH, D_MODEL), 1.0),
        'ctx': nrm(ks[2], (BATCH, CTX_LEN, D_MODEL), 1.0),
        'c_ctx': nrm(ks[3], (D_MODEL,), 1.0),
        'w_mod': nrm(ks[4], (L, D_MODEL, 6 * D_MODEL), 0.5 * D_MODEL ** -0.5),
        'b_mod': nrm(ks[5], (L, 6 * D_MODEL), 0.02),
        'norm1_g': gain(ks[6], (L, D_MODEL)),
        'w_in': nrm(ks[7], (L, D_MODEL, D_IN), D_MODEL ** -0.5),
        'sgu_norm_g': gain(ks[8], (L, A_GROUPS, A_GROUP_DIM)),
        'sgu_w': nrm(ks[9], (L, A_GROUPS, A_CHUNK, A_CHUNK), A_CHUNK ** -0.5),
        'sgu_b': gain(ks[10], (L, A_GROUPS, A_CHUNK)),
        'w_a_br': nrm(ks[11], (L, A_WIDTH, D_MODEL), A_WIDTH ** -0.5),
        'q_norm_g': gain(ks[12], (L, B_HEAD_DIM)),
        'k_norm_g': gain(ks[13], (L, B_HEAD_DIM)),
        'lambda_q1': nrm(ks[14], (L, B_HEAD_DIM), 0.1),
        'lambda_k1': nrm(ks[15], (L, B_HEAD_DIM), 0.1),
        'lambda_q2': nrm(ks[16], (L, B_HEAD_DIM), 0.1),
        'lambda_k2': nrm(ks[17], (L, B_HEAD_DIM), 0.1),
        'subln_g': gain(ks[18], (L, 2 * B_HEAD_DIM)),
        'w_b_br': nrm(ks[19], (L, B_V, D_MODEL), B_V ** -0.5),
        'conv_qkv_w': nrm(ks[20], (L, C_CONV, 3 * C_WIDTH), C_CONV ** -0.5),
        'a_log': jnp.log(jax.random.uniform(ks[21], (L, 2, C_HEADS), f32, minval=1.0, maxval=16.0)),
        'dt_bias': dt + jnp.log(-jnp.expm1(-dt)),
        'gdn_norm_g': gain(ks[23], (L, C_HEAD_DIM)),
        'w_c_br': nrm(ks[24], (L, C_WIDTH, D_MODEL), C_WIDTH ** -0.5),
        'w_o': nrm(ks[25], (L, D_MODEL, D_MODEL), D_MODEL ** -0.5),
        'norm2_g': gain(ks[26], (L, D_MODEL)),
        'w_up': nrm(ks[27], (L, D_MODEL, 2 * D_FF), D_MODEL ** -0.5),
        'conv_ffn_w': nrm(ks[28], (L, FFN_CONV, 2 * D_FF), FFN_CONV ** -0.5),
        'w_down': nrm(ks[29], (L, D_FF, D_MODEL), D_FF ** -0.5),
    }


def reference(x, c, ctx, c_ctx, w_mod, b_mod, norm1_g, w_in, sgu_norm_g, sgu_w, sgu_b, w_a_br, q_norm_g,
              k_norm_g, lambda_q1, lambda_k1, lambda_q2, lambda_k2, subln_g, w_b_br, conv_qkv_w, a_log,
              dt_bias, gdn_norm_g, w_c_br, w_o, norm2_g, w_up, conv_ffn_w, w_down):
    n = x.shape[1]
    rows = n // GRID_W
    ang_r, ang_c = axial_angles(rows)
    cx = ctx
    for l in range(DEPTH):
        last = l == DEPTH - 1
        lam_init = 0.8 - 0.6 * math.exp(-0.3 * l)
        lam = (jnp.exp(jnp.sum(lambda_q1[l].astype(jnp.float32) * lambda_k1[l].astype(jnp.float32)))
               - jnp.exp(jnp.sum(lambda_q2[l].astype(jnp.float32) * lambda_k2[l].astype(jnp.float32)))
               + lam_init)
        mod_x = jax.nn.silu(c) @ w_mod[l] + b_mod[l]
        mod_c = jax.nn.silu(c_ctx) @ w_mod[l] + b_mod[l]
        sh1_x, sc1_x, gt1_x, sh2_x, sc2_x, gt2_x = (m[:, None, :] for m in jnp.split(mod_x, 6, axis=-1))
        sh1_c, sc1_c, gt1_c, sh2_c, sc2_c, gt2_c = jnp.split(mod_c, 6, axis=-1)
        h_x = modulate(rms_norm(x, norm1_g[l]), sh1_x, sc1_x)
        h_c = modulate(rms_norm(cx, norm1_g[l]), sh1_c, sc1_c)
        y_x, y_c = hybrid_mixer(h_x, h_c, not last, ang_r, ang_c, lam, lam_init, w_in[l], sgu_norm_g[l],
                                sgu_w[l], sgu_b[l], w_a_br[l], q_norm_g[l], k_norm_g[l], subln_g[l], w_b_br[l],
                                conv_qkv_w[l], a_log[l], dt_bias[l], gdn_norm_g[l], w_c_br[l], w_o[l])
        x = x + gt1_x * y_x
        h_x = modulate(rms_norm(x, norm2_g[l]), sh2_x, sc2_x)
        x = x + gt2_x * conv_ffn(h_x, w_up[l], conv_ffn_w[l], w_down[l])
        if not last:
            cx = cx + gt1_c * y_c
            h_c = modulate(rms_norm(cx, norm2_g[l]), sh2_c, sc2_c)
            cx = cx + gt2_c * conv_ffn(h_c, w_up[l], conv_ffn_w[l], w_down[l])
    return x
```

```python
import math
import contextlib
import numpy as np
import concourse.bass as bass
import concourse.mybir as mybir
from concourse.bass_utils import run_bass_kernel_spmd

F32 = mybir.dt.float32
BF16 = mybir.dt.bfloat16
AF = mybir.ActivationFunctionType
ALU = mybir.AluOpType
AX = mybir.AxisListType

D = 1024
GRID_W = 64
A_W = 512
B_H = 4
C_H = 4
D_FF = 2816
EPS = 1e-6
D_IN = 7696
O_U, O_SV, O_BQ, O_BK, O_BV, O_DQ, O_DK, O_DV, O_DZ, O_BETA, O_GATE = (
    0, 512, 1024, 1536, 2048, 2560, 3072, 3584, 4096, 4608, 4624)
EPOCH = 30000
GELU_C = 2.0 * math.sqrt(2.0 / math.pi)


class Tl:
    __slots__ = ("t", "lw", "rd")

    def __init__(self, t):
        self.t = t
        self.lw = None
        self.rd = {}

    def __getitem__(self, idx):
        return self.t[idx]


class Sched:
    def __init__(self, nc, n_dma_slots=8):
        self.nc = nc
        self.eng = {"pe": nc.tensor, "act": nc.scalar, "dve": nc.vector, "pool": nc.gpsimd, "sp": nc.sync}
        self.cnt = {e: 0 for e in self.eng}
        self.sems = {e: [] for e in self.eng}
        self.seen = {e: {} for e in self.eng}
        self.dq = {}
        self.n_dma_slots = n_dma_slots
        self.ninst = 0

    def _esem(self, e, idx):
        ep = (idx - 1) // EPOCH
        while len(self.sems[e]) <= ep:
            self.sems[e].append(self.nc.alloc_semaphore(f"s_{e}_{len(self.sems[e])}"))
        return self.sems[e][ep], (idx - 1) % EPOCH + 1

    def _wait(self, e, tok):
        if tok is None:
            return
        if tok[0] == "eng":
            _, f, idx = tok
            if f == e and e == "pe":
                return
            sem, val = self._esem(f, idx)
            key = (f, (idx - 1) // EPOCH)
        else:
            _, sem, val, key = tok
        if self.seen[e].get(key, 0) >= val:
            return
        self.seen[e][key] = val
        self.eng[e].wait_ge(sem, val)
        self.ninst += 1

    def _deps(self, e, reads, writes):
        for t in reads:
            self._wait(e, t.lw)
        for t in writes:
            self._wait(e, t.lw)
            for tok in list(t.rd.values()):
                self._wait(e, tok)

    def _mark(self, tok, rkey, reads, writes):
        for t in reads:
            t.rd[rkey] = tok
        for t in writes:
            t.lw = tok
            t.rd = {}

    def op(self, e, fn, reads=(), writes=()):
        if e == "pool":
            e = "dve"
        self._deps(e, reads, writes)
        inst = fn(self.eng[e])
        self.cnt[e] += 1
        idx = self.cnt[e]
        sem, _ = self._esem(e, idx)
        inst.then_inc(sem, 1)
        self.ninst += 1
        self._mark(("eng", e, idx), e, reads, writes)

    def dma(self, q, out, in_, reads=(), writes=(), **kw):
        if q not in self.dq:
            self.dq[q] = {"slots": [[self.nc.alloc_semaphore(f"d_{q}_{i}"), 0]
                                    for i in range(self.n_dma_slots)], "i": 0}
        d = self.dq[q]
        si = d["i"] % self.n_dma_slots
        d["i"] += 1
        slot = d["slots"][si]
        self._deps(q, reads, writes)
        key = ("dma", q, si)
        if slot[1] > 0:
            self._wait(q, ("dma", slot[0], slot[1], key))
        inst = self.eng[q].dma_start(out=out, in_=in_, **kw)
        slot[1] += 16
        inst.then_inc(slot[0], 16)
        self.ninst += 1
        self._mark(("dma", slot[0], slot[1], key), key, reads, writes)

    def barrier(self):
        toks = []
        for f in self.eng:
            if self.cnt[f] > 0:
                toks.append(("eng", f, self.cnt[f]))
        for q, d in self.dq.items():
            for si, slot in enumerate(d["slots"]):
                if slot[1] > 0:
                    toks.append(("dma", slot[0], slot[1], ("dma", q, si)))
        for e in self.eng:
            for tok in toks:
                if tok[0] == "eng" and tok[1] == e:
                    continue
                self._wait(e, tok)


class Rot:
    def __init__(self, tiles):
        self.tiles = tiles
        self.i = 0

    def get(self):
        t = self.tiles[self.i % len(self.tiles)]
        self.i += 1
        return t


def host_consts(SEQ, CTX):
    T = CTX + SEQ
    c = {}
    c["c_ident"] = np.eye(128, dtype=np.float32)
    p = np.arange(128)
    c["c_blk64"] = (p[:, None] // 64 == p[None, :] // 64).astype(np.float32)
    R = np.zeros((64, 64), np.float32)
    for base in (0, 32):
        for i in range(16):
            R[base + i, base + 16 + i] = -1.0
            R[base + 16 + i, base + i] = 1.0
    R2 = np.zeros((128, 128), np.float32)
    R2[:64, :64] = R
    R2[64:, 64:] = R
    c["c_rotT"] = np.ascontiguousarray(R2.T)
    n_freq = 16
    inv_freq = (np.float32(10000.0) ** (-np.arange(n_freq, dtype=np.float32) / np.float32(n_freq))).astype(np.float32)
    rows = SEQ // GRID_W
    row = np.repeat(np.arange(rows, dtype=np.float32), GRID_W)
    col = np.tile(np.arange(GRID_W, dtype=np.float32), rows)
    ang_r = (row[:, None] * inv_freq).astype(np.float32)
    ang_c = (col[:, None] * inv_freq).astype(np.float32)
    cos64 = np.concatenate([np.cos(ang_r), np.cos(ang_r), np.cos(ang_c), np.cos(ang_c)], axis=1).astype(np.float32)
    sin64 = np.concatenate([np.sin(ang_r), np.sin(ang_r), np.sin(ang_c), np.sin(ang_c)], axis=1).astype(np.float32)
    cos = np.ones((128, T), np.float32)
    sin = np.zeros((128, T), np.float32)
    cos[:, CTX:] = np.concatenate([cos64, cos64], axis=1).T
    sin[:, CTX:] = np.concatenate([sin64, sin64], axis=1).T
    c["c_cos"] = cos
    c["c_sin"] = sin
    i = np.arange(64)
    lo = (i[:, None] >= i[None, :]).astype(np.float32)
    up = (i[:, None] <= i[None, :]).astype(np.float32)
    slo = (i[:, None] > i[None, :]).astype(np.float32)
    sup = (i[:, None] < i[None, :]).astype(np.float32)

    def blk8(f, b):
        return np.ascontiguousarray(np.stack([f] * 4 + [b] * 4, axis=1))
    g = np.zeros((64, 6, 8, 64), np.float32)
    g[:, 0] = blk8(up, lo)
    g[:, 1] = blk8(lo, up)
    g[:, 2] = blk8(slo, sup)
    g[:, 3] = blk8(up, lo)
    g[:, 4] = blk8(np.eye(64, dtype=np.float32), np.eye(64, dtype=np.float32))
    g[:, 5] = 1.0
    c["c_gdn"] = g
    return c


def build(NB, SEQ, CTX, DEPTH, dbg=None):
    T = CTX + SEQ
    nc = bass.Bass("TRN2", target_bir_lowering=False)
    S = Sched(nc)

    def din(name, shape):
        return nc.dram_tensor(name, list(shape), F32, kind="ExternalInput").ap()

    def dscr(name, shape, dt):
        return nc.dram_tensor(name, list(shape), dt, kind="Internal").ap()

    L = DEPTH
    x_in = din("x", (NB, SEQ, D))
    c_in = din("c", (NB, D))
    ctx_in = din("ctx", (NB, CTX, D))
    cctx_in = din("c_ctx", (D,))
    W = {}
    for name, shape in [("w_mod", (L, D, 6 * D)), ("b_mod", (L, 6 * D)), ("norm1_g", (L, D)), ("w_in", (L, D, D_IN)),
                        ("sgu_norm_g", (L, 4, 128)), ("sgu_w", (L, 4, 128, 128)), ("sgu_b", (L, 4, 128)),
                        ("w_a_br", (L, 512, D)), ("q_norm_g", (L, 64)), ("k_norm_g", (L, 64)),
                        ("lambda_q1", (L, 64)), ("lambda_k1", (L, 64)), ("lambda_q2", (L, 64)),
                        ("lambda_k2", (L, 64)), ("subln_g", (L, 128)), ("w_b_br", (L, 512, D)),
                        ("conv_qkv_w", (L, 3, 1536)), ("a_log", (L, 2, 4)), ("dt_bias", (L, 2, 4)),
                        ("gdn_norm_g", (L, 128)), ("w_c_br", (L, 512, D)), ("w_o", (L, D, D)),
                        ("norm2_g", (L, D)), ("w_up", (L, D, 2 * D_FF)), ("conv_ffn_w", (L, 3, 2 * D_FF)),
                        ("w_down", (L, D_FF, D))]:
        W[name] = din(name, shape)
    c_ident = din("c_ident", (128, 128))
    c_blk64 = din("c_blk64", (128, 128))
    c_rotT = din("c_rotT", (128, 128))
    c_cos = din("c_cos", (128, T))
    c_sin = din("c_sin", (128, T))
    c_gdn = din("c_gdn", (64, 6, 8, 64))
    out = nc.dram_tensor("out", [NB, SEQ, D], F32, kind="ExternalOutput").ap()

    cx = dscr("cx", (NB, CTX, D), F32)
    wb_in = dscr("wb_in", (D, D_IN), BF16)
    wb_br = dscr("wb_br", (3, 512, D), BF16)
    wb_o = dscr("wb_o", (D, D), BF16)
    wb_up = dscr("wb_up", (D, 2 * D_FF), BF16)
    wb_dn = dscr("wb_dn", (D_FF, D), BF16)
    modrow_d = dscr("modrow", (2, 4, D), F32)
    hT_d = dscr("hT", (D, T), BF16)
    h2T_d = dscr("h2T", (D, T), BF16)
    yT_d = dscr("yT", (3, 512, T), BF16)
    QT_d = dscr("QT", (512, T), BF16)
    KT_d = dscr("KT", (512, T), BF16)
    V_d = dscr("V", (T, 512), BF16)
    gpre_d = dscr("gpre", (1536, T), F32)
    gqT_d = dscr("gqT", (512, T), F32)
    gkT_d = dscr("gkT", (512, T), F32)
    gkt_d = dscr("gkt", (T, 512), F32)
    gvt_d = dscr("gvt", (T, 512), F32)
    zs_d = dscr("zs", (T, 512), F32)
    bg_d = dscr("bg", (T, 16), F32)
    of_d = dscr("of", (2, T, 512), F32)

    R4 = 4
    assert NB + 1 <= R4

    def sb(name, shape, dt=F32):
        return Tl(nc.alloc_sbuf_tensor(name, list(shape), dt))

    PSB = [Tl(nc.alloc_psum_tensor(f"ps{i}", [128, 512], F32)) for i in range(8)]
    ident = sb("ident", (128, 128))
    S.dma("sp", ident[:], c_ident, writes=[ident])
    eps_t = sb("eps_t", (128, 1))
    S.op("dve", lambda e: e.memset(eps_t[:], EPS), [], [eps_t])

    def stream(b, t0, n):
        if t0 < CTX:
            return cx[b, t0:t0 + n, :]
        return out[b, t0 - CTX:t0 - CTX + n, :]

    def tiles(nmax, lat_only=False):
        r = []
        if not lat_only:
            for t0 in range(0, CTX, nmax):
                r.append((t0, min(nmax, CTX - t0)))
        for t0 in range(CTX, T, nmax):
            r.append((t0, min(nmax, T - t0)))
        return r

    def seg_bounds(t0):
        return (0, CTX) if t0 < CTX else (CTX, T)

    for b in range(NB):
        for r0 in range(0, SEQ, 512):
            S.dma("sp", out[b, r0:r0 + 512, :], x_in[b, r0:r0 + 512, :])
        S.dma("sp", cx[b], ctx_in[b])

    def col_load(q, dst_tile, dst_ap, vec_ap, n):
        for c0 in range(0, n, 8):
            c1 = min(n, c0 + 8)
            S.dma(q, dst_ap[:, c0:c1], vec_ap[c0 * 128:c1 * 128].rearrange("(c p) -> p c", p=128),
                  writes=[dst_tile], allow_slow_non_contiguous=True)

    def gelu_ops(es_get, src_ap, src_tl, n, out_ap, out_tl):
        xs = es_get()
        t = es_get()
        S.op("act", lambda e: e.activation(out=xs[:, 0:n], in_=src_ap, func=AF.Identity), [src_tl], [xs])
        S.op("dve", lambda e: e.tensor_tensor(out=t[:, 0:n], in0=xs[:, 0:n], in1=xs[:, 0:n], op=ALU.mult), [xs], [t])
        S.op("dve", lambda e: e.tensor_scalar(out=t[:, 0:n], in0=t[:, 0:n], scalar1=0.044715, scalar2=1.0,
                                              op0=ALU.mult, op1=ALU.add), [t], [t])
        S.op("dve", lambda e: e.tensor_tensor(out=t[:, 0:n], in0=t[:, 0:n], in1=xs[:, 0:n], op=ALU.mult), [t, xs], [t])
        S.op("act", lambda e: e.activation(out=t[:, 0:n], in_=t[:, 0:n], func=AF.Sigmoid, scale=GELU_C), [t], [t])
        S.op("dve", lambda e: e.tensor_tensor(out=out_ap, in0=t[:, 0:n], in1=xs[:, 0:n], op=ALU.mult), [t, xs], [out_tl])

    def rstd_ops(ssq_tl, ssq_ap, out_tl, out_ap, inv_n):
        S.op("act", lambda e: e.activation(out=out_ap, in_=ssq_ap, func=AF.Sqrt, bias=eps_t[0:out_ap.shape[0], :],
                                           scale=inv_n), [ssq_tl, eps_t], [out_tl])
        S.op("dve", lambda e: e.reciprocal(out=out_ap, in_=out_ap), [out_tl], [out_tl])

    def norm_mod_T(xt, nj, n, Acol, Bcol, ri, hT, pools):
        junk, stat, xn_pool, psr = pools
        ssq = stat.get()
        for j in range(nj):
            jk = junk.get()
            S.op("act", lambda e, j=j, jk=jk: e.activation(out=jk[:], in_=xt[:, j, :], func=AF.Square,
                                                           accum_out=ssq[:, j:j + 1]), [xt], [jk, ssq])
        rs = stat.get()
        rstd_ops(ssq, ssq[:, 0:nj], rs, rs[:, 0:nj], 1.0 / D)
        xn = xn_pool.get()
        for j in range(nj):
            S.op("act", lambda e, j=j: e.activation(out=xn[:, j, :], in_=xt[:, j, :], func=AF.Identity,
                                                    scale=rs[:, j:j + 1]), [xt, rs], [xn])
        for kc in range(8):
            ps = psr.get()
            for j in range(nj):
                S.op("pe", lambda e, j=j, kc=kc, ps=ps: e.transpose(out=ps[:, j * 128:(j + 1) * 128],
                                                                     in_=xn[:, j, kc * 128:(kc + 1) * 128],
                                                                     identity=ident[:]), [xn, ident], [ps])
            S.op("act", lambda e, kc=kc, ps=ps: e.activation(out=hT[:, kc, 0:n], in_=ps[:, 0:n], func=AF.Identity,
                                                             scale=Acol[:, kc, ri:ri + 1],
                                                             bias=Bcol[:, kc, ri:ri + 1]), [ps, Acol, Bcol], [hT])

    for l in range(L):
        last = (l == L - 1)
        lam_init = 0.8 - 0.6 * math.exp(-0.3 * l)
        S.barrier()
        for r0 in range(0, D, 128):
            S.dma("pool", wb_in[r0:r0 + 128, :], W["w_in"][l, r0:r0 + 128, :])
            S.dma("pool", wb_up[r0:r0 + 128, :], W["w_up"][l, r0:r0 + 128, :])
            S.dma("pool", wb_o[r0:r0 + 128, :], W["w_o"][l, r0:r0 + 128, :])
        for r0 in range(0, D_FF, 128):
            S.dma("pool", wb_dn[r0:r0 + 128, :], W["w_down"][l, r0:r0 + 128, :])
        for bi, nm in enumerate(("w_a_br", "w_b_br", "w_c_br")):
            for r0 in range(0, 512, 128):
                S.dma("pool", wb_br[bi, r0:r0 + 128, :], W[nm][l, r0:r0 + 128, :])

        with contextlib.ExitStack() as LS:
            def lsb(name, shape, dt=F32):
                return Tl(LS.enter_context(nc.sbuf_tensor(f"{name}_{l}", list(shape), dt)))

            Acol1 = lsb("Acol1", (128, 8, R4))
            Bcol1 = lsb("Bcol1", (128, 8, R4))
            Acol2 = lsb("Acol2", (128, 8, R4))
            Bcol2 = lsb("Bcol2", (128, 8, R4))
            lam_t = lsb("lam", (128, 2))
            wsT = lsb("wsT", (128, 4, 128), BF16)
            bsb = lsb("bsb", (128, 512))
            sgng = lsb("sgng", (128, 512))
            gq_c = lsb("gq_c", (128, 1))
            gk_c = lsb("gk_c", (128, 1))
            subg = lsb("subg", (128, 128))
            gdng = lsb("gdng", (128, 512))
            cwq = lsb("cwq", (128, 12, 3))
            cwf = lsb("cwf", (128, 44, 3))
            alog_b = lsb("alog_b", (128, 8))
            dtb_b = lsb("dtb_b", (128, 8))
            blk64 = lsb("blk64", (128, 128))
            rotT = lsb("rotT", (128, 128), BF16)

            with contextlib.ExitStack() as ES:
                def esb(name, shape, dt=F32):
                    return Tl(ES.enter_context(nc.sbuf_tensor(f"{name}_{l}", list(shape), dt)))
                S.dma("sp", blk64[:], c_blk64, writes=[blk64])
                rt32 = esb("rt32", (128, 128))
                S.dma("sp", rt32[:], c_rotT, writes=[rt32])
                S.op("dve", lambda e: e.tensor_copy(out=rotT[:], in_=rt32[:]), [rt32], [rotT])
                S.dma("sp", bsb[:], W["sgu_b"][l].rearrange("g i -> (g i)").partition_broadcast(128), writes=[bsb])
                S.dma("sp", sgng[:], W["sgu_norm_g"][l].rearrange("g i -> (g i)").partition_broadcast(128),
                      writes=[sgng])
                S.dma("sp", subg[:], W["subln_g"][l].partition_broadcast(128), writes=[subg])
                for h in range(4):
                    S.dma("sp", gdng[:, h * 128:(h + 1) * 128], W["gdn_norm_g"][l].partition_broadcast(128),
                          writes=[gdng])
                S.dma("sp", alog_b[:], W["a_log"][l].rearrange("a b -> (a b)").partition_broadcast(128),
                      writes=[alog_b])
                S.dma("sp", dtb_b[:], W["dt_bias"][l].rearrange("a b -> (a b)").partition_broadcast(128),
                      writes=[dtb_b])
                S.op("act", lambda e: e.activation(out=alog_b[:], in_=alog_b[:], func=AF.Exp), [alog_b], [alog_b])
                S.op("dve", lambda e: e.tensor_scalar(out=alog_b[:], in0=alog_b[:], scalar1=-1.0, scalar2=None,
                                                      op0=ALU.mult), [alog_b], [alog_b])
                for half in range(2):
                    S.dma("sp", gq_c[half * 64:(half + 1) * 64, :], W["q_norm_g"][l].rearrange("(p o) -> p o", o=1),
                          writes=[gq_c])
                    S.dma("sp", gk_c[half * 64:(half + 1) * 64, :], W["k_norm_g"][l].rearrange("(p o) -> p o", o=1),
                          writes=[gk_c])
                for k in range(3):
                    col_load("sp", cwq, cwq[:, :, k], W["conv_qkv_w"][l, k], 12)
                    col_load("sp", cwf, cwf[:, :, k], W["conv_ffn_w"][l, k], 44)
                sw = esb("sw", (128, 4, 128))
                S.dma("sp", sw[:], W["sgu_w"][l].rearrange("g i j -> i g j"), writes=[sw])
                ps = PSB[0]
                for g in range(4):
                    S.op("pe", lambda e, g=g: e.transpose(out=ps[:, g * 128:(g + 1) * 128], in_=sw[:, g, :],
                                                          identity=ident[:]), [sw, ident], [ps])
                S.op("dve", lambda e: e.tensor_copy(out=wsT[:].rearrange("p g i -> p (g i)"), in_=ps[:]), [ps], [wsT])
                lv = esb("lv", (128, 4, 64))
                for i, nm in enumerate(("lambda_q1", "lambda_k1", "lambda_q2", "lambda_k2")):
                    S.dma("sp", lv[:, i, :], W[nm][l].partition_broadcast(128), writes=[lv])
                lp = esb("lp", (128, 2, 64))
                S.op("dve", lambda e: e.tensor_tensor(out=lp[:, 0, :], in0=lv[:, 0, :], in1=lv[:, 1, :], op=ALU.mult),
                     [lv], [lp])
                S.op("dve", lambda e: e.tensor_tensor(out=lp[:, 1, :], in0=lv[:, 2, :], in1=lv[:, 3, :], op=ALU.mult),
                     [lv], [lp])
                ls_ = esb("ls", (128, 2))
                S.op("dve", lambda e: e.tensor_reduce(out=ls_[:], in_=lp[:], axis=AX.X, op=ALU.add), [lp], [ls_])
                S.op("act", lambda e: e.activation(out=ls_[:], in_=ls_[:], func=AF.Exp), [ls_], [ls_])
                S.op("dve", lambda e: e.tensor_tensor(out=lam_t[:, 0:1], in0=ls_[:, 1:2], in1=ls_[:, 0:1],
                                                      op=ALU.subtract), [ls_], [lam_t])
                S.op("dve", lambda e: e.tensor_scalar(out=lam_t[:, 0:1], in0=lam_t[:, 0:1], scalar1=-lam_init,
                                                      scalar2=None, op0=ALU.add), [lam_t], [lam_t])
                scT = esb("scT", (128, 8, R4))
                S.op("dve", lambda e: e.memset(scT[:], 0.0), [], [scT])
                for r in range(NB + 1):
                    src = c_in[r] if r < NB else cctx_in
                    S.dma("sp", scT[:, :, r], src.rearrange("(c p) -> p c", p=128), writes=[scT],
                          allow_slow_non_contiguous=True)
                S.op("act", lambda e: e.activation(out=scT[:], in_=scT[:], func=AF.Silu), [scT], [scT])
                bmc = esb("bmc", (128, 48))
                col_load("sp", bmc, bmc[:], W["b_mod"][l], 48)
                g1c = esb("g1c", (128, 8))
                g2c = esb("g2c", (128, 8))
                col_load("sp", g1c, g1c[:], W["norm1_g"][l], 8)
                col_load("sp", g2c, g2c[:], W["norm2_g"][l], 8)
                bmr = esb("bmr", (R4, 6 * D))
                S.dma("sp", bmr[:], W["b_mod"][l].partition_broadcast(R4), writes=[bmr])
                modT = esb("modT", (128, 6, 8, R4))
                mrow = esb("mrow", (R4, 2, D))
                wmp = Rot([esb(f"wm{i}", (128, 8, 512)) for i in range(2)])
                wmv = W["w_mod"][l].rearrange("(kc p) n -> p kc n", p=128)
                for cb in range(12):
                    seg = cb // 2
                    wm = wmp.get()
                    S.dma("sp", wm[:], wmv[:, :, cb * 512:(cb + 1) * 512], writes=[wm])
                    if seg in (2, 5):
                        ps = PSB[(cb % 2) + 1]
                        for kc in range(8):
                            S.op("pe", lambda e, kc=kc, ps=ps, wm=wm: e.matmul(ps[0:R4, :], lhsT=scT[:, kc, :],
                                                                                rhs=wm[:, kc, :], start=(kc == 0),
                                                                                stop=(kc == 7)), [scT, wm], [ps])
                        gi = 0 if seg == 2 else 1
                        hf = cb % 2
                        S.op("dve", lambda e, ps=ps, gi=gi, hf=hf, cb=cb: e.tensor_tensor(
                            out=mrow[:, gi, hf * 512:(hf + 1) * 512], in0=ps[0:R4, :],
                            in1=bmr[:, cb * 512:(cb + 1) * 512], op=ALU.add), [ps, bmr], [mrow])
                    else:
                        ps = PSB[(cb % 2) + 1]
                        for c4 in range(4):
                            for kc in range(8):
                                S.op("pe", lambda e, kc=kc, c4=c4, ps=ps, wm=wm: e.matmul(
                                    ps[:, c4 * R4:(c4 + 1) * R4], lhsT=wm[:, kc, c4 * 128:(c4 + 1) * 128],
                                    rhs=scT[:, kc, :], start=(kc == 0), stop=(kc == 7)), [scT, wm], [ps])
                        for c4 in range(4):
                            fc = cb * 4 + c4
                            S.op("dve", lambda e, ps=ps, c4=c4, fc=fc, seg=seg: e.tensor_scalar(
                                out=modT[:, seg, fc % 8, :], in0=ps[:, c4 * R4:(c4 + 1) * R4],
                                scalar1=bmc[:, fc:fc + 1], scalar2=None, op0=ALU.add), [ps, bmc], [modT])
                for kc in range(8):
                    S.op("dve", lambda e, kc=kc: e.tensor_scalar(out=Acol1[:, kc, :], in0=modT[:, 1, kc, :], scalar1=1.0,
                                                                 scalar2=g1c[:, kc:kc + 1], op0=ALU.add, op1=ALU.mult),
                         [modT, g1c], [Acol1])
                    S.op("dve", lambda e, kc=kc: e.tensor_scalar(out=Acol2[:, kc, :], in0=modT[:, 4, kc, :], scalar1=1.0,
                                                                 scalar2=g2c[:, kc:kc + 1], op0=ALU.add, op1=ALU.mult),
                         [modT, g2c], [Acol2])
                S.op("dve", lambda e: e.tensor_copy(out=Bcol1[:], in_=modT[:, 0]), [modT], [Bcol1])
                S.op("dve", lambda e: e.tensor_copy(out=Bcol2[:], in_=modT[:, 3]), [modT], [Bcol2])
                S.dma("sp", modrow_d.rearrange("g r d -> r g d"), mrow[:], reads=[mrow])
                S.barrier()
            if dbg == "setup":
                break

            for b in range(NB):
                S.barrier()
                with contextlib.ExitStack() as ES:
                    cnt = [0]

                    def esb(name, shape, dt=F32):
                        cnt[0] += 1
                        return Tl(ES.enter_context(nc.sbuf_tensor(f"{name}_{l}_{b}_{cnt[0]}", list(shape), dt)))
                    xt_p = Rot([esb("xt", (128, 4, D)) for _ in range(1)])
                    xn_p = Rot([esb("xn", (128, 4, D)) for _ in range(1)])
                    junk = Rot([esb("junk", (128, D)) for _ in range(2)])
                    stat = Rot([esb("stat", (128, 8)) for _ in range(8)])
                    hT_p = Rot([esb("hT", (128, 8, 512), BF16) for _ in range(2)])
                    wt_p = Rot([esb("wt", (128, 8, 512), BF16) for _ in range(3)])
                    tmp_p = Rot([esb("tmp", (128, 512)) for _ in range(6)])
                    tb_p = Rot([esb("tb", (128, 512), BF16) for _ in range(4)])
                    uT_p = Rot([esb("uT", (128, 4, 512), BF16) for _ in range(2)])
                    ya_p = Rot([esb("ya", (128, 4, 512), BF16) for _ in range(2)])
                    qk_p = Rot([esb("qk", (128, 4, 512), BF16) for _ in range(3)])
                    cs_p = Rot([esb("cs", (128, 2, 512)) for _ in range(2)])
                    vt_p = Rot([esb("vt", (128, 4, 512), BF16) for _ in range(2)])
                    zt_p = Rot([esb("zt", (128, 4, 512)) for _ in range(1)])
                    gp_p = Rot([esb("gp", (128, 4, 512)) for _ in range(2)])
                    bg_p = Rot([esb("bgt", (128, 4, 16)) for _ in range(2)])
                    sm_p = Rot([esb("sm", (128, 16)) for _ in range(6)])
                    psr = Rot(PSB)
                    wv_in = wb_in.rearrange("(kc p) n -> p kc n", p=128)

                    for (t0, n) in tiles(512):
                        nj = n // 128
                        ri = NB if t0 < CTX else b
                        xt = xt_p.get()
                        S.dma("sp", xt[:, 0:nj, :], stream(b, t0, n).rearrange("(j p) d -> p j d", p=128), writes=[xt])
                        hT = hT_p.get()
                        norm_mod_T(xt, nj, n, Acol1, Bcol1, ri, hT, (junk, stat, xn_p, psr))
                        S.dma("pool", hT_d.rearrange("(kc p) t -> p kc t", p=128)[:, :, t0:t0 + n], hT[:, :, 0:n],
                              reads=[hT])
                        cs = cs_p.get()
                        S.dma("sp", cs[:, 0, 0:n], c_cos[:, t0:t0 + n], writes=[cs])
                        S.dma("sp", cs[:, 1, 0:n], c_sin[:, t0:t0 + n], writes=[cs])

                        def fm_group(col0):
                            wt = wt_p.get()
                            S.dma("sp", wt[:], wv_in[:, :, col0:col0 + 512], writes=[wt])
                            for cc in range(4):
                                ps = psr.get()
                                for kc in range(8):
                                    S.op("pe", lambda e, kc=kc, cc=cc, ps=ps, wt=wt: e.matmul(
                                        ps[:, 0:n], lhsT=wt[:, kc, cc * 128:(cc + 1) * 128], rhs=hT[:, kc, 0:n],
                                        start=(kc == 0), stop=(kc == 7)), [wt, hT], [ps])
                                yield cc, ps

                        def tm_group(col0, ncol=512):
                            wt = wt_p.get()
                            S.dma("sp", wt[:, :, 0:ncol], wv_in[:, :, col0:col0 + ncol], writes=[wt])
                            for j in range(nj):
                                ps = psr.get()
                                for kc in range(8):
                                    S.op("pe", lambda e, kc=kc, j=j, ps=ps, wt=wt: e.matmul(
                                        ps[:, 0:ncol], lhsT=hT[:, kc, j * 128:(j + 1) * 128], rhs=wt[:, kc, 0:ncol],
                                        start=(kc == 0), stop=(kc == 7)), [wt, hT], [ps])
                                yield j, ps

                        uT = uT_p.get()
                        for cc, ps in fm_group(O_U):
                            gelu_ops(tmp_p.get, ps[:, 0:n], ps, n, uT[:, cc, 0:n], uT)
                        ya = ya_p.get()
                        for j, ps in tm_group(O_SV):
                            gv = tmp_p.get()
                            gelu_ops(tmp_p.get, ps[:], ps, 512, gv[:], gv)
                            sq = tmp_p.get()
                            S.op("dve", lambda e, gv=gv, sq=sq: e.tensor_tensor(out=sq[:], in0=gv[:], in1=gv[:],
                                                                                op=ALU.mult), [gv], [sq])
                            ssq = stat.get()
                            S.op("dve", lambda e, sq=sq, ssq=ssq: e.tensor_reduce(
                                out=ssq[:, 0:4], in_=sq[:].rearrange("p (g c) -> p g c", g=4), axis=AX.X, op=ALU.add),
                                [sq], [ssq])
                            rs = stat.get()
                            rstd_ops(ssq, ssq[:, 0:4], rs, rs[:, 0:4], 1.0 / 128)
                            S.op("dve", lambda e, gv=gv, rs=rs: e.tensor_tensor(
                                out=gv[:].rearrange("p (g c) -> p g c", g=4),
                                in0=gv[:].rearrange("p (g c) -> p g c", g=4),
                                in1=rs[:, 0:4].unsqueeze(2).to_broadcast([128, 4, 128]), op=ALU.mult), [gv, rs], [gv])
                            vn = tb_p.get()
                            S.op("dve", lambda e, gv=gv, vn=vn: e.tensor_tensor(out=vn[:], in0=gv[:], in1=sgng[:],
                                                                                op=ALU.mult), [gv, sgng], [vn])
                            pm = psr.get()
                            for g in range(4):
                                S.op("pe", lambda e, g=g, pm=pm, vn=vn: e.matmul(
                                    pm[:, g * 128:(g + 1) * 128], lhsT=vn[:, g * 128:(g + 1) * 128], rhs=wsT[:, g, :],
                                    start=True, stop=True), [vn, wsT], [pm])
                            mx = tmp_p.get()
                            S.op("dve", lambda e, pm=pm, mx=mx: e.tensor_tensor(out=mx[:], in0=pm[:], in1=bsb[:],
                                                                                op=ALU.add), [pm, bsb], [mx])
                            S.op("dve", lambda e, mx=mx, j=j: e.tensor_tensor(
                                out=ya[:, :, j * 128:(j + 1) * 128], in0=mx[:].rearrange("p (g i) -> p g i", g=4),
                                in1=uT[:, :, j * 128:(j + 1) * 128], op=ALU.mult), [mx, uT], [ya])
                        S.dma("pool", yT_d[0].rearrange("(c p) t -> p c t", p=128)[:, :, t0:t0 + n], ya[:, :, 0:n],
                              reads=[ya])
                        for (col0, gcol, dst) in ((O_BQ, gq_c, QT_d), (O_BK, gk_c, KT_d)):
                            qk = qk_p.get()
                            for cc, ps in fm_group(col0):
                                sq = tmp_p.get()
                                S.op("act", lambda e, ps=ps, sq=sq: e.activation(out=sq[:, 0:n], in_=ps[:, 0:n],
                                                                                 func=AF.Square), [ps], [sq])
                                p2 = psr.get()
                                S.op("pe", lambda e, p2=p2, sq=sq: e.matmul(p2[:, 0:n], lhsT=blk64[:], rhs=sq[:, 0:n],
                                                                            start=True, stop=True), [blk64, sq], [p2])
                                rs = tmp_p.get()
                                rstd_ops(p2, p2[:, 0:n], rs, rs[:, 0:n], 1.0 / 64)
                                qn = tb_p.get()
                                S.op("dve", lambda e, ps=ps, rs=rs, qn=qn, gcol=gcol: e.scalar_tensor_tensor(
                                    out=qn[:, 0:n], in0=ps[:, 0:n], scalar=gcol[:, 0:1], in1=rs[:, 0:n], op0=ALU.mult,
                                    op1=ALU.mult), [ps, rs, gcol], [qn])
                                p3 = psr.get()
                                S.op("pe", lambda e, p3=p3, qn=qn: e.matmul(p3[:, 0:n], lhsT=rotT[:], rhs=qn[:, 0:n],
                                                                            start=True, stop=True), [rotT, qn], [p3])
                                t1 = tmp_p.get()
                                S.op("dve", lambda e, qn=qn, t1=t1: e.tensor_tensor(out=t1[:, 0:n], in0=qn[:, 0:n],
                                                                                    in1=cs[:, 0, 0:n], op=ALU.mult),
                                     [qn, cs], [t1])
                                t2 = tmp_p.get()
                                S.op("dve", lambda e, p3=p3, t2=t2: e.tensor_tensor(out=t2[:, 0:n], in0=p3[:, 0:n],
                                                                                    in1=cs[:, 1, 0:n], op=ALU.mult),
                                     [p3, cs], [t2])
                                S.op("pool", lambda e, t1=t1, t2=t2, cc=cc, qk=qk: e.tensor_tensor(
                                    out=qk[:, cc, 0:n], in0=t1[:, 0:n], in1=t2[:, 0:n], op=ALU.add), [t1, t2], [qk])
                            S.dma("pool", dst.rearrange("(c p) t -> p c t", p=128)[:, :, t0:t0 + n], qk[:, :, 0:n],
                                  reads=[qk])
                        vt = vt_p.get()
                        for j, ps in tm_group(O_BV):
                            S.op("act", lambda e, ps=ps, j=j: e.activation(out=vt[:, j, :], in_=ps[:], func=AF.Identity),
                                 [ps], [vt])
                        S.dma("pool", V_d[t0:t0 + n, :].rearrange("(j p) c -> p j c", p=128), vt[:, 0:nj, :], reads=[vt])
                        for gi, col0 in enumerate((O_DQ, O_DK, O_DV)):
                            gp = gp_p.get()
                            for cc, ps in fm_group(col0):
                                S.op("act", lambda e, ps=ps, cc=cc, gp=gp: e.activation(out=gp[:, cc, 0:n], in_=ps[:, 0:n],
                                                                                        func=AF.Identity), [ps], [gp])
                            S.dma("pool", gpre_d[gi * 512:(gi + 1) * 512, :].rearrange("(c p) t -> p c t", p=128)[
                                :, :, t0:t0 + n], gp[:, :, 0:n], reads=[gp])
                        zt = zt_p.get()
                        for j, ps in tm_group(O_DZ):
                            S.op("act", lambda e, ps=ps, j=j: e.activation(out=zt[:, j, :], in_=ps[:], func=AF.Silu),
                                 [ps], [zt])
                        S.dma("pool", zs_d[t0:t0 + n, :].rearrange("(j p) c -> p j c", p=128), zt[:, 0:nj, :], reads=[zt])
                        bgt = bg_p.get()
                        for j, ps in tm_group(O_BETA, 16):
                            S.op("act", lambda e, ps=ps, j=j: e.activation(out=bgt[:, j, 0:8], in_=ps[:, 0:8],
                                                                           func=AF.Sigmoid), [ps], [bgt])
                            xa = sm_p.get()
                            S.op("dve", lambda e, ps=ps, xa=xa: e.tensor_tensor(out=xa[:, 0:8], in0=ps[:, 8:16],
                                                                                in1=dtb_b[:], op=ALU.add),
                                 [ps, dtb_b], [xa])
                            ab = sm_p.get()
                            S.op("dve", lambda e, xa=xa, ab=ab: e.tensor_scalar(out=ab[:, 0:8], in0=xa[:, 0:8], scalar1=-1.0,
                                                                                scalar2=None, op0=ALU.mult), [xa], [ab])
                            S.op("dve", lambda e, xa=xa, ab=ab: e.tensor_tensor(out=ab[:, 0:8], in0=ab[:, 0:8],
                                                                                in1=xa[:, 0:8], op=ALU.min), [xa, ab], [ab])
                            S.op("act", lambda e, ab=ab: e.activation(out=ab[:, 0:8], in_=ab[:, 0:8], func=AF.Exp),
                                 [ab], [ab])
                            S.op("act", lambda e, ab=ab: e.activation(out=ab[:, 0:8], in_=ab[:, 0:8], func=AF.Ln,
                                                                      bias=1.0), [ab], [ab])
                            S.op("dve", lambda e, xa=xa, ab=ab: e.scalar_tensor_tensor(
                                out=ab[:, 0:8], in0=xa[:, 0:8], scalar=0.0, in1=ab[:, 0:8], op0=ALU.max, op1=ALU.add),
                                [xa, ab], [ab])
                            S.op("dve", lambda e, ab=ab, j=j: e.tensor_tensor(out=bgt[:, j, 8:16], in0=ab[:, 0:8],
                                                                              in1=alog_b[:], op=ALU.mult),
                                 [ab, alog_b], [bgt])
                        S.dma("pool", bg_d[t0:t0 + n, :].rearrange("(j p) c -> p j c", p=128), bgt[:, 0:nj, :],
                              reads=[bgt])
                    S.barrier()

                if dbg == "p1":
                    break
                phase_attn(nc, S, PSB, ident, eps_t, l, b, NB, CTX, T, last, lam_t, subg, lam_init, QT_d, KT_d, V_d, yT_d)
                if dbg == "attn":
                    break
                phase_gdn(nc, S, PSB, ident, eps_t, l, b, CTX, T, cwq, gdng, c_gdn, gpre_d, gqT_d, gkT_d, gkt_d, gvt_d,
                          zs_d, bg_d, of_d, yT_d, dbg=dbg)
                if dbg is not None and dbg.startswith("g"):
                    break
                phase_merge_ffn(nc, S, PSB, ident, eps_t, l, b, NB, CTX, T, last, stream, tiles, seg_bounds, norm_mod_T,
                                Acol2, Bcol2, cwf, modrow_d, wb_in, wb_br, wb_o, wb_up, wb_dn, hT_d, h2T_d, yT_d)
            if dbg is not None:
                break
    S.barrier()
    return nc, S


def phase_attn(nc, S, PSB, ident, eps_t, l, b, NB, CTX, T, last, lam_t, subg, lam_init, QT_d, KT_d, V_d, yT_d):
    NKT = T // 128
    with contextlib.ExitStack() as ES:
        cnt = [0]

        def esb(name, shape, dt=F32):
            cnt[0] += 1
            return Tl(ES.enter_context(nc.sbuf_tensor(f"a{name}_{l}_{b}_{cnt[0]}", list(shape), dt)))
        KT = esb("KT", (128, 4, T), BF16)
        V = esb("V", (128, NKT, 4, 130), BF16)
        S.op("dve", lambda e: e.memset(V[:, :, :, 128:130], 1.0), [], [V])
        for h in range(4):
            S.dma("sp", KT[:, h, :], KT_d[h * 128:(h + 1) * 128, :], writes=[KT])
        for kt in range(NKT):
            S.dma("sp", V[:, kt, :, 0:128], V_d[kt * 128:(kt + 1) * 128, :].rearrange("p (h c) -> p h c", h=4),
                  writes=[V])
        QT_p = Rot([esb("QT", (128, 4, 512), BF16) for _ in range(2)])
        PT_p = Rot([esb("PT", (128, 512), BF16) for _ in range(4)])
        o_p = Rot([esb("o", (128, 2, 128)) for _ in range(4)])
        r_p = Rot([esb("r", (128, 4)) for _ in range(8)])
        yb_p = Rot([esb("yb", (128, 4, 128)) for _ in range(2)])
        ybT_p = Rot([esb("ybT", (128, 4, 512), BF16) for _ in range(2)])
        jk_p = Rot([esb("jk", (128, 128)) for _ in range(2)])
        acc = PSB[0:4]
        st_p = Rot(PSB[4:7])
        tr_ps = PSB[7]
        qtiles = []
        if not last:
            qtiles.append((0, CTX, 0, CTX // 128))
        for t0 in range(CTX, T, 512):
            qtiles.append((t0, min(512, T - t0), 0, NKT))
        for (t0, n, ka, kb) in qtiles:
            nj = n // 128
            QT = QT_p.get()
            S.dma("sp", QT[:, :, 0:n], QT_d.rearrange("(h p) t -> p h t", p=128)[:, :, t0:t0 + n], writes=[QT])
            ybT = ybT_p.get()
            om = {}
            for h in range(4):
                for m in range(2):
                    pr = slice(m * 64, (m + 1) * 64)
                    for kt in range(ka, kb):
                        st = st_p.get()
                        S.op("pe", lambda e, st=st, h=h, kt=kt, pr=pr: e.matmul(
                            st[:, 0:n], lhsT=KT[pr, h, kt * 128:(kt + 1) * 128], rhs=QT[pr, h, 0:n], start=True,
                            stop=True), [KT, QT], [st])
                        PT = PT_p.get()
                        S.op("act", lambda e, st=st, PT=PT: e.activation(out=PT[:, 0:n], in_=st[:, 0:n], func=AF.Exp,
                                                                         scale=0.125), [st], [PT])
                        for j in range(nj):
                            S.op("pe", lambda e, j=j, PT=PT, kt=kt, h=h: e.matmul(
                                acc[j][:, 0:129], lhsT=PT[:, j * 128:(j + 1) * 128], rhs=V[:, kt, h, 0:129],
                                start=(kt == ka), stop=(kt == kb - 1)), [PT, V], [acc[j]])
                    for j in range(nj):
                        if m == 0:
                            om[j] = o_p.get()
                        r = r_p.get()
                        S.op("dve", lambda e, j=j, r=r: e.reciprocal(out=r[:, 0:1], in_=acc[j][:, 128:129]),
                             [acc[j]], [r])
                        if m == 1:
                            S.op("dve", lambda e, r=r: e.tensor_tensor(out=r[:, 0:1], in0=r[:, 0:1], in1=lam_t[:, 0:1],
                                                                       op=ALU.mult), [r, lam_t], [r])
                        S.op("act", lambda e, j=j, r=r, m=m, o=om[j]: e.activation(
                            out=o[:, m, :], in_=acc[j][:, 0:128], func=AF.Identity, scale=r[:, 0:1]), [acc[j], r], [om[j]])
                for j in range(nj):
                    o = om[j]
                    S.op("dve", lambda e, o=o: e.tensor_tensor(out=o[:, 0, :], in0=o[:, 0, :], in1=o[:, 1, :],
                                                               op=ALU.add), [o], [o])
                    ssq = r_p.get()
                    jk = jk_p.get()
                    S.op("act", lambda e, o=o, jk=jk, ssq=ssq: e.activation(out=jk[:], in_=o[:, 0, :], func=AF.Square,
                                                                            accum_out=ssq[:, 0:1]), [o], [jk, ssq])
                    rs = r_p.get()
                    S.op("act", lambda e, ssq=ssq, rs=rs: e.activation(out=rs[:, 0:1], in_=ssq[:, 0:1], func=AF.Sqrt,
                                                                       bias=eps_t[:, :], scale=1.0 / 128),
                         [ssq, eps_t], [rs])
                    S.op("dve", lambda e, rs=rs: e.reciprocal(out=rs[:, 0:1], in_=rs[:, 0:1]), [rs], [rs])
                    yj = jk_p.get()
                    S.op("dve", lambda e, o=o, rs=rs, yj=yj: e.scalar_tensor_tensor(
                        out=yj[:], in0=o[:, 0, :], scalar=rs[:, 0:1], in1=subg[:], op0=ALU.mult, op1=ALU.mult),
                        [o, rs, subg], [yj])
                    S.op("pe", lambda e, yj=yj: e.transpose(out=tr_ps[:, 0:128], in_=yj[:], identity=ident[:]),
                         [yj, ident], [tr_ps])
                    S.op("act", lambda e, j=j, h=h: e.activation(out=ybT[:, h, j * 128:(j + 1) * 128],
                                                                 in_=tr_ps[:, 0:128], func=AF.Identity,
                                                                 scale=(1.0 - lam_init)), [tr_ps], [ybT])
            S.dma("pool", yT_d[1].rearrange("(c p) t -> p c t", p=128)[:, :, t0:t0 + n], ybT[:, :, 0:n], reads=[ybT])
        S.barrier()


def phase_gdn(nc, S, PSB, ident, eps_t, l, b, CTX, T, cwq, gdng, c_gdn, gpre_d, gqT_d, gkT_d, gkt_d, gvt_d, zs_d, bg_d,
              of_d, yT_d, dbg=None):
    DK = 128 ** -0.5
    with contextlib.ExitStack() as ES:
        cnt = [0]

        def esb(name, shape, dt=F32):
            cnt[0] += 1
            return Tl(ES.enter_context(nc.sbuf_tensor(f"g{name}_{l}_{b}_{cnt[0]}", list(shape), dt)))
        in_p = Rot([esb("in", (128, 514)) for _ in range(3)])
        t_p = Rot([esb("t", (128, 512)) for _ in range(8)])
        tk_p = Rot([esb("tk", (128, 4, 128)) for _ in range(3)])
        ones = esb("ones", (128, 128))
        S.op("dve", lambda e: e.memset(ones[:], 1.0), [], [ones])
        psr = Rot(PSB)
        segs = [(0, CTX), (CTX, T)]
        for fc in range(12):
            kind = fc // 4
            h = fc % 4
            for (s0, s1) in segs:
                for t0 in range(s0, s1, 512):
                    n = min(512, s1 - t0)
                    xin = in_p.get()
                    lo = max(t0 - 1, s0)
                    hi = min(t0 + n + 1, s1)
                    if lo != t0 - 1 or hi != t0 + n + 1:
                        S.op("pool", lambda e, xin=xin: e.memset(xin[:], 0.0), [], [xin])
                    S.dma("sp", xin[:, lo - (t0 - 1):hi - (t0 - 1)], gpre_d[fc * 128:(fc + 1) * 128, lo:hi], writes=[xin])
                    y = t_p.get()
                    S.op("act", lambda e, xin=xin, y=y: e.activation(out=y[:, 0:n], in_=xin[:, 0:n], func=AF.Identity,
                                                                     scale=cwq[:, fc, 0:1]), [xin, cwq], [y])
                    S.op("dve", lambda e, xin=xin, y=y: e.scalar_tensor_tensor(
                        out=y[:, 0:n], in0=xin[:, 1:n + 1], scalar=cwq[:, fc, 1:2], in1=y[:, 0:n], op0=ALU.mult,
                        op1=ALU.add), [xin, cwq, y], [y])
                    S.op("dve", lambda e, xin=xin, y=y: e.scalar_tensor_tensor(
                        out=y[:, 0:n], in0=xin[:, 2:n + 2], scalar=cwq[:, fc, 2:3], in1=y[:, 0:n], op0=ALU.mult,
                        op1=ALU.add), [xin, cwq, y], [y])
                    S.op("act", lambda e, y=y: e.activation(out=y[:, 0:n], in_=y[:, 0:n], func=AF.Silu), [y], [y])
                    if kind < 2:
                        sq = t_p.get()
                        S.op("dve", lambda e, y=y, sq=sq: e.tensor_tensor(out=sq[:, 0:n], in0=y[:, 0:n], in1=y[:, 0:n],
                                                                          op=ALU.mult), [y], [sq])
                        p2 = psr.get()
                        S.op("pe", lambda e, p2=p2, sq=sq: e.matmul(p2[:, 0:n], lhsT=ones[:], rhs=sq[:, 0:n], start=True,
                                                                    stop=True), [ones, sq], [p2])
                        rs = t_p.get()
                        S.op("act", lambda e, p2=p2, rs=rs: e.activation(out=rs[:, 0:n], in_=p2[:, 0:n], func=AF.Sqrt,
                                                                         bias=eps_t[:, :], scale=1.0), [p2, eps_t], [rs])
                        S.op("dve", lambda e, rs=rs: e.reciprocal(out=rs[:, 0:n], in_=rs[:, 0:n]), [rs], [rs])
                        S.op("dve", lambda e, y=y, rs=rs: e.tensor_tensor(out=y[:, 0:n], in0=y[:, 0:n], in1=rs[:, 0:n],
                                                                          op=ALU.mult), [y, rs], [y])
                        dstT = gqT_d if kind == 0 else gkT_d
                        S.dma("pool", dstT[h * 128:(h + 1) * 128, t0:t0 + n], y[:, 0:n], reads=[y])
                    if kind >= 1:
                        nj = n // 128
                        pt = psr.get()
                        for j in range(nj):
                            S.op("pe", lambda e, j=j, pt=pt, y=y: e.transpose(out=pt[:, j * 128:(j + 1) * 128],
                                                                              in_=y[:, j * 128:(j + 1) * 128],
                                                                              identity=ident[:]), [y, ident], [pt])
                        tk = tk_p.get()
                        S.op("act", lambda e, pt=pt, tk=tk: e.activation(out=tk[:].rearrange("p j d -> p (j d)")[:, 0:n],
                                                                         in_=pt[:, 0:n], func=AF.Identity), [pt], [tk])
                        dst = gkt_d if kind == 1 else gvt_d
                        S.dma("pool", dst[t0:t0 + n, h * 128:(h + 1) * 128].rearrange("(j p) d -> p j d", p=128),
                              tk[:, 0:nj, :], reads=[tk])
        S.barrier()
    if dbg == "g1":
        return
    NCH = T // 64
    NCC = CTX // 64
    order_f = list(range(NCH))
    order_b = list(range(NCC - 1, -1, -1)) + list(range(NCH - 1, NCC - 1, -1))
    with contextlib.ExitStack() as ES:
        cnt = [0]

        def esb(name, shape, dt=F32, zero=False):
            cnt[0] += 1
            t = Tl(ES.enter_context(nc.sbuf_tensor(f"s{name}_{l}_{b}_{cnt[0]}", list(shape), dt)))
            if zero:
                S.op("pool", lambda e: e.memset(t[:], 0.0), [], [t])
            return t
        gc = esb("gc", (64, 6, 8, 64))
        S.dma("sp", gc[:], c_gdn, writes=[gc])
        triC, m_incl, m_strict, m_inclT, id8, ones8 = (gc[:, i] for i in range(6))
        triP = esb("triP", (128, 2, 128), zero=True)
        for d in range(2):
            S.op("dve", lambda e, d=d: e.tensor_copy(out=triP[0:64, d, 0:64], in_=triC[:, d * 4, :]), [gc, triP], [triP])
        nones = esb("nones", (128, 128), zero=True)
        S.op("dve", lambda e: e.memset(nones[0:64, :], -1.0), [nones], [nones])
        ones128 = esb("ones128", (128, 128), zero=True)
        S.op("dve", lambda e: e.memset(ones128[0:64, :], 1.0), [ones128], [ones128])
        St = esb("S", (128, 8, 128), zero=True)
        Sb = esb("Sb", (128, 8, 128), BF16, zero=True)
        NB2 = 2
        qT_p = Rot([esb("qT", (128, 9, 64), zero=True) for _ in range(NB2)])
        kT_p = Rot([esb("kT", (128, 9, 64), zero=True) for _ in range(NB2)])
        kt_p = Rot([esb("kt", (64, 8, 128)) for _ in range(NB2)])
        vt_p = Rot([esb("vt", (64, 8, 128)) for _ in range(NB2)])
        b8_p = Rot([esb("b8", (128, 2, 8), zero=True) for _ in range(NB2)])
        s8_p = Rot([esb("s8", (128, 8)) for _ in range(16)])
        B64 = Rot([esb("B64", (128, 9, 64), zero=True) for _ in range(14)])
        B64b = Rot([esb("B64b", (128, 9, 64), BF16, zero=True) for _ in range(2)])
        B128 = Rot([esb("B128", (128, 8, 128), zero=True) for _ in range(5)])
        B128b = Rot([esb("B128b", (128, 8, 128), BF16, zero=True) for _ in range(4)])
        W64b = Rot([esb("W64b", (128, 9, 64), BF16, zero=True) for _ in range(4)])
        T512 = Rot([esb("T512", (128, 4, 128)) for _ in range(3)])
        psr = Rot(PSB)

        def v3(ps, np_, a, bb):
            return ps[0:np_, 0:a * bb].rearrange("p (a b) -> p a b", a=a)

        def bc(ap, np_, a, bb):
            return ap.unsqueeze(2).to_broadcast([np_, a, bb])

        def f8(X):
            return X[0:64, 0:8, :]

        def l2(X, k):
            return X[:, k:k + 2, :].rearrange("p a b -> p (a b)")

        for s in range(NCH if dbg not in ("g2pre", "g2inv", "g2one") else 1):
            tf = order_f[s] * 64
            tb = order_b[s] * 64
            qT = qT_p.get()
            kT = kT_p.get()
            kt = kt_p.get()
            vt = vt_p.get()
            b8 = b8_p.get()
            for (bo, tt, dcol) in ((0, tf, 0), (4, tb, 4)):
                S.dma("sp", qT[:, bo:bo + 4, :], gqT_d.rearrange("(h p) t -> p h t", p=128)[:, :, tt:tt + 64], writes=[qT])
                S.dma("sp", kT[:, bo:bo + 4, :], gkT_d.rearrange("(h p) t -> p h t", p=128)[:, :, tt:tt + 64], writes=[kT])
                S.dma("sp", kt[:, bo:bo + 4, :], gkt_d[tt:tt + 64, :].rearrange("p (h d) -> p h d", h=4), writes=[kt])
                S.dma("sp", vt[:, bo:bo + 4, :], gvt_d[tt:tt + 64, :].rearrange("p (h d) -> p h d", h=4), writes=[vt])
                S.dma("sp", b8[0:64, 0, bo:bo + 4], bg_d[tt:tt + 64, dcol:dcol + 4], writes=[b8])
                S.dma("sp", b8[0:64, 1, bo:bo + 4], bg_d[tt:tt + 64, 8 + dcol:8 + dcol + 4], writes=[b8])
            beta8 = b8[0:64, 0, :]
            g8 = b8[0:64, 1, :]
            g8p = b8[:, 1, :]
            gTri = B64.get()
            S.op("dve", lambda e, gTri=gTri, g8=g8: e.tensor_tensor(out=f8(gTri), in0=triC, in1=bc(g8, 64, 8, 64),
                                                                    op=ALU.mult), [gc, b8], [gTri])
            psA = psr.get()
            for d in range(2):
                S.op("pe", lambda e, d=d, psA=psA, g8p=g8p: e.matmul(psA[:, d * 4:(d + 1) * 4], lhsT=triP[:, d, :],
                                                                     rhs=g8p[:, d * 4:(d + 1) * 4], start=True, stop=True),
                     [triP, b8], [psA])
            S.op("pe", lambda e, psA=psA, g8p=g8p: e.matmul(psA[:, 8:16], lhsT=ones128[:], rhs=g8p, start=True, stop=True),
                 [ones128, b8], [psA])
            G = s8_p.get()
            Gt = s8_p.get()
            S.op("act", lambda e, G=G, psA=psA: e.activation(out=G[0:64, :], in_=psA[0:64, 0:8], func=AF.Identity),
                 [psA], [G])
            S.op("act", lambda e, Gt=Gt, psA=psA: e.activation(out=Gt[:], in_=psA[:, 8:16], func=AF.Identity),
                 [psA], [Gt])
            eG = s8_p.get()
            S.op("act", lambda e, G=G, eG=eG: e.activation(out=eG[0:64, :], in_=G[0:64, :], func=AF.Exp), [G], [eG])
            gtot = s8_p.get()
            S.op("act", lambda e, Gt=Gt, gtot=gtot: e.activation(out=gtot[:], in_=Gt[:], func=AF.Exp), [Gt], [gtot])
            etl = s8_p.get()
            S.op("dve", lambda e, etl=etl, Gt=Gt, G=G: e.tensor_tensor(out=etl[0:64, :], in0=Gt[0:64, :], in1=G[0:64, :],
                                                                       op=ALU.subtract), [Gt, G], [etl])
            S.op("act", lambda e, etl=etl: e.activation(out=etl[0:64, :], in_=etl[0:64, :], func=AF.Exp), [etl], [etl])
            beG = s8_p.get()
            S.op("dve", lambda e, beG=beG, eG=eG, beta8=beta8: e.tensor_tensor(out=beG[0:64, :], in0=eG[0:64, :],
                                                                               in1=beta8, op=ALU.mult), [eG, b8], [beG])
            eGq = s8_p.get()
            S.op("dve", lambda e, eGq=eGq, eG=eG: e.tensor_scalar(out=eGq[0:64, :], in0=eG[0:64, :], scalar1=DK,
                                                                  scalar2=None, op0=ALU.mult), [eG], [eGq])
            psD = psr.get()
            for k in range(8):
                S.op("pe", lambda e, k=k, psD=psD, gTri=gTri: e.matmul(psD[:, k * 64:(k + 1) * 64], lhsT=l2(gTri, k),
                                                                       rhs=ones128[:, 0:64], start=True, stop=False),
                     [gTri, ones128], [psD])
                S.op("pe", lambda e, k=k, psD=psD, gTri=gTri: e.matmul(psD[:, k * 64:(k + 1) * 64], lhsT=nones[:],
                                                                       rhs=gTri[:, k, :], start=False, stop=True),
                     [gTri, nones], [psD])
            seg = B64.get()
            S.op("dve", lambda e, seg=seg, psD=psD: e.tensor_scalar(out=f8(seg), in0=v3(psD, 64, 8, 64), scalar1=0.0,
                                                                    scalar2=None, op0=ALU.min), [psD], [seg])
            S.op("act", lambda e, seg=seg: e.activation(out=f8(seg), in_=f8(seg), func=AF.Exp), [seg], [seg])
            segS = B64.get()
            S.op("pool", lambda e, seg=seg, segS=segS: e.tensor_tensor(out=f8(segS), in0=f8(seg), in1=m_strict,
                                                                       op=ALU.mult), [seg, gc], [segS])
            sgT = B64.get()
            S.op("dve", lambda e, sgT=sgT, psD=psD: e.tensor_scalar(out=f8(sgT), in0=v3(psD, 64, 8, 64), scalar1=-1.0,
                                                                    scalar2=0.0, op0=ALU.mult, op1=ALU.min),
                 [psD], [sgT])
            S.op("act", lambda e, sgT=sgT: e.activation(out=f8(sgT), in_=f8(sgT), func=AF.Exp), [sgT], [sgT])
            S.op("dve", lambda e, sgT=sgT: e.scalar_tensor_tensor(out=f8(sgT), in0=f8(sgT), scalar=DK, in1=m_inclT,
                                                                  op0=ALU.mult, op1=ALU.mult), [sgT, gc], [sgT])
            psK = psr.get()
            psQ = psr.get()
            for k in range(8):
                S.op("pe", lambda e, k=k, psK=psK, kT=kT: e.matmul(psK[:, k * 64:(k + 1) * 64], lhsT=l2(kT, k),
                                                                   rhs=kT[:, k, :], start=True, stop=True), [kT], [psK])
            for k in range(8):
                S.op("pe", lambda e, k=k, psQ=psQ, kT=kT, qT=qT: e.matmul(psQ[:, k * 64:(k + 1) * 64], lhsT=l2(kT, k),
                                                                          rhs=qT[:, k, :], start=True, stop=True),
                     [kT, qT], [psQ])
            A = B64.get()
            S.op("dve", lambda e, A=A, psK=psK, segS=segS: e.tensor_tensor(out=f8(A), in0=v3(psK, 64, 8, 64), in1=f8(segS),
                                                                           op=ALU.mult), [psK, segS], [A])
            S.op("dve", lambda e, A=A, beta8=beta8: e.tensor_tensor(out=f8(A), in0=f8(A), in1=bc(beta8, 64, 8, 64),
                                                                    op=ALU.mult), [A, b8], [A])
            inT = B64b.get()
            S.op("dve", lambda e, inT=inT, psQ=psQ, sgT=sgT: e.tensor_tensor(out=f8(inT), in0=v3(psQ, 64, 8, 64),
                                                                             in1=f8(sgT), op=ALU.mult), [psQ, sgT], [inT])
            if dbg == "g2pre":
                continue
            psT = psr.get()
            for k in range(8):
                S.op("pe", lambda e, k=k, psT=psT, A=A: e.matmul(psT[:, k * 64:(k + 1) * 64], lhsT=l2(A, k),
                                                                 rhs=ident[:, 0:64], start=True, stop=True),
                     [A, ident], [psT])
            AT = B64.get()
            S.op("act", lambda e, AT=AT, psT=psT: e.activation(out=f8(AT), in_=v3(psT, 64, 8, 64), func=AF.Identity),
                 [psT], [AT])
            TT = B64.get()
            S.op("dve", lambda e, TT=TT, AT=AT: e.tensor_tensor(out=f8(TT), in0=id8, in1=f8(AT),
                                                                op=ALU.subtract), [gc, AT], [TT])
            P, PT_ = A, AT
            for lev in range(1, 6):
                psP = psr.get()
                for k in range(8):
                    S.op("pe", lambda e, k=k, psP=psP, P=P, PT_=PT_: e.matmul(
                        psP[:, k * 64:(k + 1) * 64], lhsT=l2(PT_, k), rhs=P[:, k, :], start=True, stop=True),
                        [P, PT_], [psP])
                if lev < 5:
                    psPT = psr.get()
                    for k in range(8):
                        S.op("pe", lambda e, k=k, psPT=psPT, P=P, PT_=PT_: e.matmul(
                            psPT[:, k * 64:(k + 1) * 64], lhsT=l2(P, k), rhs=PT_[:, k, :], start=True, stop=True),
                            [P, PT_], [psPT])
                Pn = B64.get()
                S.op("act", lambda e, Pn=Pn, psP=psP: e.activation(out=f8(Pn), in_=v3(psP, 64, 8, 64), func=AF.Identity),
                     [psP], [Pn])
                if lev < 5:
                    PTn = B64.get()
                    S.op("dve", lambda e, PTn=PTn, psPT=psPT: e.tensor_copy(out=f8(PTn), in_=v3(psPT, 64, 8, 64)),
                         [psPT], [PTn])
                else:
                    PTn = None
                psZ = psr.get()
                for k in range(8):
                    S.op("pe", lambda e, k=k, psZ=psZ, Pn=Pn, TT=TT: e.matmul(
                        psZ[:, k * 64:(k + 1) * 64], lhsT=l2(Pn, k), rhs=TT[:, k, :], start=True, stop=True),
                        [Pn, TT], [psZ])
                TTn = B64.get()
                S.op("dve", lambda e, TTn=TTn, TT=TT, psZ=psZ: e.tensor_tensor(out=f8(TTn), in0=f8(TT),
                                                                               in1=v3(psZ, 64, 8, 64), op=ALU.add),
                     [TT, psZ], [TTn])
                TT = TTn
                P, PT_ = Pn, PTn
            if dbg == "g2inv":
                continue
            vb = B128.get()
            S.op("pool", lambda e, vb=vb, vt=vt, beta8=beta8: e.tensor_tensor(out=vb[0:64], in0=vt[:],
                                                                              in1=bc(beta8, 64, 8, 128), op=ALU.mult),
                 [vt, b8], [vb])
            kbg = B128.get()
            S.op("dve", lambda e, kbg=kbg, kt=kt, beG=beG: e.tensor_tensor(out=kbg[0:64], in0=kt[:],
                                                                           in1=bc(beG[0:64, :], 64, 8, 128), op=ALU.mult),
                 [kt, beG], [kbg])
            ktl = B128b.get()
            S.op("pool", lambda e, ktl=ktl, kt=kt, etl=etl: e.tensor_tensor(out=ktl[0:64], in0=kt[:],
                                                                            in1=bc(etl[0:64, :], 64, 8, 128),
                                                                            op=ALU.mult), [kt, etl], [ktl])
            qTb = W64b.get()
            S.op("pool", lambda e, qTb=qTb, qT=qT: e.tensor_copy(out=qTb[:, 0:8, :], in_=qT[:, 0:8, :]), [qT], [qTb])
            u = B128.get()
            for hf in range(2):
                psU = psr.get()
                for k4 in range(4):
                    k = hf * 4 + k4
                    S.op("pe", lambda e, k=k, k4=k4, psU=psU, TT=TT, vb=vb: e.matmul(
                        psU[:, k4 * 128:(k4 + 1) * 128], lhsT=l2(TT, k), rhs=vb[:, k, :], start=True, stop=True),
                        [TT, vb], [psU])
                S.op("act", lambda e, hf=hf, psU=psU, u=u: e.activation(out=u[0:64, hf * 4:(hf + 1) * 4, :],
                                                                        in_=v3(psU, 64, 4, 128), func=AF.Identity),
                     [psU], [u])
            psW = psr.get()
            for k in range(8):
                S.op("pe", lambda e, k=k, psW=psW, kbg=kbg, TT=TT: e.matmul(psW[:, k * 64:(k + 1) * 64], lhsT=kbg[:, k, :],
                                                                            rhs=TT[:, k, :], start=True, stop=True),
                     [kbg, TT], [psW])
            wTb = W64b.get()
            S.op("act", lambda e, wTb=wTb, psW=psW: e.activation(out=wTb[:, 0:8, :], in_=v3(psW, 128, 8, 64),
                                                                 func=AF.Identity), [psW], [wTb])
            vn = B128b.get()
            for hf in range(2):
                hs = slice(hf * 4, (hf + 1) * 4)
                psWS = psr.get()
                for k4 in range(4):
                    k = hf * 4 + k4
                    S.op("pe", lambda e, k=k, k4=k4, psWS=psWS, wTb=wTb: e.matmul(
                        psWS[:, k4 * 128:(k4 + 1) * 128], lhsT=l2(wTb, k), rhs=Sb[:, k, :], start=True, stop=True),
                        [wTb, Sb], [psWS])
                S.op("dve", lambda e, hs=hs, psWS=psWS, vn=vn, u=u: e.tensor_tensor(
                    out=vn[0:64, hs, :], in0=u[0:64, hs, :], in1=v3(psWS, 64, 4, 128), op=ALU.subtract), [u, psWS], [vn])
                psQS = psr.get()
                for k4 in range(4):
                    k = hf * 4 + k4
                    S.op("pe", lambda e, k=k, k4=k4, psQS=psQS, qTb=qTb: e.matmul(
                        psQS[:, k4 * 128:(k4 + 1) * 128], lhsT=l2(qTb, k), rhs=Sb[:, k, :], start=True, stop=True),
                        [qTb, Sb], [psQS])
                psIV = psr.get()
                for k4 in range(4):
                    k = hf * 4 + k4
                    S.op("pe", lambda e, k=k, k4=k4, psIV=psIV, inT=inT, vn=vn: e.matmul(
                        psIV[:, k4 * 128:(k4 + 1) * 128], lhsT=l2(inT, k), rhs=vn[:, k, :], start=True, stop=True),
                        [inT, vn], [psIV])
                o = T512.get()
                S.op("dve", lambda e, o=o, psQS=psQS, eGq=eGq, hs=hs: e.tensor_tensor(
                    out=o[0:64], in0=v3(psQS, 64, 4, 128), in1=bc(eGq[0:64, hs], 64, 4, 128), op=ALU.mult),
                    [psQS, eGq], [o])
                S.op("dve", lambda e, o=o, psIV=psIV: e.tensor_tensor(out=o[0:64], in0=o[0:64], in1=v3(psIV, 64, 4, 128),
                                                                      op=ALU.add), [o, psIV], [o])
                tt = tf if hf == 0 else tb
                S.dma("pool", of_d[hf, tt:tt + 64, :].rearrange("p (h d) -> p h d", h=4), o[0:64], reads=[o])
                psKV = psr.get()
                for k4 in range(4):
                    k = hf * 4 + k4
                    S.op("pe", lambda e, k=k, k4=k4, psKV=psKV, ktl=ktl, vn=vn: e.matmul(
                        psKV[:, k4 * 128:(k4 + 1) * 128], lhsT=ktl[:, k, :], rhs=vn[:, k, :], start=True, stop=True),
                        [ktl, vn], [psKV])
                sd = T512.get()
                S.op("pool", lambda e, sd=sd, hs=hs, gtot=gtot: e.tensor_tensor(
                    out=sd[:], in0=St[:, hs, :], in1=bc(gtot[:, hs], 128, 4, 128), op=ALU.mult), [St, gtot], [sd])
                S.op("dve", lambda e, sd=sd, hs=hs, psKV=psKV: e.tensor_tensor(
                    out=St[:, hs, :], in0=sd[:], in1=v3(psKV, 128, 4, 128), op=ALU.add), [sd, psKV], [St])
                S.op("act", lambda e, hs=hs: e.activation(out=Sb[:, hs, :], in_=St[:, hs, :], func=AF.Identity),
                     [St], [Sb])
        S.barrier()
    if dbg is not None and dbg.startswith("g2"):
        return
    with contextlib.ExitStack() as ES:
        cnt = [0]

        def esb(name, shape, dt=F32):
            cnt[0] += 1
            return Tl(ES.enter_context(nc.sbuf_tensor(f"o{name}_{l}_{b}_{cnt[0]}", list(shape), dt)))
        of_p = Rot([esb("of", (128, 512)) for _ in range(2)])
        ob_p = Rot([esb("ob", (128, 512)) for _ in range(2)])
        z_p = Rot([esb("z", (128, 512)) for _ in range(2)])
        sq_p = Rot([esb("sq", (128, 512)) for _ in range(2)])
        st_p = Rot([esb("st", (128, 4)) for _ in range(4)])
        yT_p = Rot([esb("yT", (128, 4, 128), BF16) for _ in range(2)])
        psr = Rot(PSB)
        for t0 in range(0, T, 128):
            of = of_p.get()
            ob = ob_p.get()
            z = z_p.get()
            S.dma("sp", of[:], of_d[0, t0:t0 + 128, :], writes=[of])
            S.dma("sp", ob[:], of_d[1, t0:t0 + 128, :], writes=[ob])
            S.dma("sp", z[:], zs_d[t0:t0 + 128, :], writes=[z])
            S.op("dve", lambda e, of=of, ob=ob: e.tensor_tensor(out=of[:], in0=of[:], in1=ob[:], op=ALU.add), [of, ob], [of])
            sq = sq_p.get()
            S.op("pool", lambda e, of=of, sq=sq: e.tensor_tensor(out=sq[:], in0=of[:], in1=of[:], op=ALU.mult), [of], [sq])
            ssq = st_p.get()
            S.op("dve", lambda e, sq=sq, ssq=ssq: e.tensor_reduce(out=ssq[:], in_=sq[:].rearrange("p (h d) -> p h d", h=4),
                                                                  axis=AX.X, op=ALU.add), [sq], [ssq])
            rs = st_p.get()
            S.op("act", lambda e, ssq=ssq, rs=rs: e.activation(out=rs[:], in_=ssq[:], func=AF.Sqrt, bias=eps_t[:, :],
                                                               scale=1.0 / 128), [ssq, eps_t], [rs])
            S.op("dve", lambda e, rs=rs: e.reciprocal(out=rs[:], in_=rs[:]), [rs], [rs])
            S.op("dve", lambda e, of=of, rs=rs: e.tensor_tensor(
                out=of[:].rearrange("p (h d) -> p h d", h=4), in0=of[:].rearrange("p (h d) -> p h d", h=4),
                in1=rs[:].unsqueeze(2).to_broadcast([128, 4, 128]), op=ALU.mult), [of, rs], [of])
            S.op("pool", lambda e, of=of: e.tensor_tensor(out=of[:], in0=of[:], in1=gdng[:], op=ALU.mult), [of, gdng], [of])
            S.op("dve", lambda e, of=of, z=z: e.tensor_tensor(out=of[:], in0=of[:], in1=z[:], op=ALU.mult), [of, z], [of])
            pt = psr.get()
            for h in range(4):
                S.op("pe", lambda e, h=h, pt=pt, of=of: e.transpose(out=pt[:, h * 128:(h + 1) * 128],
                                                                    in_=of[:, h * 128:(h + 1) * 128], identity=ident[:]),
                     [of, ident], [pt])
            yT = yT_p.get()
            S.op("act", lambda e, pt=pt, yT=yT: e.activation(out=yT[:].rearrange("p h t -> p (h t)"), in_=pt[:],
                                                             func=AF.Identity), [pt], [yT])
            S.dma("pool", yT_d[2].rearrange("(c p) t -> p c t", p=128)[:, :, t0:t0 + 128], yT[:], reads=[yT])
        S.barrier()


def phase_merge_ffn(nc, S, PSB, ident, eps_t, l, b, NB, CTX, T, last, stream, tiles, seg_bounds, norm_mod_T, Acol2, Bcol2,
                    cwf, modrow_d, wb_in, wb_br, wb_o, wb_up, wb_dn, hT_d, h2T_d, yT_d):
    with contextlib.ExitStack() as ES:
        cnt = [0]

        def esb(name, shape, dt=F32):
            cnt[0] += 1
            return Tl(ES.enter_context(nc.sbuf_tensor(f"m{name}_{l}_{b}_{cnt[0]}", list(shape), dt)))
        wg = esb("wg", (128, 8, 3072), BF16)
        wbr = esb("wbr", (128, 3, 4, D), BF16)
        wo = esb("wo", (128, 8, D), BF16)
        for kc in range(8):
            S.dma("sp", wg[:, kc, :], wb_in[kc * 128:(kc + 1) * 128, O_GATE:O_GATE + 3072], writes=[wg])
            S.dma("sp", wo[:, kc, :], wb_o[kc * 128:(kc + 1) * 128, :], writes=[wo])
        for br in range(3):
            S.dma("sp", wbr[:, br, :, :], wb_br[br].rearrange("(c p) n -> p c n", p=128), writes=[wbr])
        gtb = esb("gtb", (128, 2, D))
        S.dma("sp", gtb[:, 0, :], modrow_d[0, b].partition_broadcast(128), writes=[gtb])
        S.dma("sp", gtb[:, 1, :], modrow_d[0, NB].partition_broadcast(128), writes=[gtb])
        hT_p = Rot([esb("hT", (128, 8, 512), BF16) for _ in range(1)])
        yb_p = Rot([esb("yb", (128, 3, 4, 512), BF16) for _ in range(1)])
        yT_p = Rot([esb("yT", (128, 8, 512), BF16) for _ in range(1)])
        sg_p = Rot([esb("sg", (128, 512)) for _ in range(4)])
        ac_p = Rot([esb("ac", (128, 512)) for _ in range(3)])
        xt_p = Rot([esb("xt", (128, 4, D)) for _ in range(1)])
        xn_p = Rot([esb("xn", (128, 4, D)) for _ in range(1)])
        junk = Rot([esb("junk", (128, D)) for _ in range(1)])
        stat = Rot([esb("stat", (128, 8)) for _ in range(6)])
        h2_p = Rot([esb("h2", (128, 8, 512), BF16) for _ in range(1)])
        psr = Rot(PSB)
        for (t0, n) in tiles(512, lat_only=last):
            nj = n // 128
            ri = NB if t0 < CTX else b
            gi = 1 if t0 < CTX else 0
            hT = hT_p.get()
            S.dma("sp", hT[:, :, 0:n], hT_d.rearrange("(kc p) t -> p kc t", p=128)[:, :, t0:t0 + n], writes=[hT])
            yb = yb_p.get()
            for br in range(3):
                S.dma("sp", yb[:, br, :, 0:n], yT_d[br].rearrange("(c p) t -> p c t", p=128)[:, :, t0:t0 + n], writes=[yb])
            xt = xt_p.get()
            S.dma("sp", xt[:, 0:nj, :], stream(b, t0, n).rearrange("(j p) d -> p j d", p=128), writes=[xt])
            yT = yT_p.get()
            for fc in range(8):
                acc_t = ac_p.get()
                for br in range(3):
                    pg = psr.get()
                    for kc in range(8):
                        S.op("pe", lambda e, kc=kc, pg=pg, br=br, fc=fc: e.matmul(
                            pg[:, 0:n], lhsT=wg[:, kc, br * D + fc * 128:br * D + (fc + 1) * 128], rhs=hT[:, kc, 0:n],
                            start=(kc == 0), stop=(kc == 7)), [wg, hT], [pg])
                    sg = sg_p.get()
                    S.op("act", lambda e, pg=pg, sg=sg: e.activation(out=sg[:, 0:n], in_=pg[:, 0:n], func=AF.Sigmoid),
                         [pg], [sg])
                    pb = psr.get()
                    for kc in range(4):
                        S.op("pe", lambda e, kc=kc, pb=pb, br=br, fc=fc: e.matmul(
                            pb[:, 0:n], lhsT=wbr[:, br, kc, fc * 128:(fc + 1) * 128], rhs=yb[:, br, kc, 0:n],
                            start=(kc == 0), stop=(kc == 3)), [wbr, yb], [pb])
                    if br == 0:
                        S.op("dve", lambda e, sg=sg, pb=pb, acc_t=acc_t: e.tensor_tensor(
                            out=acc_t[:, 0:n], in0=sg[:, 0:n], in1=pb[:, 0:n], op=ALU.mult), [sg, pb], [acc_t])
                    else:
                        S.op("dve", lambda e, sg=sg, pb=pb: e.tensor_tensor(out=sg[:, 0:n], in0=sg[:, 0:n], in1=pb[:, 0:n],
                                                                            op=ALU.mult), [sg, pb], [sg])
                        if br == 1:
                            S.op("pool", lambda e, sg=sg, acc_t=acc_t: e.tensor_tensor(
                                out=acc_t[:, 0:n], in0=acc_t[:, 0:n], in1=sg[:, 0:n], op=ALU.add), [sg, acc_t], [acc_t])
                        else:
                            S.op("pool", lambda e, sg=sg, acc_t=acc_t, fc=fc: e.tensor_tensor(
                                out=yT[:, fc, 0:n], in0=acc_t[:, 0:n], in1=sg[:, 0:n], op=ALU.add), [sg, acc_t], [yT])
            for j in range(nj):
                for hf in range(2):
                    po = psr.get()
                    for kc in range(8):
                        S.op("pe", lambda e, kc=kc, po=po, j=j, hf=hf: e.matmul(
                            po[:], lhsT=yT[:, kc, j * 128:(j + 1) * 128], rhs=wo[:, kc, hf * 512:(hf + 1) * 512],
                            start=(kc == 0), stop=(kc == 7)), [yT, wo], [po])
                    tm = sg_p.get()
                    S.op("dve", lambda e, po=po, tm=tm, hf=hf: e.tensor_tensor(
                        out=tm[:], in0=po[:], in1=gtb[:, gi, hf * 512:(hf + 1) * 512], op=ALU.mult), [po, gtb], [tm])
                    S.op("pool", lambda e, tm=tm, j=j, hf=hf: e.tensor_tensor(
                        out=xt[:, j, hf * 512:(hf + 1) * 512], in0=xt[:, j, hf * 512:(hf + 1) * 512], in1=tm[:],
                        op=ALU.add), [tm, xt], [xt])
            S.dma("pool", stream(b, t0, n).rearrange("(j p) d -> p j d", p=128), xt[:, 0:nj, :], reads=[xt])
            h2 = h2_p.get()
            norm_mod_T(xt, nj, n, Acol2, Bcol2, ri, h2, (junk, stat, xn_p, psr))
            S.dma("pool", h2T_d.rearrange("(kc p) t -> p kc t", p=128)[:, :, t0:t0 + n], h2[:, :, 0:n], reads=[h2])
        S.barrier()
    NT = 256
    with contextlib.ExitStack() as ES:
        cnt = [0]

        def esb(name, shape, dt=F32):
            cnt[0] += 1
            return Tl(ES.enter_context(nc.sbuf_tensor(f"f{name}_{l}_{b}_{cnt[0]}", list(shape), dt)))
        wu = esb("wu", (128, 8, 2 * D_FF), BF16)
        wd = esb("wd", (128, 22, D), BF16)
        for kc in range(8):
            S.dma("sp", wu[:, kc, :], wb_up[kc * 128:(kc + 1) * 128, :], writes=[wu])
        for c0 in range(0, 22, 2):
            S.dma("sp", wd[:, c0:c0 + 2, :], wb_dn[c0 * 128:(c0 + 2) * 128, :].rearrange("(c p) n -> p c n", p=128),
                  writes=[wd])
        gtb = esb("gtb", (128, 2, D))
        S.dma("sp", gtb[:, 0, :], modrow_d[1, b].partition_broadcast(128), writes=[gtb])
        S.dma("sp", gtb[:, 1, :], modrow_d[1, NB].partition_broadcast(128), writes=[gtb])
        h2_p = Rot([esb("h2", (128, 8, NT + 2), BF16) for _ in range(2)])
        aT_p = Rot([esb("aT", (128, 22, NT), BF16) for _ in range(1)])
        cg_p = Rot([esb("cg", (128, NT)) for _ in range(4)])
        xt_p = Rot([esb("xt", (128, 2, D)) for _ in range(1)])
        tm_p = Rot([esb("tm", (128, 512)) for _ in range(3)])
        psr = Rot(PSB)
        for (t0, n) in tiles(NT, lat_only=last):
            nj = n // 128
            gi = 1 if t0 < CTX else 0
            s0, s1 = seg_bounds(t0)
            lo = max(t0 - 1, s0)
            hi = min(t0 + n + 1, s1)
            h2 = h2_p.get()
            if lo != t0 - 1 or hi != t0 + n + 1:
                S.op("pool", lambda e, h2=h2: e.memset(h2[:], 0.0), [], [h2])
            S.dma("sp", h2[:, :, lo - (t0 - 1):hi - (t0 - 1)], h2T_d.rearrange("(kc p) t -> p kc t", p=128)[:, :, lo:hi],
                  writes=[h2])
            xt = xt_p.get()
            S.dma("sp", xt[:, 0:nj, :], stream(b, t0, n).rearrange("(j p) d -> p j d", p=128), writes=[xt])
            aT = aT_p.get()
            for cc in range(22):
                cgv = []
                for gv in range(2):
                    col = gv * D_FF + cc * 128
                    wi = gv * 22 + cc
                    pu = psr.get()
                    for kc in range(8):
                        S.op("pe", lambda e, kc=kc, pu=pu, col=col: e.matmul(
                            pu[:, 0:n + 2], lhsT=wu[:, kc, col:col + 128], rhs=h2[:, kc, 0:n + 2], start=(kc == 0),
                            stop=(kc == 7)), [wu, h2], [pu])
                    cg = cg_p.get()
                    S.op("act", lambda e, pu=pu, cg=cg, wi=wi: e.activation(out=cg[:, 0:n], in_=pu[:, 0:n],
                                                                            func=AF.Identity, scale=cwf[:, wi, 0:1]),
                         [pu, cwf], [cg])
                    S.op("dve", lambda e, pu=pu, cg=cg, wi=wi: e.scalar_tensor_tensor(
                        out=cg[:, 0:n], in0=pu[:, 1:n + 1], scalar=cwf[:, wi, 1:2], in1=cg[:, 0:n], op0=ALU.mult,
                        op1=ALU.add), [pu, cwf, cg], [cg])
                    S.op("dve", lambda e, pu=pu, cg=cg, wi=wi: e.scalar_tensor_tensor(
                        out=cg[:, 0:n], in0=pu[:, 2:n + 2], scalar=cwf[:, wi, 2:3], in1=cg[:, 0:n], op0=ALU.mult,
                        op1=ALU.add), [pu, cwf, cg], [cg])
                    cgv.append(cg)
                S.op("act", lambda e, cg=cgv[0]: e.activation(out=cg[:, 0:n], in_=cg[:, 0:n], func=AF.Silu),
                     [cgv[0]], [cgv[0]])
                S.op("pool", lambda e, cc=cc, a=cgv[0], v=cgv[1]: e.tensor_tensor(out=aT[:, cc, 0:n], in0=a[:, 0:n],
                                                                                  in1=v[:, 0:n], op=ALU.mult),
                     [cgv[0], cgv[1]], [aT])
            for j in range(nj):
                for hf in range(2):
                    po = psr.get()
                    for cc in range(22):
                        S.op("pe", lambda e, cc=cc, po=po, j=j, hf=hf: e.matmul(
                            po[:], lhsT=aT[:, cc, j * 128:(j + 1) * 128], rhs=wd[:, cc, hf * 512:(hf + 1) * 512],
                            start=(cc == 0), stop=(cc == 21)), [aT, wd], [po])
                    tm = tm_p.get()
                    S.op("dve", lambda e, po=po, tm=tm, hf=hf: e.tensor_tensor(
                        out=tm[:], in0=po[:], in1=gtb[:, gi, hf * 512:(hf + 1) * 512], op=ALU.mult), [po, gtb], [tm])
                    S.op("dve", lambda e, tm=tm, j=j, hf=hf: e.tensor_tensor(
                        out=xt[:, j, hf * 512:(hf + 1) * 512], in0=xt[:, j, hf * 512:(hf + 1) * 512], in1=tm[:],
                        op=ALU.add), [tm, xt], [xt])
            S.dma("pool", stream(b, t0, n).rearrange("(j p) d -> p j d", p=128), xt[:, 0:nj, :], reads=[xt])
        S.barrier()


_CACHE = {}


def run(inputs, n_cores, NB, SEQ, CTX, DEPTH):
    key = (NB, SEQ, CTX, DEPTH)
    if key not in _CACHE:
        _CACHE[key] = build(NB, SEQ, CTX, DEPTH)[0]
    nc = _CACHE[key]
    consts = host_consts(SEQ, CTX)
    shared = {k: np.ascontiguousarray(np.asarray(v, dtype=np.float32)) for k, v in inputs.items()
              if k not in ("x", "c", "ctx")}
    in_maps = []
    for i in range(n_cores):
        m = dict(shared)
        m.update(consts)
        for k in ("x", "c", "ctx"):
            m[k] = np.ascontiguousarray(np.asarray(inputs[k][i * NB:(i + 1) * NB], dtype=np.float32))
        in_maps.append(m)
    res = run_bass_kernel_spmd(nc, in_maps, core_ids=list(range(n_cores)))
    return np.concatenate([np.asarray(r["out"]) for r in res.results], axis=0).astype(np.float32)


def kernel(**inputs):
    return run(inputs, 8, 2, 4096, 256, 2)
```

```python
import math
import contextlib
import numpy as np
import concourse.bass as bass
import concourse.mybir as mybir
from concourse.bass_utils import run_bass_kernel_spmd

F32 = mybir.dt.float32
BF16 = mybir.dt.bfloat16
AF = mybir.ActivationFunctionType
ALU = mybir.AluOpType
AX = mybir.AxisListType

D = 1024
GRID_W = 64
A_W = 512
B_H = 4
C_H = 4
D_FF = 2816
EPS = 1e-6
D_IN = 7696
O_U, O_SV, O_BQ, O_BK, O_BV, O_DQ, O_DK, O_DV, O_DZ, O_BETA, O_GATE = (
    0, 512, 1024, 1536, 2048, 2560, 3072, 3584, 4096, 4608, 4624)
EPOCH = 30000
GELU_C = 2.0 * math.sqrt(2.0 / math.pi)


class Tl:
    __slots__ = ("t", "lw", "rd")

    def __init__(self, t):
        self.t = t
        self.lw = None
        self.rd = {}

    def __getitem__(self, idx):
        return self.t[idx]


class Sched:
    def __init__(self, nc, n_dma_slots=8):
        self.nc = nc
        self.eng = {"pe": nc.tensor, "act": nc.scalar, "dve": nc.vector, "pool": nc.gpsimd, "sp": nc.sync}
        self.cnt = {e: 0 for e in self.eng}
        self.sems = {e: [] for e in self.eng}
        self.seen = {e: {} for e in self.eng}
        self.dq = {}
        self.n_dma_slots = n_dma_slots
        self.ninst = 0

    def _esem(self, e, idx):
        ep = (idx - 1) // EPOCH
        while len(self.sems[e]) <= ep:
            self.sems[e].append(self.nc.alloc_semaphore(f"s_{e}_{len(self.sems[e])}"))
        return self.sems[e][ep], (idx - 1) % EPOCH + 1

    def _wait(self, e, tok):
        if tok is None:
            return
        if tok[0] == "eng":
            _, f, idx = tok
            if f == e and e == "pe":
                return
            sem, val = self._esem(f, idx)
            key = (f, (idx - 1) // EPOCH)
        else:
            _, sem, val, key = tok
        if self.seen[e].get(key, 0) >= val:
            return
        self.seen[e][key] = val
        self.eng[e].wait_ge(sem, val)
        self.ninst += 1

    def _deps(self, e, reads, writes):
        for t in reads:
            self._wait(e, t.lw)
        for t in writes:
            self._wait(e, t.lw)
            for tok in list(t.rd.values()):
                self._wait(e, tok)

    def _mark(self, tok, rkey, reads, writes):
        for t in reads:
            t.rd[rkey] = tok
        for t in writes:
            t.lw = tok
            t.rd = {}

    def op(self, e, fn, reads=(), writes=()):
        if e == "pool":
            e = "dve"
        self._deps(e, reads, writes)
        inst = fn(self.eng[e])
        self.cnt[e] += 1
        idx = self.cnt[e]
        sem, _ = self._esem(e, idx)
        inst.then_inc(sem, 1)
        self.ninst += 1
        self._mark(("eng", e, idx), e, reads, writes)

    def dma(self, q, out, in_, reads=(), writes=(), **kw):
        if q not in self.dq:
            self.dq[q] = {"slots": [[self.nc.alloc_semaphore(f"d_{q}_{i}"), 0]
                                    for i in range(self.n_dma_slots)], "i": 0}
        d = self.dq[q]
        si = d["i"] % self.n_dma_slots
        d["i"] += 1
        slot = d["slots"][si]
        self._deps(q, reads, writes)
        key = ("dma", q, si)
        if slot[1] > 0:
            self._wait(q, ("dma", slot[0], slot[1], key))
        inst = self.eng[q].dma_start(out=out, in_=in_, **kw)
        slot[1] += 16
        inst.then_inc(slot[0], 16)
        self.ninst += 1
        self._mark(("dma", slot[0], slot[1], key), key, reads, writes)

    def barrier(self):
        toks = []
        for f in self.eng:
            if self.cnt[f] > 0:
                toks.append(("eng", f, self.cnt[f]))
        for q, d in self.dq.items():
            for si, slot in enumerate(d["slots"]):
                if slot[1] > 0:
                    toks.append(("dma", slot[0], slot[1], ("dma", q, si)))
        for e in self.eng:
            for tok in toks:
                if tok[0] == "eng" and tok[1] == e:
                    continue
                self._wait(e, tok)


class Rot:
    def __init__(self, tiles):
        self.tiles = tiles
        self.i = 0

    def get(self):
        t = self.tiles[self.i % len(self.tiles)]
        self.i += 1
        return t


def host_consts(SEQ, CTX):
    T = CTX + SEQ
    c = {}
    c["c_ident"] = np.eye(128, dtype=np.float32)
    p = np.arange(128)
    c["c_blk64"] = (p[:, None] // 64 == p[None, :] // 64).astype(np.float32)
    R = np.zeros((64, 64), np.float32)
    for base in (0, 32):
        for i in range(16):
            R[base + i, base + 16 + i] = -1.0
            R[base + 16 + i, base + i] = 1.0
    R2 = np.zeros((128, 128), np.float32)
    R2[:64, :64] = R
    R2[64:, 64:] = R
    c["c_rotT"] = np.ascontiguousarray(R2.T)
    n_freq = 16
    inv_freq = (np.float32(10000.0) ** (-np.arange(n_freq, dtype=np.float32) / np.float32(n_freq))).astype(np.float32)
    rows = SEQ // GRID_W
    row = np.repeat(np.arange(rows, dtype=np.float32), GRID_W)
    col = np.tile(np.arange(GRID_W, dtype=np.float32), rows)
    ang_r = (row[:, None] * inv_freq).astype(np.float32)
    ang_c = (col[:, None] * inv_freq).astype(np.float32)
    cos64 = np.concatenate([np.cos(ang_r), np.cos(ang_r), np.cos(ang_c), np.cos(ang_c)], axis=1).astype(np.float32)
    sin64 = np.concatenate([np.sin(ang_r), np.sin(ang_r), np.sin(ang_c), np.sin(ang_c)], axis=1).astype(np.float32)
    cos = np.ones((128, T), np.float32)
    sin = np.zeros((128, T), np.float32)
    cos[:, CTX:] = np.concatenate([cos64, cos64], axis=1).T
    sin[:, CTX:] = np.concatenate([sin64, sin64], axis=1).T
    c["c_cos"] = cos
    c["c_sin"] = sin
    i = np.arange(64)
    lo = (i[:, None] >= i[None, :]).astype(np.float32)
    up = (i[:, None] <= i[None, :]).astype(np.float32)
    slo = (i[:, None] > i[None, :]).astype(np.float32)
    sup = (i[:, None] < i[None, :]).astype(np.float32)

    def blk8(f, b):
        return np.ascontiguousarray(np.stack([f] * 4 + [b] * 4, axis=1))
    g = np.zeros((64, 6, 8, 64), np.float32)
    g[:, 0] = blk8(up, lo)
    g[:, 1] = blk8(lo, up)
    g[:, 2] = blk8(slo, sup)
    g[:, 3] = blk8(up, lo)
    g[:, 4] = blk8(np.eye(64, dtype=np.float32), np.eye(64, dtype=np.float32))
    g[:, 5] = 1.0
    c["c_gdn"] = g
    return c


def build(NB, SEQ, CTX, DEPTH, dbg=None):
    T = CTX + SEQ
    nc = bass.Bass("TRN2", target_bir_lowering=False)
    S = Sched(nc)

    def din(name, shape):
        return nc.dram_tensor(name, list(shape), F32, kind="ExternalInput").ap()

    def dscr(name, shape, dt):
        return nc.dram_tensor(name, list(shape), dt, kind="Internal").ap()

    L = DEPTH
    x_in = din("x", (NB, SEQ, D))
    c_in = din("c", (NB, D))
    ctx_in = din("ctx", (NB, CTX, D))
    cctx_in = din("c_ctx", (D,))
    W = {}
    for name, shape in [("w_mod", (L, D, 6 * D)), ("b_mod", (L, 6 * D)), ("norm1_g", (L, D)), ("w_in", (L, D, D_IN)),
                        ("sgu_norm_g", (L, 4, 128)), ("sgu_w", (L, 4, 128, 128)), ("sgu_b", (L, 4, 128)),
                        ("w_a_br", (L, 512, D)), ("q_norm_g", (L, 64)), ("k_norm_g", (L, 64)),
                        ("lambda_q1", (L, 64)), ("lambda_k1", (L, 64)), ("lambda_q2", (L, 64)),
                        ("lambda_k2", (L, 64)), ("subln_g", (L, 128)), ("w_b_br", (L, 512, D)),
                        ("conv_qkv_w", (L, 3, 1536)), ("a_log", (L, 2, 4)), ("dt_bias", (L, 2, 4)),
                        ("gdn_norm_g", (L, 128)), ("w_c_br", (L, 512, D)), ("w_o", (L, D, D)),
                        ("norm2_g", (L, D)), ("w_up", (L, D, 2 * D_FF)), ("conv_ffn_w", (L, 3, 2 * D_FF)),
                        ("w_down", (L, D_FF, D))]:
        W[name] = din(name, shape)
    c_ident = din("c_ident", (128, 128))
    c_blk64 = din("c_blk64", (128, 128))
    c_rotT = din("c_rotT", (128, 128))
    c_cos = din("c_cos", (128, T))
    c_sin = din("c_sin", (128, T))
    c_gdn = din("c_gdn", (64, 6, 8, 64))
    out = nc.dram_tensor("out", [NB, SEQ, D], F32, kind="ExternalOutput").ap()

    cx = dscr("cx", (NB, CTX, D), F32)
    wb_in = dscr("wb_in", (D, D_IN), BF16)
    wb_br = dscr("wb_br", (3, 512, D), BF16)
    wb_o = dscr("wb_o", (D, D), BF16)
    wb_up = dscr("wb_up", (D, 2 * D_FF), BF16)
    wb_dn = dscr("wb_dn", (D_FF, D), BF16)
    modrow_d = dscr("modrow", (2, 4, D), F32)
    hT_d = dscr("hT", (D, T), BF16)
    h2T_d = dscr("h2T", (D, T), BF16)
    yT_d = dscr("yT", (3, 512, T), BF16)
    QT_d = dscr("QT", (512, T), BF16)
    KT_d = dscr("KT", (512, T), BF16)
    V_d = dscr("V", (T, 512), BF16)
    gpre_d = dscr("gpre", (1536, T), F32)
    gqT_d = dscr("gqT", (512, T), F32)
    gkT_d = dscr("gkT", (512, T), F32)
    gkt_d = dscr("gkt", (T, 512), F32)
    gvt_d = dscr("gvt", (T, 512), F32)
    zs_d = dscr("zs", (T, 512), F32)
    bg_d = dscr("bg", (T, 16), F32)
    of_d = dscr("of", (2, T, 512), F32)

    R4 = 4
    assert NB + 1 <= R4

    def sb(name, shape, dt=F32):
        return Tl(nc.alloc_sbuf_tensor(name, list(shape), dt))

    PSB = [Tl(nc.alloc_psum_tensor(f"ps{i}", [128, 512], F32)) for i in range(8)]
    ident = sb("ident", (128, 128))
    S.dma("sp", ident[:], c_ident, writes=[ident])
    eps_t = sb("eps_t", (128, 1))
    S.op("dve", lambda e: e.memset(eps_t[:], EPS), [], [eps_t])

    def stream(b, t0, n):
        if t0 < CTX:
            return cx[b, t0:t0 + n, :]
        return out[b, t0 - CTX:t0 - CTX + n, :]

    def tiles(nmax, lat_only=False):
        r = []
        if not lat_only:
            for t0 in range(0, CTX, nmax):
                r.append((t0, min(nmax, CTX - t0)))
        for t0 in range(CTX, T, nmax):
            r.append((t0, min(nmax, T - t0)))
        return r

    def seg_bounds(t0):
        return (0, CTX) if t0 < CTX else (CTX, T)

    for b in range(NB):
        for r0 in range(0, SEQ, 512):
            S.dma("sp", out[b, r0:r0 + 512, :], x_in[b, r0:r0 + 512, :])
        S.dma("sp", cx[b], ctx_in[b])

    def col_load(q, dst_tile, dst_ap, vec_ap, n):
        for c0 in range(0, n, 8):
            c1 = min(n, c0 + 8)
            S.dma(q, dst_ap[:, c0:c1], vec_ap[c0 * 128:c1 * 128].rearrange("(c p) -> p c", p=128),
                  writes=[dst_tile], allow_slow_non_contiguous=True)

    def gelu_ops(es_get, src_ap, src_tl, n, out_ap, out_tl):
        xs = es_get()
        t = es_get()
        S.op("act", lambda e: e.activation(out=xs[:, 0:n], in_=src_ap, func=AF.Identity), [src_tl], [xs])
        S.op("dve", lambda e: e.tensor_tensor(out=t[:, 0:n], in0=xs[:, 0:n], in1=xs[:, 0:n], op=ALU.mult), [xs], [t])
        S.op("dve", lambda e: e.tensor_scalar(out=t[:, 0:n], in0=t[:, 0:n], scalar1=0.044715, scalar2=1.0,
                                              op0=ALU.mult, op1=ALU.add), [t], [t])
        S.op("dve", lambda e: e.tensor_tensor(out=t[:, 0:n], in0=t[:, 0:n], in1=xs[:, 0:n], op=ALU.mult), [t, xs], [t])
        S.op("act", lambda e: e.activation(out=t[:, 0:n], in_=t[:, 0:n], func=AF.Sigmoid, scale=GELU_C), [t], [t])
        S.op("dve", lambda e: e.tensor_tensor(out=out_ap, in0=t[:, 0:n], in1=xs[:, 0:n], op=ALU.mult), [t, xs], [out_tl])

    def rstd_ops(ssq_tl, ssq_ap, out_tl, out_ap, inv_n):
        S.op("act", lambda e: e.activation(out=out_ap, in_=ssq_ap, func=AF.Sqrt, bias=eps_t[0:out_ap.shape[0], :],
                                           scale=inv_n), [ssq_tl, eps_t], [out_tl])
        S.op("dve", lambda e: e.reciprocal(out=out_ap, in_=out_ap), [out_tl], [out_tl])

    def norm_mod_T(xt, nj, n, Acol, Bcol, ri, hT, pools):
        junk, stat, xn_pool, psr = pools
        ssq = stat.get()
        for j in range(nj):
            jk = junk.get()
            S.op("act", lambda e, j=j, jk=jk: e.activation(out=jk[:], in_=xt[:, j, :], func=AF.Square,
                                                           accum_out=ssq[:, j:j + 1]), [xt], [jk, ssq])
        rs = stat.get()
        rstd_ops(ssq, ssq[:, 0:nj], rs, rs[:, 0:nj], 1.0 / D)
        xn = xn_pool.get()
        for j in range(nj):
            S.op("act", lambda e, j=j: e.activation(out=xn[:, j, :], in_=xt[:, j, :], func=AF.Identity,
                                                    scale=rs[:, j:j + 1]), [xt, rs], [xn])
        for kc in range(8):
            ps = psr.get()
            for j in range(nj):
                S.op("pe", lambda e, j=j, kc=kc, ps=ps: e.transpose(out=ps[:, j * 128:(j + 1) * 128],
                                                                     in_=xn[:, j, kc * 128:(kc + 1) * 128],
                                                                     identity=ident[:]), [xn, ident], [ps])
            S.op("act", lambda e, kc=kc, ps=ps: e.activation(out=hT[:, kc, 0:n], in_=ps[:, 0:n], func=AF.Identity,
                                                             scale=Acol[:, kc, ri:ri + 1],
                                                             bias=Bcol[:, kc, ri:ri + 1]), [ps, Acol, Bcol], [hT])

    for l in range(L):
        last = (l == L - 1)
        lam_init = 0.8 - 0.6 * math.exp(-0.3 * l)
        S.barrier()
        for r0 in range(0, D, 128):
            S.dma("pool", wb_in[r0:r0 + 128, :], W["w_in"][l, r0:r0 + 128, :])
            S.dma("pool", wb_up[r0:r0 + 128, :], W["w_up"][l, r0:r0 + 128, :])
            S.dma("pool", wb_o[r0:r0 + 128, :], W["w_o"][l, r0:r0 + 128, :])
        for r0 in range(0, D_FF, 128):
            S.dma("pool", wb_dn[r0:r0 + 128, :], W["w_down"][l, r0:r0 + 128, :])
        for bi, nm in enumerate(("w_a_br", "w_b_br", "w_c_br")):
            for r0 in range(0, 512, 128):
                S.dma("pool", wb_br[bi, r0:r0 + 128, :], W[nm][l, r0:r0 + 128, :])

        with contextlib.ExitStack() as LS:
            def lsb(name, shape, dt=F32):
                return Tl(LS.enter_context(nc.sbuf_tensor(f"{name}_{l}", list(shape), dt)))

            Acol1 = lsb("Acol1", (128, 8, R4))
            Bcol1 = lsb("Bcol1", (128, 8, R4))
            Acol2 = lsb("Acol2", (128, 8, R4))
            Bcol2 = lsb("Bcol2", (128, 8, R4))
            lam_t = lsb("lam", (128, 2))
            wsT = lsb("wsT", (128, 4, 128), BF16)
            bsb = lsb("bsb", (128, 512))
            sgng = lsb("sgng", (128, 512))
            gq_c = lsb("gq_c", (128, 1))
            gk_c = lsb("gk_c", (128, 1))
            subg = lsb("subg", (128, 128))
            gdng = lsb("gdng", (128, 512))
            cwq = lsb("cwq", (128, 12, 3))
            cwf = lsb("cwf", (128, 44, 3))
            alog_b = lsb("alog_b", (128, 8))
            dtb_b = lsb("dtb_b", (128, 8))
            blk64 = lsb("blk64", (128, 128))
            rotT = lsb("rotT", (128, 128), BF16)

            with contextlib.ExitStack() as ES:
                def esb(name, shape, dt=F32):
                    return Tl(ES.enter_context(nc.sbuf_tensor(f"{name}_{l}", list(shape), dt)))
                S.dma("sp", blk64[:], c_blk64, writes=[blk64])
                rt32 = esb("rt32", (128, 128))
                S.dma("sp", rt32[:], c_rotT, writes=[rt32])
                S.op("dve", lambda e: e.tensor_copy(out=rotT[:], in_=rt32[:]), [rt32], [rotT])
                S.dma("sp", bsb[:], W["sgu_b"][l].rearrange("g i -> (g i)").partition_broadcast(128), writes=[bsb])
                S.dma("sp", sgng[:], W["sgu_norm_g"][l].rearrange("g i -> (g i)").partition_broadcast(128),
                      writes=[sgng])
                S.dma("sp", subg[:], W["subln_g"][l].partition_broadcast(128), writes=[subg])
                for h in range(4):
                    S.dma("sp", gdng[:, h * 128:(h + 1) * 128], W["gdn_norm_g"][l].partition_broadcast(128),
                          writes=[gdng])
                S.dma("sp", alog_b[:], W["a_log"][l].rearrange("a b -> (a b)").partition_broadcast(128),
                      writes=[alog_b])
                S.dma("sp", dtb_b[:], W["dt_bias"][l].rearrange("a b -> (a b)").partition_broadcast(128),
                      writes=[dtb_b])
                S.op("act", lambda e: e.activation(out=alog_b[:], in_=alog_b[:], func=AF.Exp), [alog_b], [alog_b])
                S.op("dve", lambda e: e.tensor_scalar(out=alog_b[:], in0=alog_b[:], scalar1=-1.0, scalar2=None,
                                                      op0=ALU.mult), [alog_b], [alog_b])
                for half in range(2):
                    S.dma("sp", gq_c[half * 64:(half + 1) * 64, :], W["q_norm_g"][l].rearrange("(p o) -> p o", o=1),
                          writes=[gq_c])
                    S.dma("sp", gk_c[half * 64:(half + 1) * 64, :], W["k_norm_g"][l].rearrange("(p o) -> p o", o=1),
                          writes=[gk_c])
                for k in range(3):
                    col_load("sp", cwq, cwq[:, :, k], W["conv_qkv_w"][l, k], 12)
                    col_load("sp", cwf, cwf[:, :, k], W["conv_ffn_w"][l, k], 44)
                sw = esb("sw", (128, 4, 128))
                S.dma("sp", sw[:], W["sgu_w"][l].rearrange("g i j -> i g j"), writes=[sw])
                ps = PSB[0]
                for g in range(4):
                    S.op("pe", lambda e, g=g: e.transpose(out=ps[:, g * 128:(g + 1) * 128], in_=sw[:, g, :],
                                                          identity=ident[:]), [sw, ident], [ps])
                S.op("dve", lambda e: e.tensor_copy(out=wsT[:].rearrange("p g i -> p (g i)"), in_=ps[:]), [ps], [wsT])
                lv = esb("lv", (128, 4, 64))
                for i, nm in enumerate(("lambda_q1", "lambda_k1", "lambda_q2", "lambda_k2")):
                    S.dma("sp", lv[:, i, :], W[nm][l].partition_broadcast(128), writes=[lv])
                lp = esb("lp", (128, 2, 64))
                S.op("dve", lambda e: e.tensor_tensor(out=lp[:, 0, :], in0=lv[:, 0, :], in1=lv[:, 1, :], op=ALU.mult),
                     [lv], [lp])
                S.op("dve", lambda e: e.tensor_tensor(out=lp[:, 1, :], in0=lv[:, 2, :], in1=lv[:, 3, :], op=ALU.mult),
                     [lv], [lp])
                ls_ = esb("ls", (128, 2))
                S.op("dve", lambda e: e.tensor_reduce(out=ls_[:], in_=lp[:], axis=AX.X, op=ALU.add), [lp], [ls_])
                S.op("act", lambda e: e.activation(out=ls_[:], in_=ls_[:], func=AF.Exp), [ls_], [ls_])
                S.op("dve", lambda e: e.tensor_tensor(out=lam_t[:, 0:1], in0=ls_[:, 1:2], in1=ls_[:, 0:1],
                                                      op=ALU.subtract), [ls_], [lam_t])
                S.op("dve", lambda e: e.tensor_scalar(out=lam_t[:, 0:1], in0=lam_t[:, 0:1], scalar1=-lam_init,
                                                      scalar2=None, op0=ALU.add), [lam_t], [lam_t])
                scT = esb("scT", (128, 8, R4))
                S.op("dve", lambda e: e.memset(scT[:], 0.0), [], [scT])
                for r in range(NB + 1):
                    src = c_in[r] if r < NB else cctx_in
                    S.dma("sp", scT[:, :, r], src.rearrange("(c p) -> p c", p=128), writes=[scT],
                          allow_slow_non_contiguous=True)
                S.op("act", lambda e: e.activation(out=scT[:], in_=scT[:], func=AF.Silu), [scT], [scT])
                bmc = esb("bmc", (128, 48))
                col_load("sp", bmc, bmc[:], W["b_mod"][l], 48)
                g1c = esb("g1c", (128, 8))
                g2c = esb("g2c", (128, 8))
                col_load("sp", g1c, g1c[:], W["norm1_g"][l], 8)
                col_load("sp", g2c, g2c[:], W["norm2_g"][l], 8)
                bmr = esb("bmr", (R4, 6 * D))
                S.dma("sp", bmr[:], W["b_mod"][l].partition_broadcast(R4), writes=[bmr])
                modT = esb("modT", (128, 6, 8, R4))
                mrow = esb("mrow", (R4, 2, D))
                wmp = Rot([esb(f"wm{i}", (128, 8, 512)) for i in range(2)])
                wmv = W["w_mod"][l].rearrange("(kc p) n -> p kc n", p=128)
                for cb in range(12):
                    seg = cb // 2
                    wm = wmp.get()
                    S.dma("sp", wm[:], wmv[:, :, cb * 512:(cb + 1) * 512], writes=[wm])
                    if seg in (2, 5):
                        ps = PSB[(cb % 2) + 1]
                        for kc in range(8):
                            S.op("pe", lambda e, kc=kc, ps=ps, wm=wm: e.matmul(ps[0:R4, :], lhsT=scT[:, kc, :],
                                                                                rhs=wm[:, kc, :], start=(kc == 0),
                                                                                stop=(kc == 7)), [scT, wm], [ps])
                        gi = 0 if seg == 2 else 1
                        hf = cb % 2
                        S.op("dve", lambda e, ps=ps, gi=gi, hf=hf, cb=cb: e.tensor_tensor(
                            out=mrow[:, gi, hf * 512:(hf + 1) * 512], in0=ps[0:R4, :],
                            in1=bmr[:, cb * 512:(cb + 1) * 512], op=ALU.add), [ps, bmr], [mrow])
                    else:
                        ps = PSB[(cb % 2) + 1]
                        for c4 in range(4):
                            for kc in range(8):
                                S.op("pe", lambda e, kc=kc, c4=c4, ps=ps, wm=wm: e.matmul(
                                    ps[:, c4 * R4:(c4 + 1) * R4], lhsT=wm[:, kc, c4 * 128:(c4 + 1) * 128],
                                    rhs=scT[:, kc, :], start=(kc == 0), stop=(kc == 7)), [scT, wm], [ps])
                        for c4 in range(4):
                            fc = cb * 4 + c4
                            S.op("dve", lambda e, ps=ps, c4=c4, fc=fc, seg=seg: e.tensor_scalar(
                                out=modT[:, seg, fc % 8, :], in0=ps[:, c4 * R4:(c4 + 1) * R4],
                                scalar1=bmc[:, fc:fc + 1], scalar2=None, op0=ALU.add), [ps, bmc], [modT])
                for kc in range(8):
                    S.op("dve", lambda e, kc=kc: e.tensor_scalar(out=Acol1[:, kc, :], in0=modT[:, 1, kc, :], scalar1=1.0,
                                                                 scalar2=g1c[:, kc:kc + 1], op0=ALU.add, op1=ALU.mult),
                         [modT, g1c], [Acol1])
                    S.op("dve", lambda e, kc=kc: e.tensor_scalar(out=Acol2[:, kc, :], in0=modT[:, 4, kc, :], scalar1=1.0,
                                                                 scalar2=g2c[:, kc:kc + 1], op0=ALU.add, op1=ALU.mult),
                         [modT, g2c], [Acol2])
                S.op("dve", lambda e: e.tensor_copy(out=Bcol1[:], in_=modT[:, 0]), [modT], [Bcol1])
                S.op("dve", lambda e: e.tensor_copy(out=Bcol2[:], in_=modT[:, 3]), [modT], [Bcol2])
                S.dma("sp", modrow_d.rearrange("g r d -> r g d"), mrow[:], reads=[mrow])
                S.barrier()
            if dbg == "setup":
                break

            for b in range(NB):
                S.barrier()
                with contextlib.ExitStack() as ES:
                    cnt = [0]

                    def esb(name, shape, dt=F32):
                        cnt[0] += 1
                        return Tl(ES.enter_context(nc.sbuf_tensor(f"{name}_{l}_{b}_{cnt[0]}", list(shape), dt)))
                    xt_p = Rot([esb("xt", (128, 4, D)) for _ in range(1)])
                    xn_p = Rot([esb("xn", (128, 4, D)) for _ in range(1)])
                    junk = Rot([esb("junk", (128, D)) for _ in range(2)])
                    stat = Rot([esb("stat", (128, 8)) for _ in range(8)])
                    hT_p = Rot([esb("hT", (128, 8, 512), BF16) for _ in range(2)])
                    wt_p = Rot([esb("wt", (128, 8, 512), BF16) for _ in range(3)])
                    tmp_p = Rot([esb("tmp", (128, 512)) for _ in range(6)])
                    tb_p = Rot([esb("tb", (128, 512), BF16) for _ in range(4)])
                    uT_p = Rot([esb("uT", (128, 4, 512), BF16) for _ in range(2)])
                    ya_p = Rot([esb("ya", (128, 4, 512), BF16) for _ in range(2)])
                    qk_p = Rot([esb("qk", (128, 4, 512), BF16) for _ in range(3)])
                    cs_p = Rot([esb("cs", (128, 2, 512)) for _ in range(2)])
                    vt_p = Rot([esb("vt", (128, 4, 512), BF16) for _ in range(2)])
                    zt_p = Rot([esb("zt", (128, 4, 512)) for _ in range(1)])
                    gp_p = Rot([esb("gp", (128, 4, 512)) for _ in range(2)])
                    bg_p = Rot([esb("bgt", (128, 4, 16)) for _ in range(2)])
                    sm_p = Rot([esb("sm", (128, 16)) for _ in range(6)])
                    psr = Rot(PSB)
                    wv_in = wb_in.rearrange("(kc p) n -> p kc n", p=128)

                    for (t0, n) in tiles(512):
                        nj = n // 128
                        ri = NB if t0 < CTX else b
                        xt = xt_p.get()
                        S.dma("sp", xt[:, 0:nj, :], stream(b, t0, n).rearrange("(j p) d -> p j d", p=128), writes=[xt])
                        hT = hT_p.get()
                        norm_mod_T(xt, nj, n, Acol1, Bcol1, ri, hT, (junk, stat, xn_p, psr))
                        S.dma("pool", hT_d.rearrange("(kc p) t -> p kc t", p=128)[:, :, t0:t0 + n], hT[:, :, 0:n],
                              reads=[hT])
                        cs = cs_p.get()
                        S.dma("sp", cs[:, 0, 0:n], c_cos[:, t0:t0 + n], writes=[cs])
                        S.dma("sp", cs[:, 1, 0:n], c_sin[:, t0:t0 + n], writes=[cs])

                        def fm_group(col0):
                            wt = wt_p.get()
                            S.dma("sp", wt[:], wv_in[:, :, col0:col0 + 512], writes=[wt])
                            for cc in range(4):
                                ps = psr.get()
                                for kc in range(8):
                                    S.op("pe", lambda e, kc=kc, cc=cc, ps=ps, wt=wt: e.matmul(
                                        ps[:, 0:n], lhsT=wt[:, kc, cc * 128:(cc + 1) * 128], rhs=hT[:, kc, 0:n],
                                        start=(kc == 0), stop=(kc == 7)), [wt, hT], [ps])
                                yield cc, ps

                        def tm_group(col0, ncol=512):
                            wt = wt_p.get()
                            S.dma("sp", wt[:, :, 0:ncol], wv_in[:, :, col0:col0 + ncol], writes=[wt])
                            for j in range(nj):
                                ps = psr.get()
                                for kc in range(8):
                                    S.op("pe", lambda e, kc=kc, j=j, ps=ps, wt=wt: e.matmul(
                                        ps[:, 0:ncol], lhsT=hT[:, kc, j * 128:(j + 1) * 128], rhs=wt[:, kc, 0:ncol],
                                        start=(kc == 0), stop=(kc == 7)), [wt, hT], [ps])
                                yield j, ps

                        uT = uT_p.get()
                        for cc, ps in fm_group(O_U):
                            gelu_ops(tmp_p.get, ps[:, 0:n], ps, n, uT[:, cc, 0:n], uT)
                        ya = ya_p.get()
                        for j, ps in tm_group(O_SV):
                            gv = tmp_p.get()
                            gelu_ops(tmp_p.get, ps[:], ps, 512, gv[:], gv)
                            sq = tmp_p.get()
                            S.op("dve", lambda e, gv=gv, sq=sq: e.tensor_tensor(out=sq[:], in0=gv[:], in1=gv[:],
                                                                                op=ALU.mult), [gv], [sq])
                            ssq = stat.get()
                            S.op("dve", lambda e, sq=sq, ssq=ssq: e.tensor_reduce(
                                out=ssq[:, 0:4], in_=sq[:].rearrange("p (g c) -> p g c", g=4), axis=AX.X, op=ALU.add),
                                [sq], [ssq])
                            rs = stat.get()
                            rstd_ops(ssq, ssq[:, 0:4], rs, rs[:, 0:4], 1.0 / 128)
                            S.op("dve", lambda e, gv=gv, rs=rs: e.tensor_tensor(
                                out=gv[:].rearrange("p (g c) -> p g c", g=4),
                                in0=gv[:].rearrange("p (g c) -> p g c", g=4),
                                in1=rs[:, 0:4].unsqueeze(2).to_broadcast([128, 4, 128]), op=ALU.mult), [gv, rs], [gv])
                            vn = tb_p.get()
                            S.op("dve", lambda e, gv=gv, vn=vn: e.tensor_tensor(out=vn[:], in0=gv[:], in1=sgng[:],
                                                                                op=ALU.mult), [gv, sgng], [vn])
                            pm = psr.get()
                            for g in range(4):
                                S.op("pe", lambda e, g=g, pm=pm, vn=vn: e.matmul(
                                    pm[:, g * 128:(g + 1) * 128], lhsT=vn[:, g * 128:(g + 1) * 128], rhs=wsT[:, g, :],
                                    start=True, stop=True), [vn, wsT], [pm])
                            mx = tmp_p.get()
                            S.op("dve", lambda e, pm=pm, mx=mx: e.tensor_tensor(out=mx[:], in0=pm[:], in1=bsb[:],
                                                                                op=ALU.add), [pm, bsb], [mx])
                            S.op("dve", lambda e, mx=mx, j=j: e.tensor_tensor(
                                out=ya[:, :, j * 128:(j + 1) * 128], in0=mx[:].rearrange("p (g i) -> p g i", g=4),
                                in1=uT[:, :, j * 128:(j + 1) * 128], op=ALU.mult), [mx, uT], [ya])
                        S.dma("pool", yT_d[0].rearrange("(c p) t -> p c t", p=128)[:, :, t0:t0 + n], ya[:, :, 0:n],
                              reads=[ya])
                        for (col0, gcol, dst) in ((O_BQ, gq_c, QT_d), (O_BK, gk_c, KT_d)):
                            qk = qk_p.get()
                            for cc, ps in fm_group(col0):
                                sq = tmp_p.get()
                                S.op("act", lambda e, ps=ps, sq=sq: e.activation(out=sq[:, 0:n], in_=ps[:, 0:n],
                                                                                 func=AF.Square), [ps], [sq])
                                p2 = psr.get()
                                S.op("pe", lambda e, p2=p2, sq=sq: e.matmul(p2[:, 0:n], lhsT=blk64[:], rhs=sq[:, 0:n],
                                                                            start=True, stop=True), [blk64, sq], [p2])
                                rs = tmp_p.get()
                                rstd_ops(p2, p2[:, 0:n], rs, rs[:, 0:n], 1.0 / 64)
                                qn = tb_p.get()
                                S.op("dve", lambda e, ps=ps, rs=rs, qn=qn, gcol=gcol: e.scalar_tensor_tensor(
                                    out=qn[:, 0:n], in0=ps[:, 0:n], scalar=gcol[:, 0:1], in1=rs[:, 0:n], op0=ALU.mult,
                                    op1=ALU.mult), [ps, rs, gcol], [qn])
                                p3 = psr.get()
                                S.op("pe", lambda e, p3=p3, qn=qn: e.matmul(p3[:, 0:n], lhsT=rotT[:], rhs=qn[:, 0:n],
                                                                            start=True, stop=True), [rotT, qn], [p3])
                                t1 = tmp_p.get()
                                S.op("dve", lambda e, qn=qn, t1=t1: e.tensor_tensor(out=t1[:, 0:n], in0=qn[:, 0:n],
                                                                                    in1=cs[:, 0, 0:n], op=ALU.mult),
                                     [qn, cs], [t1])
                                t2 = tmp_p.get()
                                S.op("dve", lambda e, p3=p3, t2=t2: e.tensor_tensor(out=t2[:, 0:n], in0=p3[:, 0:n],
                                                                                    in1=cs[:, 1, 0:n], op=ALU.mult),
                                     [p3, cs], [t2])
                                S.op("pool", lambda e, t1=t1, t2=t2, cc=cc, qk=qk: e.tensor_tensor(
                                    out=qk[:, cc, 0:n], in0=t1[:, 0:n], in1=t2[:, 0:n], op=ALU.add), [t1, t2], [qk])
                            S.dma("pool", dst.rearrange("(c p) t -> p c t", p=128)[:, :, t0:t0 + n], qk[:, :, 0:n],
                                  reads=[qk])
                        vt = vt_p.get()
                        for j, ps in tm_group(O_BV):
                            S.op("act", lambda e, ps=ps, j=j: e.activation(out=vt[:, j, :], in_=ps[:], func=AF.Identity),
                                 [ps], [vt])
                        S.dma("pool", V_d[t0:t0 + n, :].rearrange("(j p) c -> p j c", p=128), vt[:, 0:nj, :], reads=[vt])
                        for gi, col0 in enumerate((O_DQ, O_DK, O_DV)):
                            gp = gp_p.get()
                            for cc, ps in fm_group(col0):
                                S.op("act", lambda e, ps=ps, cc=cc, gp=gp: e.activation(out=gp[:, cc, 0:n], in_=ps[:, 0:n],
                                                                                        func=AF.Identity), [ps], [gp])
                            S.dma("pool", gpre_d[gi * 512:(gi + 1) * 512, :].rearrange("(c p) t -> p c t", p=128)[
                                :, :, t0:t0 + n], gp[:, :, 0:n], reads=[gp])
                        zt = zt_p.get()
                        for j, ps in tm_group(O_DZ):
                            S.op("act", lambda e, ps=ps, j=j: e.activation(out=zt[:, j, :], in_=ps[:], func=AF.Silu),
                                 [ps], [zt])
                        S.dma("pool", zs_d[t0:t0 + n, :].rearrange("(j p) c -> p j c", p=128), zt[:, 0:nj, :], reads=[zt])
                        bgt = bg_p.get()
                        for j, ps in tm_group(O_BETA, 16):
                            S.op("act", lambda e, ps=ps, j=j: e.activation(out=bgt[:, j, 0:8], in_=ps[:, 0:8],
                                                                           func=AF.Sigmoid), [ps], [bgt])
                            xa = sm_p.get()
                            S.op("dve", lambda e, ps=ps, xa=xa: e.tensor_tensor(out=xa[:, 0:8], in0=ps[:, 8:16],
                                                                                in1=dtb_b[:], op=ALU.add),
                                 [ps, dtb_b], [xa])
                            ab = sm_p.get()
                            S.op("dve", lambda e, xa=xa, ab=ab: e.tensor_scalar(out=ab[:, 0:8], in0=xa[:, 0:8], scalar1=-1.0,
                                                                                scalar2=None, op0=ALU.mult), [xa], [ab])
                            S.op("dve", lambda e, xa=xa, ab=ab: e.tensor_tensor(out=ab[:, 0:8], in0=ab[:, 0:8],
                                                                                in1=xa[:, 0:8], op=ALU.min), [xa, ab], [ab])
                            S.op("act", lambda e, ab=ab: e.activation(out=ab[:, 0:8], in_=ab[:, 0:8], func=AF.Exp),
                                 [ab], [ab])
                            S.op("act", lambda e, ab=ab: e.activation(out=ab[:, 0:8], in_=ab[:, 0:8], func=AF.Ln,
                                                                      bias=1.0), [ab], [ab])
                            S.op("dve", lambda e, xa=xa, ab=ab: e.scalar_tensor_tensor(
                                out=ab[:, 0:8], in0=xa[:, 0:8], scalar=0.0, in1=ab[:, 0:8], op0=ALU.max, op1=ALU.add),
                                [xa, ab], [ab])
                            S.op("dve", lambda e, ab=ab, j=j: e.tensor_tensor(out=bgt[:, j, 8:16], in0=ab[:, 0:8],
                                                                              in1=alog_b[:], op=ALU.mult),
                                 [ab, alog_b], [bgt])
                        S.dma("pool", bg_d[t0:t0 + n, :].rearrange("(j p) c -> p j c", p=128), bgt[:, 0:nj, :],
                              reads=[bgt])
                    S.barrier()

                if dbg == "p1":
                    break
                phase_attn(nc, S, PSB, ident, eps_t, l, b, NB, CTX, T, last, lam_t, subg, lam_init, QT_d, KT_d, V_d, yT_d)
                if dbg == "attn":
                    break
                phase_gdn(nc, S, PSB, ident, eps_t, l, b, CTX, T, cwq, gdng, c_gdn, gpre_d, gqT_d, gkT_d, gkt_d, gvt_d,
                          zs_d, bg_d, of_d, yT_d, dbg=dbg)
                if dbg is not None and dbg.startswith("g"):
                    break
                phase_merge_ffn(nc, S, PSB, ident, eps_t, l, b, NB, CTX, T, last, stream, tiles, seg_bounds, norm_mod_T,
                                Acol2, Bcol2, cwf, modrow_d, wb_in, wb_br, wb_o, wb_up, wb_dn, hT_d, h2T_d, yT_d)
            if dbg is not None:
                break
    S.barrier()
    return nc, S


def phase_attn(nc, S, PSB, ident, eps_t, l, b, NB, CTX, T, last, lam_t, subg, lam_init, QT_d, KT_d, V_d, yT_d):
    NKT = T // 128
    with contextlib.ExitStack() as ES:
        cnt = [0]

        def esb(name, shape, dt=F32):
            cnt[0] += 1
            return Tl(ES.enter_context(nc.sbuf_tensor(f"a{name}_{l}_{b}_{cnt[0]}", list(shape), dt)))
        KT = esb("KT", (128, 4, T), BF16)
        V = esb("V", (128, NKT, 4, 130), BF16)
        S.op("dve", lambda e: e.memset(V[:, :, :, 128:130], 1.0), [], [V])
        for h in range(4):
            S.dma("sp", KT[:, h, :], KT_d[h * 128:(h + 1) * 128, :], writes=[KT])
        for kt in range(NKT):
            S.dma("sp", V[:, kt, :, 0:128], V_d[kt * 128:(kt + 1) * 128, :].rearrange("p (h c) -> p h c", h=4),
                  writes=[V])
        QT_p = Rot([esb("QT", (128, 4, 512), BF16) for _ in range(2)])
        PT_p = Rot([esb("PT", (128, 512), BF16) for _ in range(4)])
        o_p = Rot([esb("o", (128, 2, 128)) for _ in range(4)])
        r_p = Rot([esb("r", (128, 4)) for _ in range(8)])
        yb_p = Rot([esb("yb", (128, 4, 128)) for _ in range(2)])
        ybT_p = Rot([esb("ybT", (128, 4, 512), BF16) for _ in range(2)])
        jk_p = Rot([esb("jk", (128, 128)) for _ in range(2)])
        acc = PSB[0:4]
        st_p = Rot(PSB[4:7])
        tr_ps = PSB[7]
        qtiles = []
        if not last:
            qtiles.append((0, CTX, 0, CTX // 128))
        for t0 in range(CTX, T, 512):
            qtiles.append((t0, min(512, T - t0), 0, NKT))
        for (t0, n, ka, kb) in qtiles:
            nj = n // 128
            QT = QT_p.get()
            S.dma("sp", QT[:, :, 0:n], QT_d.rearrange("(h p) t -> p h t", p=128)[:, :, t0:t0 + n], writes=[QT])
            ybT = ybT_p.get()
            om = {}
            for h in range(4):
                for m in range(2):
                    pr = slice(m * 64, (m + 1) * 64)
                    for kt in range(ka, kb):
                        st = st_p.get()
                        S.op("pe", lambda e, st=st, h=h, kt=kt, pr=pr: e.matmul(
                            st[:, 0:n], lhsT=KT[pr, h, kt * 128:(kt + 1) * 128], rhs=QT[pr, h, 0:n], start=True,
                            stop=True), [KT, QT], [st])
                        PT = PT_p.get()
                        S.op("act", lambda e, st=st, PT=PT: e.activation(out=PT[:, 0:n], in_=st[:, 0:n], func=AF.Exp,
                                                                         scale=0.125), [st], [PT])
                        for j in range(nj):
                            S.op("pe", lambda e, j=j, PT=PT, kt=kt, h=h: e.matmul(
                                acc[j][:, 0:129], lhsT=PT[:, j * 128:(j + 1) * 128], rhs=V[:, kt, h, 0:129],
                                start=(kt == ka), stop=(kt == kb - 1)), [PT, V], [acc[j]])
                    for j in range(nj):
                        if m == 0:
                            om[j] = o_p.get()
                        r = r_p.get()
                        S.op("dve", lambda e, j=j, r=r: e.reciprocal(out=r[:, 0:1], in_=acc[j][:, 128:129]),
                             [acc[j]], [r])
                        if m == 1:
                            S.op("dve", lambda e, r=r: e.tensor_tensor(out=r[:, 0:1], in0=r[:, 0:1], in1=lam_t[:, 0:1],
                                                                       op=ALU.mult), [r, lam_t], [r])
                        S.op("act", lambda e, j=j, r=r, m=m, o=om[j]: e.activation(
                            out=o[:, m, :], in_=acc[j][:, 0:128], func=AF.Identity, scale=r[:, 0:1]), [acc[j], r], [om[j]])
                for j in range(nj):
                    o = om[j]
                    S.op("dve", lambda e, o=o: e.tensor_tensor(out=o[:, 0, :], in0=o[:, 0, :], in1=o[:, 1, :],
                                                               op=ALU.add), [o], [o])
                    ssq = r_p.get()
                    jk = jk_p.get()
                    S.op("act", lambda e, o=o, jk=jk, ssq=ssq: e.activation(out=jk[:], in_=o[:, 0, :], func=AF.Square,
                                                                            accum_out=ssq[:, 0:1]), [o], [jk, ssq])
                    rs = r_p.get()
                    S.op("act", lambda e, ssq=ssq, rs=rs: e.activation(out=rs[:, 0:1], in_=ssq[:, 0:1], func=AF.Sqrt,
                                                                       bias=eps_t[:, :], scale=1.0 / 128),
                         [ssq, eps_t], [rs])
                    S.op("dve", lambda e, rs=rs: e.reciprocal(out=rs[:, 0:1], in_=rs[:, 0:1]), [rs], [rs])
                    yj = jk_p.get()
                    S.op("dve", lambda e, o=o, rs=rs, yj=yj: e.scalar_tensor_tensor(
                        out=yj[:], in0=o[:, 0, :], scalar=rs[:, 0:1], in1=subg[:], op0=ALU.mult, op1=ALU.mult),
                        [o, rs, subg], [yj])
                    S.op("pe", lambda e, yj=yj: e.transpose(out=tr_ps[:, 0:128], in_=yj[:], identity=ident[:]),
                         [yj, ident], [tr_ps])
                    S.op("act", lambda e, j=j, h=h: e.activation(out=ybT[:, h, j * 128:(j + 1) * 128],
                                                                 in_=tr_ps[:, 0:128], func=AF.Identity,
                                                                 scale=(1.0 - lam_init)), [tr_ps], [ybT])
            S.dma("pool", yT_d[1].rearrange("(c p) t -> p c t", p=128)[:, :, t0:t0 + n], ybT[:, :, 0:n], reads=[ybT])
        S.barrier()


def phase_gdn(nc, S, PSB, ident, eps_t, l, b, CTX, T, cwq, gdng, c_gdn, gpre_d, gqT_d, gkT_d, gkt_d, gvt_d, zs_d, bg_d,
              of_d, yT_d, dbg=None):
    DK = 128 ** -0.5
    with contextlib.ExitStack() as ES:
        cnt = [0]

        def esb(name, shape, dt=F32):
            cnt[0] += 1
            return Tl(ES.enter_context(nc.sbuf_tensor(f"g{name}_{l}_{b}_{cnt[0]}", list(shape), dt)))
        in_p = Rot([esb("in", (128, 514)) for _ in range(3)])
        t_p = Rot([esb("t", (128, 512)) for _ in range(8)])
        tk_p = Rot([esb("tk", (128, 4, 128)) for _ in range(3)])
        ones = esb("ones", (128, 128))
        S.op("dve", lambda e: e.memset(ones[:], 1.0), [], [ones])
        psr = Rot(PSB)
        segs = [(0, CTX), (CTX, T)]
        for fc in range(12):
            kind = fc // 4
            h = fc % 4
            for (s0, s1) in segs:
                for t0 in range(s0, s1, 512):
                    n = min(512, s1 - t0)
                    xin = in_p.get()
                    lo = max(t0 - 1, s0)
                    hi = min(t0 + n + 1, s1)
                    if lo != t0 - 1 or hi != t0 + n + 1:
                        S.op("pool", lambda e, xin=xin: e.memset(xin[:], 0.0), [], [xin])
                    S.dma("sp", xin[:, lo - (t0 - 1):hi - (t0 - 1)], gpre_d[fc * 128:(fc + 1) * 128, lo:hi], writes=[xin])
                    y = t_p.get()
                    S.op("act", lambda e, xin=xin, y=y: e.activation(out=y[:, 0:n], in_=xin[:, 0:n], func=AF.Identity,
                                                                     scale=cwq[:, fc, 0:1]), [xin, cwq], [y])
                    S.op("dve", lambda e, xin=xin, y=y: e.scalar_tensor_tensor(
                        out=y[:, 0:n], in0=xin[:, 1:n + 1], scalar=cwq[:, fc, 1:2], in1=y[:, 0:n], op0=ALU.mult,
                        op1=ALU.add), [xin, cwq, y], [y])
                    S.op("dve", lambda e, xin=xin, y=y: e.scalar_tensor_tensor(
                        out=y[:, 0:n], in0=xin[:, 2:n + 2], scalar=cwq[:, fc, 2:3], in1=y[:, 0:n], op0=ALU.mult,
                        op1=ALU.add), [xin, cwq, y], [y])
                    S.op("act", lambda e, y=y: e.activation(out=y[:, 0:n], in_=y[:, 0:n], func=AF.Silu), [y], [y])
                    if kind < 2:
                        sq = t_p.get()
                        S.op("dve", lambda e, y=y, sq=sq: e.tensor_tensor(out=sq[:, 0:n], in0=y[:, 0:n], in1=y[:, 0:n],
                                                                          op=ALU.mult), [y], [sq])
                        p2 = psr.get()
                        S.op("pe", lambda e, p2=p2, sq=sq: e.matmul(p2[:, 0:n], lhsT=ones[:], rhs=sq[:, 0:n], start=True,
                                                                    stop=True), [ones, sq], [p2])
                        rs = t_p.get()
                        S.op("act", lambda e, p2=p2, rs=rs: e.activation(out=rs[:, 0:n], in_=p2[:, 0:n], func=AF.Sqrt,
                                                                         bias=eps_t[:, :], scale=1.0), [p2, eps_t], [rs])
                        S.op("dve", lambda e, rs=rs: e.reciprocal(out=rs[:, 0:n], in_=rs[:, 0:n]), [rs], [rs])
                        S.op("dve", lambda e, y=y, rs=rs: e.tensor_tensor(out=y[:, 0:n], in0=y[:, 0:n], in1=rs[:, 0:n],
                                                                          op=ALU.mult), [y, rs], [y])
                        dstT = gqT_d if kind == 0 else gkT_d
                        S.dma("pool", dstT[h * 128:(h + 1) * 128, t0:t0 + n], y[:, 0:n], reads=[y])
                    if kind >= 1:
                        nj = n // 128
                        pt = psr.get()
                        for j in range(nj):
                            S.op("pe", lambda e, j=j, pt=pt, y=y: e.transpose(out=pt[:, j * 128:(j + 1) * 128],
                                                                              in_=y[:, j * 128:(j + 1) * 128],
                                                                              identity=ident[:]), [y, ident], [pt])
                        tk = tk_p.get()
                        S.op("act", lambda e, pt=pt, tk=tk: e.activation(out=tk[:].rearrange("p j d -> p (j d)")[:, 0:n],
                                                                         in_=pt[:, 0:n], func=AF.Identity), [pt], [tk])
                        dst = gkt_d if kind == 1 else gvt_d
                        S.dma("pool", dst[t0:t0 + n, h * 128:(h + 1) * 128].rearrange("(j p) d -> p j d", p=128),
                              tk[:, 0:nj, :], reads=[tk])
        S.barrier()
    if dbg == "g1":
        return
    NCH = T // 64
    NCC = CTX // 64
    order_f = list(range(NCH))
    order_b = list(range(NCC - 1, -1, -1)) + list(range(NCH - 1, NCC - 1, -1))
    with contextlib.ExitStack() as ES:
        cnt = [0]

        zsrc = [None]

        def esb(name, shape, dt=F32, zero=False, r=False):
            cnt[0] += 1
            t = Tl(ES.enter_context(nc.sbuf_tensor(f"s{name}_{l}_{b}_{cnt[0]}", list(shape), dt)))
            if zero and r:
                fl = t[:].rearrange("p a b -> p (a b)")
                S.op("dve", lambda e: e.tensor_copy(out=fl.bitcast(mybir.dt.float32r), in_=zsrc[0][:, 0:fl.shape[1]]),
                     [zsrc[0]], [t])
            elif zero:
                S.op("dve", lambda e: e.memset(t[:], 0.0), [], [t])
            return t
        zsrc[0] = esb("zsrc", (128, 1024), zero=True)
        gc = esb("gc", (64, 6, 8, 64))
        S.dma("sp", gc[:], c_gdn, writes=[gc])
        triC, m_incl, m_strict, m_inclT, id8, ones8 = (gc[:, i] for i in range(6))
        triP = esb("triP", (128, 2, 128), zero=True)
        for d in range(2):
            S.op("dve", lambda e, d=d: e.tensor_copy(out=triP[0:64, d, 0:64], in_=triC[:, d * 4, :]), [gc, triP], [triP])
        nones = esb("nones", (128, 128), zero=True)
        S.op("dve", lambda e: e.memset(nones[0:64, :], -1.0), [nones], [nones])
        ones128 = esb("ones128", (128, 128), zero=True)
        S.op("dve", lambda e: e.memset(ones128[0:64, :], 1.0), [ones128], [ones128])
        identr = esb("identr", (128, 64))
        S.op("dve", lambda e: e.tensor_copy(out=identr[:].bitcast(mybir.dt.float32r), in_=ident[:, 0:64]), [ident], [identr])
        St = esb("S", (128, 8, 128), zero=True)
        Sb = esb("Sb", (128, 8, 128), BF16, zero=True)
        NB2 = 2
        qT_p = Rot([esb("qT", (128, 9, 64), zero=True) for _ in range(NB2)])
        kT_p = Rot([esb("kT", (128, 9, 64), zero=True) for _ in range(NB2)])
        kt_p = Rot([esb("kt", (64, 8, 128)) for _ in range(NB2)])
        vt_p = Rot([esb("vt", (64, 8, 128)) for _ in range(NB2)])
        b8_p = Rot([esb("b8", (128, 2, 8), zero=True) for _ in range(NB2)])
        s8_p = Rot([esb("s8", (128, 8)) for _ in range(16)])
        B64 = Rot([esb("B64", (128, 9, 64), zero=True) for _ in range(8)])
        B64r = Rot([esb("B64r", (128, 9, 64), zero=True, r=True) for _ in range(20)])
        B64b = Rot([esb("B64b", (128, 9, 64), BF16, zero=True) for _ in range(2)])
        B128 = Rot([esb("B128", (128, 8, 128), zero=True) for _ in range(3)])
        B128r = Rot([esb("B128r", (128, 8, 128), zero=True, r=True) for _ in range(4)])
        B128b = Rot([esb("B128b", (128, 8, 128), BF16, zero=True) for _ in range(4)])
        W64b = Rot([esb("W64b", (128, 9, 64), BF16, zero=True) for _ in range(4)])
        T512 = Rot([esb("T512", (128, 4, 128)) for _ in range(3)])
        psr = Rot(PSB)

        def v3(ps, np_, a, bb):
            return ps[0:np_, 0:a * bb].rearrange("p (a b) -> p a b", a=a)

        def bc(ap, np_, a, bb):
            return ap.unsqueeze(2).to_broadcast([np_, a, bb])

        def f8(X):
            return X[0:64, 0:8, :]

        F32R = mybir.dt.float32r

        def l2(X, k):
            return X[:, k:k + 2, :].rearrange("p a b -> p (a b)").bitcast(F32R)

        def l2f(X, k):
            return X[:, k:k + 2, :].rearrange("p a b -> p (a b)")

        def rr(ap):
            return ap.bitcast(F32R)

        for s in range(NCH if dbg not in ("g2pre", "g2inv", "g2one") else 1):
            tf = order_f[s] * 64
            tb = order_b[s] * 64
            qT = qT_p.get()
            kT = kT_p.get()
            kt = kt_p.get()
            vt = vt_p.get()
            b8 = b8_p.get()
            for (bo, tt, dcol) in ((0, tf, 0), (4, tb, 4)):
                S.dma("sp", qT[:, bo:bo + 4, :], gqT_d.rearrange("(h p) t -> p h t", p=128)[:, :, tt:tt + 64], writes=[qT])
                S.dma("sp", kT[:, bo:bo + 4, :], gkT_d.rearrange("(h p) t -> p h t", p=128)[:, :, tt:tt + 64], writes=[kT])
                S.dma("sp", kt[:, bo:bo + 4, :], gkt_d[tt:tt + 64, :].rearrange("p (h d) -> p h d", h=4), writes=[kt])
                S.dma("sp", vt[:, bo:bo + 4, :], gvt_d[tt:tt + 64, :].rearrange("p (h d) -> p h d", h=4), writes=[vt])
                S.dma("sp", b8[0:64, 0, bo:bo + 4], bg_d[tt:tt + 64, dcol:dcol + 4], writes=[b8])
                S.dma("sp", b8[0:64, 1, bo:bo + 4], bg_d[tt:tt + 64, 8 + dcol:8 + dcol + 4], writes=[b8])
            beta8 = b8[0:64, 0, :]
            g8 = b8[0:64, 1, :]
            g8p = b8[:, 1, :]
            gTri = B64.get()
            S.op("dve", lambda e, gTri=gTri, g8=g8: e.tensor_tensor(out=f8(gTri), in0=triC, in1=bc(g8, 64, 8, 64),
                                                                    op=ALU.mult), [gc, b8], [gTri])
            psA = psr.get()
            for d in range(2):
                S.op("pe", lambda e, d=d, psA=psA, g8p=g8p: e.matmul(psA[:, d * 4:(d + 1) * 4], lhsT=triP[:, d, :],
                                                                     rhs=g8p[:, d * 4:(d + 1) * 4], start=True, stop=True),
                     [triP, b8], [psA])
            S.op("pe", lambda e, psA=psA, g8p=g8p: e.matmul(psA[:, 8:16], lhsT=ones128[:], rhs=g8p, start=True, stop=True),
                 [ones128, b8], [psA])
            G = s8_p.get()
            Gt = s8_p.get()
            S.op("act", lambda e, G=G, psA=psA: e.activation(out=G[0:64, :], in_=psA[0:64, 0:8], func=AF.Identity),
                 [psA], [G])
            S.op("act", lambda e, Gt=Gt, psA=psA: e.activation(out=Gt[:], in_=psA[:, 8:16], func=AF.Identity),
                 [psA], [Gt])
            eG = s8_p.get()
            S.op("act", lambda e, G=G, eG=eG: e.activation(out=eG[0:64, :], in_=G[0:64, :], func=AF.Exp), [G], [eG])
            gtot = s8_p.get()
            S.op("act", lambda e, Gt=Gt, gtot=gtot: e.activation(out=gtot[:], in_=Gt[:], func=AF.Exp), [Gt], [gtot])
            etl = s8_p.get()
            S.op("dve", lambda e, etl=etl, Gt=Gt, G=G: e.tensor_tensor(out=etl[0:64, :], in0=Gt[0:64, :], in1=G[0:64, :],
                                                                       op=ALU.subtract), [Gt, G], [etl])
            S.op("act", lambda e, etl=etl: e.activation(out=etl[0:64, :], in_=etl[0:64, :], func=AF.Exp), [etl], [etl])
            beG = s8_p.get()
            S.op("dve", lambda e, beG=beG, eG=eG, beta8=beta8: e.tensor_tensor(out=beG[0:64, :], in0=eG[0:64, :],
                                                                               in1=beta8, op=ALU.mult), [eG, b8], [beG])
            eGq = s8_p.get()
            S.op("dve", lambda e, eGq=eGq, eG=eG: e.tensor_scalar(out=eGq[0:64, :], in0=eG[0:64, :], scalar1=DK,
                                                                  scalar2=None, op0=ALU.mult), [eG], [eGq])
            psD = psr.get()
            for k in range(8):
                S.op("pe", lambda e, k=k, psD=psD, gTri=gTri: e.matmul(psD[:, k * 64:(k + 1) * 64], lhsT=l2f(gTri, k),
                                                                       rhs=ones128[:, 0:64], start=True, stop=False),
                     [gTri, ones128], [psD])
                S.op("pe", lambda e, k=k, psD=psD, gTri=gTri: e.matmul(psD[:, k * 64:(k + 1) * 64], lhsT=nones[:],
                                                                       rhs=gTri[:, k, :], start=False, stop=True),
                     [gTri, nones], [psD])
            seg = B64.get()
            S.op("dve", lambda e, seg=seg, psD=psD: e.tensor_scalar(out=f8(seg), in0=v3(psD, 64, 8, 64), scalar1=0.0,
                                                                    scalar2=None, op0=ALU.min), [psD], [seg])
            S.op("act", lambda e, seg=seg: e.activation(out=f8(seg), in_=f8(seg), func=AF.Exp), [seg], [seg])
            segS = B64.get()
            S.op("pool", lambda e, seg=seg, segS=segS: e.tensor_tensor(out=f8(segS), in0=f8(seg), in1=m_strict,
                                                                       op=ALU.mult), [seg, gc], [segS])
            sgT = B64.get()
            S.op("dve", lambda e, sgT=sgT, psD=psD: e.tensor_scalar(out=f8(sgT), in0=v3(psD, 64, 8, 64), scalar1=-1.0,
                                                                    scalar2=0.0, op0=ALU.mult, op1=ALU.min),
                 [psD], [sgT])
            S.op("act", lambda e, sgT=sgT: e.activation(out=f8(sgT), in_=f8(sgT), func=AF.Exp), [sgT], [sgT])
            S.op("dve", lambda e, sgT=sgT: e.scalar_tensor_tensor(out=f8(sgT), in0=f8(sgT), scalar=DK, in1=m_inclT,
                                                                  op0=ALU.mult, op1=ALU.mult), [sgT, gc], [sgT])
            psK = psr.get()
            psQ = psr.get()
            for k in range(8):
                S.op("pe", lambda e, k=k, psK=psK, kT=kT: e.matmul(psK[:, k * 64:(k + 1) * 64], lhsT=l2f(kT, k),
                                                                   rhs=kT[:, k, :], start=True, stop=True), [kT], [psK])
            for k in range(8):
                S.op("pe", lambda e, k=k, psQ=psQ, kT=kT, qT=qT: e.matmul(psQ[:, k * 64:(k + 1) * 64], lhsT=l2f(kT, k),
                                                                          rhs=qT[:, k, :], start=True, stop=True),
                     [kT, qT], [psQ])
            A1 = B64.get()
            S.op("dve", lambda e, A1=A1, psK=psK, segS=segS: e.tensor_tensor(out=f8(A1), in0=v3(psK, 64, 8, 64),
                                                                             in1=f8(segS), op=ALU.mult), [psK, segS], [A1])
            A = B64r.get()
            S.op("dve", lambda e, A=A, A1=A1, beta8=beta8: e.tensor_tensor(out=rr(f8(A)), in0=f8(A1),
                                                                           in1=bc(beta8, 64, 8, 64), op=ALU.mult),
                 [A1, b8], [A])
            inT = B64b.get()
            S.op("dve", lambda e, inT=inT, psQ=psQ, sgT=sgT: e.tensor_tensor(out=f8(inT), in0=v3(psQ, 64, 8, 64),
                                                                             in1=f8(sgT), op=ALU.mult), [psQ, sgT], [inT])
            if dbg == "g2pre":
                continue
            psT = psr.get()
            for k in range(8):
                S.op("pe", lambda e, k=k, psT=psT, A=A: e.matmul(psT[:, k * 64:(k + 1) * 64], lhsT=l2(A, k),
                                                                 rhs=rr(identr[:]), start=True, stop=True),
                     [A, identr], [psT])
            AT = B64r.get()
            S.op("act", lambda e, AT=AT, psT=psT: e.activation(out=rr(f8(AT)), in_=v3(psT, 64, 8, 64), func=AF.Identity),
                 [psT], [AT])
            TT = B64r.get()
            S.op("dve", lambda e, TT=TT, AT=AT: e.tensor_tensor(out=rr(f8(TT)), in0=id8, in1=f8(AT),
                                                                op=ALU.subtract), [gc, AT], [TT])
            P, PT_ = A, AT
            for lev in range(1, 6):
                psP = psr.get()
                for k in range(8):
                    S.op("pe", lambda e, k=k, psP=psP, P=P, PT_=PT_: e.matmul(
                        psP[:, k * 64:(k + 1) * 64], lhsT=l2(PT_, k), rhs=rr(P[:, k, :]), start=True, stop=True),
                        [P, PT_], [psP])
                if lev < 5:
                    psPT = psr.get()
                    for k in range(8):
                        S.op("pe", lambda e, k=k, psPT=psPT, P=P, PT_=PT_: e.matmul(
                            psPT[:, k * 64:(k + 1) * 64], lhsT=l2(P, k), rhs=rr(PT_[:, k, :]), start=True, stop=True),
                            [P, PT_], [psPT])
                Pn = B64r.get()
                S.op("act", lambda e, Pn=Pn, psP=psP: e.activation(out=rr(f8(Pn)), in_=v3(psP, 64, 8, 64), func=AF.Identity),
                     [psP], [Pn])
                if lev < 5:
                    PTn = B64r.get()
                    S.op("dve", lambda e, PTn=PTn, psPT=psPT: e.tensor_copy(out=rr(f8(PTn)), in_=v3(psPT, 64, 8, 64)),
                         [psPT], [PTn])
                else:
                    PTn = None
                psZ = psr.get()
                for k in range(8):
                    S.op("pe", lambda e, k=k, psZ=psZ, Pn=Pn, TT=TT: e.matmul(
                        psZ[:, k * 64:(k + 1) * 64], lhsT=l2(Pn, k), rhs=rr(TT[:, k, :]), start=True, stop=True),
                        [Pn, TT], [psZ])
                TTn = B64r.get()
                S.op("dve", lambda e, TTn=TTn, TT=TT, psZ=psZ: e.tensor_tensor(out=rr(f8(TTn)), in0=f8(TT),
                                                                               in1=v3(psZ, 64, 8, 64), op=ALU.add),
                     [TT, psZ], [TTn])
                TT = TTn
                P, PT_ = Pn, PTn
            if dbg == "g2inv":
                continue
            vb = B128r.get()
            S.op("pool", lambda e, vb=vb, vt=vt, beta8=beta8: e.tensor_tensor(out=rr(vb[0:64]), in0=vt[:],
                                                                              in1=bc(beta8, 64, 8, 128), op=ALU.mult),
                 [vt, b8], [vb])
            kbg = B128r.get()
            S.op("dve", lambda e, kbg=kbg, kt=kt, beG=beG: e.tensor_tensor(out=rr(kbg[0:64]), in0=kt[:],
                                                                           in1=bc(beG[0:64, :], 64, 8, 128), op=ALU.mult),
                 [kt, beG], [kbg])
            ktl = B128b.get()
            S.op("pool", lambda e, ktl=ktl, kt=kt, etl=etl: e.tensor_tensor(out=ktl[0:64], in0=kt[:],
                                                                            in1=bc(etl[0:64, :], 64, 8, 128),
                                                                            op=ALU.mult), [kt, etl], [ktl])
            qTb = W64b.get()
            S.op("pool", lambda e, qTb=qTb, qT=qT: e.tensor_copy(out=qTb[:, 0:8, :], in_=qT[:, 0:8, :]), [qT], [qTb])
            u = B128.get()
            for hf in range(2):
                psU = psr.get()
                for k4 in range(4):
                    k = hf * 4 + k4
                    S.op("pe", lambda e, k=k, k4=k4, psU=psU, TT=TT, vb=vb: e.matmul(
                        psU[:, k4 * 128:(k4 + 1) * 128], lhsT=l2(TT, k), rhs=rr(vb[:, k, :]), start=True, stop=True),
                        [TT, vb], [psU])
                S.op("act", lambda e, hf=hf, psU=psU, u=u: e.activation(out=u[0:64, hf * 4:(hf + 1) * 4, :],
                                                                        in_=v3(psU, 64, 4, 128), func=AF.Identity),
                     [psU], [u])
            psW = psr.get()
            for k in range(8):
                S.op("pe", lambda e, k=k, psW=psW, kbg=kbg, TT=TT: e.matmul(psW[:, k * 64:(k + 1) * 64], lhsT=rr(kbg[:, k, :]),
                                                                            rhs=rr(TT[:, k, :]), start=True, stop=True),
                     [kbg, TT], [psW])
            wTb = W64b.get()
            S.op("act", lambda e, wTb=wTb, psW=psW: e.activation(out=wTb[:, 0:8, :], in_=v3(psW, 128, 8, 64),
                                                                 func=AF.Identity), [psW], [wTb])
            vn = B128b.get()
            for hf in range(2):
                hs = slice(hf * 4, (hf + 1) * 4)
                psWS = psr.get()
                for k4 in range(4):
                    k = hf * 4 + k4
                    S.op("pe", lambda e, k=k, k4=k4, psWS=psWS, wTb=wTb: e.matmul(
                        psWS[:, k4 * 128:(k4 + 1) * 128], lhsT=l2f(wTb, k), rhs=Sb[:, k, :], start=True, stop=True),
                        [wTb, Sb], [psWS])
                S.op("dve", lambda e, hs=hs, psWS=psWS, vn=vn, u=u: e.tensor_tensor(
                    out=vn[0:64, hs, :], in0=u[0:64, hs, :], in1=v3(psWS, 64, 4, 128), op=ALU.subtract), [u, psWS], [vn])
                psQS = psr.get()
                for k4 in range(4):
                    k = hf * 4 + k4
                    S.op("pe", lambda e, k=k, k4=k4, psQS=psQS, qTb=qTb: e.matmul(
                        psQS[:, k4 * 128:(k4 + 1) * 128], lhsT=l2f(qTb, k), rhs=Sb[:, k, :], start=True, stop=True),
                        [qTb, Sb], [psQS])
                psIV = psr.get()
                for k4 in range(4):
                    k = hf * 4 + k4
                    S.op("pe", lambda e, k=k, k4=k4, psIV=psIV, inT=inT, vn=vn: e.matmul(
                        psIV[:, k4 * 128:(k4 + 1) * 128], lhsT=l2f(inT, k), rhs=vn[:, k, :], start=True, stop=True),
                        [inT, vn], [psIV])
                o = T512.get()
                S.op("dve", lambda e, o=o, psQS=psQS, eGq=eGq, hs=hs: e.tensor_tensor(
                    out=o[0:64], in0=v3(psQS, 64, 4, 128), in1=bc(eGq[0:64, hs], 64, 4, 128), op=ALU.mult),
                    [psQS, eGq], [o])
                S.op("dve", lambda e, o=o, psIV=psIV: e.tensor_tensor(out=o[0:64], in0=o[0:64], in1=v3(psIV, 64, 4, 128),
                                                                      op=ALU.add), [o, psIV], [o])
                tt = tf if hf == 0 else tb
                S.dma("pool", of_d[hf, tt:tt + 64, :].rearrange("p (h d) -> p h d", h=4), o[0:64], reads=[o])
                psKV = psr.get()
                for k4 in range(4):
                    k = hf * 4 + k4
                    S.op("pe", lambda e, k=k, k4=k4, psKV=psKV, ktl=ktl, vn=vn: e.matmul(
                        psKV[:, k4 * 128:(k4 + 1) * 128], lhsT=ktl[:, k, :], rhs=vn[:, k, :], start=True, stop=True),
                        [ktl, vn], [psKV])
                sd = T512.get()
                S.op("pool", lambda e, sd=sd, hs=hs, gtot=gtot: e.tensor_tensor(
                    out=sd[:], in0=St[:, hs, :], in1=bc(gtot[:, hs], 128, 4, 128), op=ALU.mult), [St, gtot], [sd])
                S.op("dve", lambda e, sd=sd, hs=hs, psKV=psKV: e.tensor_tensor(
                    out=St[:, hs, :], in0=sd[:], in1=v3(psKV, 128, 4, 128), op=ALU.add), [sd, psKV], [St])
                S.op("act", lambda e, hs=hs: e.activation(out=Sb[:, hs, :], in_=St[:, hs, :], func=AF.Identity),
                     [St], [Sb])
        S.barrier()
    if dbg is not None and dbg.startswith("g2"):
        return
    with contextlib.ExitStack() as ES:
        cnt = [0]

        def esb(name, shape, dt=F32):
            cnt[0] += 1
            return Tl(ES.enter_context(nc.sbuf_tensor(f"o{name}_{l}_{b}_{cnt[0]}", list(shape), dt)))
        of_p = Rot([esb("of", (128, 512)) for _ in range(2)])
        ob_p = Rot([esb("ob", (128, 512)) for _ in range(2)])
        z_p = Rot([esb("z", (128, 512)) for _ in range(2)])
        sq_p = Rot([esb("sq", (128, 512)) for _ in range(2)])
        st_p = Rot([esb("st", (128, 4)) for _ in range(4)])
        yT_p = Rot([esb("yT", (128, 4, 128), BF16) for _ in range(2)])
        psr = Rot(PSB)
        for t0 in range(0, T, 128):
            of = of_p.get()
            ob = ob_p.get()
            z = z_p.get()
            S.dma("sp", of[:], of_d[0, t0:t0 + 128, :], writes=[of])
            S.dma("sp", ob[:], of_d[1, t0:t0 + 128, :], writes=[ob])
            S.dma("sp", z[:], zs_d[t0:t0 + 128, :], writes=[z])
            S.op("dve", lambda e, of=of, ob=ob: e.tensor_tensor(out=of[:], in0=of[:], in1=ob[:], op=ALU.add), [of, ob], [of])
            sq = sq_p.get()
            S.op("pool", lambda e, of=of, sq=sq: e.tensor_tensor(out=sq[:], in0=of[:], in1=of[:], op=ALU.mult), [of], [sq])
            ssq = st_p.get()
            S.op("dve", lambda e, sq=sq, ssq=ssq: e.tensor_reduce(out=ssq[:], in_=sq[:].rearrange("p (h d) -> p h d", h=4),
                                                                  axis=AX.X, op=ALU.add), [sq], [ssq])
            rs = st_p.get()
            S.op("act", lambda e, ssq=ssq, rs=rs: e.activation(out=rs[:], in_=ssq[:], func=AF.Sqrt, bias=eps_t[:, :],
                                                               scale=1.0 / 128), [ssq, eps_t], [rs])
            S.op("dve", lambda e, rs=rs: e.reciprocal(out=rs[:], in_=rs[:]), [rs], [rs])
            S.op("dve", lambda e, of=of, rs=rs: e.tensor_tensor(
                out=of[:].rearrange("p (h d) -> p h d", h=4), in0=of[:].rearrange("p (h d) -> p h d", h=4),
                in1=rs[:].unsqueeze(2).to_broadcast([128, 4, 128]), op=ALU.mult), [of, rs], [of])
            S.op("pool", lambda e, of=of: e.tensor_tensor(out=of[:], in0=of[:], in1=gdng[:], op=ALU.mult), [of, gdng], [of])
            S.op("dve", lambda e, of=of, z=z: e.tensor_tensor(out=of[:], in0=of[:], in1=z[:], op=ALU.mult), [of, z], [of])
            pt = psr.get()
            for h in range(4):
                S.op("pe", lambda e, h=h, pt=pt, of=of: e.transpose(out=pt[:, h * 128:(h + 1) * 128],
                                                                    in_=of[:, h * 128:(h + 1) * 128], identity=ident[:]),
                     [of, ident], [pt])
            yT = yT_p.get()
            S.op("act", lambda e, pt=pt, yT=yT: e.activation(out=yT[:].rearrange("p h t -> p (h t)"), in_=pt[:],
                                                             func=AF.Identity), [pt], [yT])
            S.dma("pool", yT_d[2].rearrange("(c p) t -> p c t", p=128)[:, :, t0:t0 + 128], yT[:], reads=[yT])
        S.barrier()


def phase_merge_ffn(nc, S, PSB, ident, eps_t, l, b, NB, CTX, T, last, stream, tiles, seg_bounds, norm_mod_T, Acol2, Bcol2,
                    cwf, modrow_d, wb_in, wb_br, wb_o, wb_up, wb_dn, hT_d, h2T_d, yT_d):
    with contextlib.ExitStack() as ES:
        cnt = [0]

        def esb(name, shape, dt=F32):
            cnt[0] += 1
            return Tl(ES.enter_context(nc.sbuf_tensor(f"m{name}_{l}_{b}_{cnt[0]}", list(shape), dt)))
        wg = esb("wg", (128, 8, 3072), BF16)
        wbr = esb("wbr", (128, 3, 4, D), BF16)
        wo = esb("wo", (128, 8, D), BF16)
        for kc in range(8):
            S.dma("sp", wg[:, kc, :], wb_in[kc * 128:(kc + 1) * 128, O_GATE:O_GATE + 3072], writes=[wg])
            S.dma("sp", wo[:, kc, :], wb_o[kc * 128:(kc + 1) * 128, :], writes=[wo])
        for br in range(3):
            S.dma("sp", wbr[:, br, :, :], wb_br[br].rearrange("(c p) n -> p c n", p=128), writes=[wbr])
        gtb = esb("gtb", (128, 2, D))
        S.dma("sp", gtb[:, 0, :], modrow_d[0, b].partition_broadcast(128), writes=[gtb])
        S.dma("sp", gtb[:, 1, :], modrow_d[0, NB].partition_broadcast(128), writes=[gtb])
        hT_p = Rot([esb("hT", (128, 8, 512), BF16) for _ in range(1)])
        yb_p = Rot([esb("yb", (128, 3, 4, 512), BF16) for _ in range(1)])
        yT_p = Rot([esb("yT", (128, 8, 512), BF16) for _ in range(1)])
        sg_p = Rot([esb("sg", (128, 512)) for _ in range(4)])
        ac_p = Rot([esb("ac", (128, 512)) for _ in range(3)])
        xt_p = Rot([esb("xt", (128, 4, D)) for _ in range(1)])
        xn_p = Rot([esb("xn", (128, 4, D)) for _ in range(1)])
        junk = Rot([esb("junk", (128, D)) for _ in range(1)])
        stat = Rot([esb("stat", (128, 8)) for _ in range(6)])
        h2_p = Rot([esb("h2", (128, 8, 512), BF16) for _ in range(1)])
        psr = Rot(PSB)
        for (t0, n) in tiles(512, lat_only=last):
            nj = n // 128
            ri = NB if t0 < CTX else b
            gi = 1 if t0 < CTX else 0
            hT = hT_p.get()
            S.dma("sp", hT[:, :, 0:n], hT_d.rearrange("(kc p) t -> p kc t", p=128)[:, :, t0:t0 + n], writes=[hT])
            yb = yb_p.get()
            for br in range(3):
                S.dma("sp", yb[:, br, :, 0:n], yT_d[br].rearrange("(c p) t -> p c t", p=128)[:, :, t0:t0 + n], writes=[yb])
            xt = xt_p.get()
            S.dma("sp", xt[:, 0:nj, :], stream(b, t0, n).rearrange("(j p) d -> p j d", p=128), writes=[xt])
            yT = yT_p.get()
            for fc in range(8):
                acc_t = ac_p.get()
                for br in range(3):
                    pg = psr.get()
                    for kc in range(8):
                        S.op("pe", lambda e, kc=kc, pg=pg, br=br, fc=fc: e.matmul(
                            pg[:, 0:n], lhsT=wg[:, kc, br * D + fc * 128:br * D + (fc + 1) * 128], rhs=hT[:, kc, 0:n],
                            start=(kc == 0), stop=(kc == 7)), [wg, hT], [pg])
                    sg = sg_p.get()
                    S.op("act", lambda e, pg=pg, sg=sg: e.activation(out=sg[:, 0:n], in_=pg[:, 0:n], func=AF.Sigmoid),
                         [pg], [sg])
                    pb = psr.get()
                    for kc in range(4):
                        S.op("pe", lambda e, kc=kc, pb=pb, br=br, fc=fc: e.matmul(
                            pb[:, 0:n], lhsT=wbr[:, br, kc, fc * 128:(fc + 1) * 128], rhs=yb[:, br, kc, 0:n],
                            start=(kc == 0), stop=(kc == 3)), [wbr, yb], [pb])
                    if br == 0:
                        S.op("dve", lambda e, sg=sg, pb=pb, acc_t=acc_t: e.tensor_tensor(
                            out=acc_t[:, 0:n], in0=sg[:, 0:n], in1=pb[:, 0:n], op=ALU.mult), [sg, pb], [acc_t])
                    else:
                        S.op("dve", lambda e, sg=sg, pb=pb: e.tensor_tensor(out=sg[:, 0:n], in0=sg[:, 0:n], in1=pb[:, 0:n],
                                                                            op=ALU.mult), [sg, pb], [sg])
                        if br == 1:
                            S.op("pool", lambda e, sg=sg, acc_t=acc_t: e.tensor_tensor(
                                out=acc_t[:, 0:n], in0=acc_t[:, 0:n], in1=sg[:, 0:n], op=ALU.add), [sg, acc_t], [acc_t])
                        else:
                            S.op("pool", lambda e, sg=sg, acc_t=acc_t, fc=fc: e.tensor_tensor(
                                out=yT[:, fc, 0:n], in0=acc_t[:, 0:n], in1=sg[:, 0:n], op=ALU.add), [sg, acc_t], [yT])
            for j in range(nj):
                for hf in range(2):
                    po = psr.get()
                    for kc in range(8):
                        S.op("pe", lambda e, kc=kc, po=po, j=j, hf=hf: e.matmul(
                            po[:], lhsT=yT[:, kc, j * 128:(j + 1) * 128], rhs=wo[:, kc, hf * 512:(hf + 1) * 512],
                            start=(kc == 0), stop=(kc == 7)), [yT, wo], [po])
                    tm = sg_p.get()
                    S.op("dve", lambda e, po=po, tm=tm, hf=hf: e.tensor_tensor(
                        out=tm[:], in0=po[:], in1=gtb[:, gi, hf * 512:(hf + 1) * 512], op=ALU.mult), [po, gtb], [tm])
                    S.op("pool", lambda e, tm=tm, j=j, hf=hf: e.tensor_tensor(
                        out=xt[:, j, hf * 512:(hf + 1) * 512], in0=xt[:, j, hf * 512:(hf + 1) * 512], in1=tm[:],
                        op=ALU.add), [tm, xt], [xt])
            S.dma("pool", stream(b, t0, n).rearrange("(j p) d -> p j d", p=128), xt[:, 0:nj, :], reads=[xt])
            h2 = h2_p.get()
            norm_mod_T(xt, nj, n, Acol2, Bcol2, ri, h2, (junk, stat, xn_p, psr))
            S.dma("pool", h2T_d.rearrange("(kc p) t -> p kc t", p=128)[:, :, t0:t0 + n], h2[:, :, 0:n], reads=[h2])
        S.barrier()
    NT = 256
    with contextlib.ExitStack() as ES:
        cnt = [0]

        def esb(name, shape, dt=F32):
            cnt[0] += 1
            return Tl(ES.enter_context(nc.sbuf_tensor(f"f{name}_{l}_{b}_{cnt[0]}", list(shape), dt)))
        wu = esb("wu", (128, 8, 2 * D_FF), BF16)
        wd = esb("wd", (128, 22, D), BF16)
        for kc in range(8):
            S.dma("sp", wu[:, kc, :], wb_up[kc * 128:(kc + 1) * 128, :], writes=[wu])
        for c0 in range(0, 22, 2):
            S.dma("sp", wd[:, c0:c0 + 2, :], wb_dn[c0 * 128:(c0 + 2) * 128, :].rearrange("(c p) n -> p c n", p=128),
                  writes=[wd])
        gtb = esb("gtb", (128, 2, D))
        S.dma("sp", gtb[:, 0, :], modrow_d[1, b].partition_broadcast(128), writes=[gtb])
        S.dma("sp", gtb[:, 1, :], modrow_d[1, NB].partition_broadcast(128), writes=[gtb])
        h2_p = Rot([esb("h2", (128, 8, NT + 2), BF16) for _ in range(2)])
        aT_p = Rot([esb("aT", (128, 22, NT), BF16) for _ in range(1)])
        cg_p = Rot([esb("cg", (128, NT)) for _ in range(4)])
        xt_p = Rot([esb("xt", (128, 2, D)) for _ in range(1)])
        tm_p = Rot([esb("tm", (128, 512)) for _ in range(3)])
        psr = Rot(PSB)
        for (t0, n) in tiles(NT, lat_only=last):
            nj = n // 128
            gi = 1 if t0 < CTX else 0
            s0, s1 = seg_bounds(t0)
            lo = max(t0 - 1, s0)
            hi = min(t0 + n + 1, s1)
            h2 = h2_p.get()
            if lo != t0 - 1 or hi != t0 + n + 1:
                S.op("pool", lambda e, h2=h2: e.memset(h2[:], 0.0), [], [h2])
            S.dma("sp", h2[:, :, lo - (t0 - 1):hi - (t0 - 1)], h2T_d.rearrange("(kc p) t -> p kc t", p=128)[:, :, lo:hi],
                  writes=[h2])
            xt = xt_p.get()
            S.dma("sp", xt[:, 0:nj, :], stream(b, t0, n).rearrange("(j p) d -> p j d", p=128), writes=[xt])
            aT = aT_p.get()
            for cc in range(22):
                cgv = []
                for gv in range(2):
                    col = gv * D_FF + cc * 128
                    wi = gv * 22 + cc
                    pu = psr.get()
                    for kc in range(8):
                        S.op("pe", lambda e, kc=kc, pu=pu, col=col: e.matmul(
                            pu[:, 0:n + 2], lhsT=wu[:, kc, col:col + 128], rhs=h2[:, kc, 0:n + 2], start=(kc == 0),
                            stop=(kc == 7)), [wu, h2], [pu])
                    cg = cg_p.get()
                    S.op("act", lambda e, pu=pu, cg=cg, wi=wi: e.activation(out=cg[:, 0:n], in_=pu[:, 0:n],
                                                                            func=AF.Identity, scale=cwf[:, wi, 0:1]),
                         [pu, cwf], [cg])
                    S.op("dve", lambda e, pu=pu, cg=cg, wi=wi: e.scalar_tensor_tensor(
                        out=cg[:, 0:n], in0=pu[:, 1:n + 1], scalar=cwf[:, wi, 1:2], in1=cg[:, 0:n], op0=ALU.mult,
                        op1=ALU.add), [pu, cwf, cg], [cg])
                    S.op("dve", lambda e, pu=pu, cg=cg, wi=wi: e.scalar_tensor_tensor(
                        out=cg[:, 0:n], in0=pu[:, 2:n + 2], scalar=cwf[:, wi, 2:3], in1=cg[:, 0:n], op0=ALU.mult,
                        op1=ALU.add), [pu, cwf, cg], [cg])
                    cgv.append(cg)
                S.op("act", lambda e, cg=cgv[0]: e.activation(out=cg[:, 0:n], in_=cg[:, 0:n], func=AF.Silu),
                     [cgv[0]], [cgv[0]])
                S.op("pool", lambda e, cc=cc, a=cgv[0], v=cgv[1]: e.tensor_tensor(out=aT[:, cc, 0:n], in0=a[:, 0:n],
                                                                                  in1=v[:, 0:n], op=ALU.mult),
                     [cgv[0], cgv[1]], [aT])
            for j in range(nj):
                for hf in range(2):
                    po = psr.get()
                    for cc in range(22):
                        S.op("pe", lambda e, cc=cc, po=po, j=j, hf=hf: e.matmul(
                            po[:], lhsT=aT[:, cc, j * 128:(j + 1) * 128], rhs=wd[:, cc, hf * 512:(hf + 1) * 512],
                            start=(cc == 0), stop=(cc == 21)), [aT, wd], [po])
                    tm = tm_p.get()
                    S.op("dve", lambda e, po=po, tm=tm, hf=hf: e.tensor_tensor(
                        out=tm[:], in0=po[:], in1=gtb[:, gi, hf * 512:(hf + 1) * 512], op=ALU.mult), [po, gtb], [tm])
                    S.op("dve", lambda e, tm=tm, j=j, hf=hf: e.tensor_tensor(
                        out=xt[:, j, hf * 512:(hf + 1) * 512], in0=xt[:, j, hf * 512:(hf + 1) * 512], in1=tm[:],
                        op=ALU.add), [tm, xt], [xt])
            S.dma("pool", stream(b, t0, n).rearrange("(j p) d -> p j d", p=128), xt[:, 0:nj, :], reads=[xt])
        S.barrier()


_CACHE = {}


def run(inputs, n_cores, NB, SEQ, CTX, DEPTH):
    key = (NB, SEQ, CTX, DEPTH)
    if key not in _CACHE:
        _CACHE[key] = build(NB, SEQ, CTX, DEPTH)[0]
    nc = _CACHE[key]
    consts = host_consts(SEQ, CTX)
    shared = {k: np.ascontiguousarray(np.asarray(v, dtype=np.float32)) for k, v in inputs.items()
              if k not in ("x", "c", "ctx")}
    in_maps = []
    for i in range(n_cores):
        m = dict(shared)
        m.update(consts)
        for k in ("x", "c", "ctx"):
            m[k] = np.ascontiguousarray(np.asarray(inputs[k][i * NB:(i + 1) * NB], dtype=np.float32))
        in_maps.append(m)
    res = run_bass_kernel_spmd(nc, in_maps, core_ids=list(range(n_cores)))
    return np.concatenate([np.asarray(r["out"]) for r in res.results], axis=0).astype(np.float32)


def kernel(**inputs):
    return run(inputs, 8, 2, 4096, 256, 2)
```

```python
import math
import contextlib
import numpy as np
import concourse.bass as bass
import concourse.mybir as mybir
from concourse.bass_utils import run_bass_kernel_spmd

F32 = mybir.dt.float32
BF16 = mybir.dt.bfloat16
AF = mybir.ActivationFunctionType
ALU = mybir.AluOpType
AX = mybir.AxisListType

D = 1024
GRID_W = 64
A_W = 512
B_H = 4
C_H = 4
D_FF = 2816
EPS = 1e-6
D_IN = 7696
O_U, O_SV, O_BQ, O_BK, O_BV, O_DQ, O_DK, O_DV, O_DZ, O_BETA, O_GATE = (
    0, 512, 1024, 1536, 2048, 2560, 3072, 3584, 4096, 4608, 4624)
EPOCH = 30000
GELU_C = 2.0 * math.sqrt(2.0 / math.pi)


class Tl:
    __slots__ = ("t", "lw", "rd")

    def __init__(self, t):
        self.t = t
        self.lw = None
        self.rd = {}

    def __getitem__(self, idx):
        return self.t[idx]


class Sched:
    def __init__(self, nc, n_dma_slots=8):
        self.nc = nc
        self.eng = {"pe": nc.tensor, "act": nc.scalar, "dve": nc.vector, "pool": nc.gpsimd, "sp": nc.sync}
        self.cnt = {e: 0 for e in self.eng}
        self.sems = {e: [] for e in self.eng}
        self.seen = {e: {} for e in self.eng}
        self.dq = {}
        self.n_dma_slots = n_dma_slots
        self.ninst = 0

    def _esem(self, e, idx):
        ep = (idx - 1) // EPOCH
        while len(self.sems[e]) <= ep:
            self.sems[e].append(self.nc.alloc_semaphore(f"s_{e}_{len(self.sems[e])}"))
        return self.sems[e][ep], (idx - 1) % EPOCH + 1

    def _wait(self, e, tok):
        if tok is None:
            return
        if tok[0] == "eng":
            _, f, idx = tok
            if f == e and e == "pe":
                return
            sem, val = self._esem(f, idx)
            key = (f, (idx - 1) // EPOCH)
        else:
            _, sem, val, key = tok
        if self.seen[e].get(key, 0) >= val:
            return
        self.seen[e][key] = val
        self.eng[e].wait_ge(sem, val)
        self.ninst += 1

    def _deps(self, e, reads, writes):
        for t in reads:
            self._wait(e, t.lw)
        for t in writes:
            self._wait(e, t.lw)
            for tok in list(t.rd.values()):
                self._wait(e, tok)

    def _mark(self, tok, rkey, reads, writes):
        for t in reads:
            t.rd[rkey] = tok
        for t in writes:
            t.lw = tok
            t.rd = {}

    def op(self, e, fn, reads=(), writes=()):
        if e == "pool":
            e = "dve"
        self._deps(e, reads, writes)
        inst = fn(self.eng[e])
        self.cnt[e] += 1
        idx = self.cnt[e]
        sem, _ = self._esem(e, idx)
        inst.then_inc(sem, 1)
        self.ninst += 1
        self._mark(("eng", e, idx), e, reads, writes)

    def dma(self, q, out, in_, reads=(), writes=(), **kw):
        if q not in self.dq:
            self.dq[q] = {"slots": [[self.nc.alloc_semaphore(f"d_{q}_{i}"), 0]
                                    for i in range(self.n_dma_slots)], "i": 0}
        d = self.dq[q]
        si = d["i"] % self.n_dma_slots
        d["i"] += 1
        slot = d["slots"][si]
        self._deps(q, reads, writes)
        key = ("dma", q, si)
        if slot[1] > 0:
            self._wait(q, ("dma", slot[0], slot[1], key))
        inst = self.eng[q].dma_start(out=out, in_=in_, **kw)
        slot[1] += 16
        inst.then_inc(slot[0], 16)
        self.ninst += 1
        self._mark(("dma", slot[0], slot[1], key), key, reads, writes)

    def barrier(self):
        toks = []
        for f in self.eng:
            if self.cnt[f] > 0:
                toks.append(("eng", f, self.cnt[f]))
        for q, d in self.dq.items():
            for si, slot in enumerate(d["slots"]):
                if slot[1] > 0:
                    toks.append(("dma", slot[0], slot[1], ("dma", q, si)))
        for e in self.eng:
            for tok in toks:
                if tok[0] == "eng" and tok[1] == e:
                    continue
                self._wait(e, tok)


class Rot:
    def __init__(self, tiles):
        self.tiles = tiles
        self.i = 0

    def get(self):
        t = self.tiles[self.i % len(self.tiles)]
        self.i += 1
        return t


def host_consts(SEQ, CTX):
    T = CTX + SEQ
    c = {}
    c["c_ident"] = np.eye(128, dtype=np.float32)
    p = np.arange(128)
    c["c_blk64"] = (p[:, None] // 64 == p[None, :] // 64).astype(np.float32)
    R = np.zeros((64, 64), np.float32)
    for base in (0, 32):
        for i in range(16):
            R[base + i, base + 16 + i] = -1.0
            R[base + 16 + i, base + i] = 1.0
    R2 = np.zeros((128, 128), np.float32)
    R2[:64, :64] = R
    R2[64:, 64:] = R
    c["c_rotT"] = np.ascontiguousarray(R2.T)
    n_freq = 16
    inv_freq = (np.float32(10000.0) ** (-np.arange(n_freq, dtype=np.float32) / np.float32(n_freq))).astype(np.float32)
    rows = SEQ // GRID_W
    row = np.repeat(np.arange(rows, dtype=np.float32), GRID_W)
    col = np.tile(np.arange(GRID_W, dtype=np.float32), rows)
    ang_r = (row[:, None] * inv_freq).astype(np.float32)
    ang_c = (col[:, None] * inv_freq).astype(np.float32)
    cos64 = np.concatenate([np.cos(ang_r), np.cos(ang_r), np.cos(ang_c), np.cos(ang_c)], axis=1).astype(np.float32)
    sin64 = np.concatenate([np.sin(ang_r), np.sin(ang_r), np.sin(ang_c), np.sin(ang_c)], axis=1).astype(np.float32)
    cos = np.ones((128, T), np.float32)
    sin = np.zeros((128, T), np.float32)
    cos[:, CTX:] = np.concatenate([cos64, cos64], axis=1).T
    sin[:, CTX:] = np.concatenate([sin64, sin64], axis=1).T
    c["c_cos"] = cos
    c["c_sin"] = sin
    i = np.arange(64)
    lo = (i[:, None] >= i[None, :]).astype(np.float32)
    up = (i[:, None] <= i[None, :]).astype(np.float32)
    slo = (i[:, None] > i[None, :]).astype(np.float32)
    sup = (i[:, None] < i[None, :]).astype(np.float32)

    def blk8(f, b):
        return np.ascontiguousarray(np.stack([f] * 4 + [b] * 4, axis=1))
    g = np.zeros((64, 6, 8, 64), np.float32)
    g[:, 0] = blk8(up, lo)
    g[:, 1] = blk8(lo, up)
    g[:, 2] = blk8(slo, sup)
    g[:, 3] = blk8(up, lo)
    g[:, 4] = blk8(np.eye(64, dtype=np.float32), np.eye(64, dtype=np.float32))
    g[:, 5] = 1.0
    c["c_gdn"] = g
    return c


def build(NB, SEQ, CTX, DEPTH, dbg=None):
    T = CTX + SEQ
    nc = bass.Bass("TRN2", target_bir_lowering=False)
    S = Sched(nc)

    def din(name, shape):
        return nc.dram_tensor(name, list(shape), F32, kind="ExternalInput").ap()

    def dscr(name, shape, dt):
        return nc.dram_tensor(name, list(shape), dt, kind="Internal").ap()

    L = DEPTH
    x_in = din("x", (NB, SEQ, D))
    c_in = din("c", (NB, D))
    ctx_in = din("ctx", (NB, CTX, D))
    cctx_in = din("c_ctx", (D,))
    W = {}
    for name, shape in [("w_mod", (L, D, 6 * D)), ("b_mod", (L, 6 * D)), ("norm1_g", (L, D)), ("w_in", (L, D, D_IN)),
                        ("sgu_norm_g", (L, 4, 128)), ("sgu_w", (L, 4, 128, 128)), ("sgu_b", (L, 4, 128)),
                        ("w_a_br", (L, 512, D)), ("q_norm_g", (L, 64)), ("k_norm_g", (L, 64)),
                        ("lambda_q1", (L, 64)), ("lambda_k1", (L, 64)), ("lambda_q2", (L, 64)),
                        ("lambda_k2", (L, 64)), ("subln_g", (L, 128)), ("w_b_br", (L, 512, D)),
                        ("conv_qkv_w", (L, 3, 1536)), ("a_log", (L, 2, 4)), ("dt_bias", (L, 2, 4)),
                        ("gdn_norm_g", (L, 128)), ("w_c_br", (L, 512, D)), ("w_o", (L, D, D)),
                        ("norm2_g", (L, D)), ("w_up", (L, D, 2 * D_FF)), ("conv_ffn_w", (L, 3, 2 * D_FF)),
                        ("w_down", (L, D_FF, D))]:
        W[name] = din(name, shape)
    c_ident = din("c_ident", (128, 128))
    c_blk64 = din("c_blk64", (128, 128))
    c_rotT = din("c_rotT", (128, 128))
    c_cos = din("c_cos", (128, T))
    c_sin = din("c_sin", (128, T))
    c_gdn = din("c_gdn", (64, 6, 8, 64))
    out = nc.dram_tensor("out", [NB, SEQ, D], F32, kind="ExternalOutput").ap()

    cx = dscr("cx", (NB, CTX, D), F32)
    wb_in = dscr("wb_in", (D, D_IN), BF16)
    wb_br = dscr("wb_br", (3, 512, D), BF16)
    wb_o = dscr("wb_o", (D, D), BF16)
    wb_up = dscr("wb_up", (D, 2 * D_FF), BF16)
    wb_dn = dscr("wb_dn", (D_FF, D), BF16)
    modrow_d = dscr("modrow", (2, 4, D), F32)
    hT_d = dscr("hT", (D, T), BF16)
    h2T_d = dscr("h2T", (D, T), BF16)
    yT_d = dscr("yT", (3, 512, T), BF16)
    QT_d = dscr("QT", (512, T), BF16)
    KT_d = dscr("KT", (512, T), BF16)
    V_d = dscr("V", (T, 512), BF16)
    gpre_d = dscr("gpre", (1536, T), F32)
    gqT_d = dscr("gqT", (512, T), F32)
    gkT_d = dscr("gkT", (512, T), F32)
    gkt_d = dscr("gkt", (T, 512), F32)
    gvt_d = dscr("gvt", (T, 512), F32)
    zs_d = dscr("zs", (T, 512), F32)
    bg_d = dscr("bg", (T, 16), F32)
    of_d = dscr("of", (2, T, 512), F32)

    R4 = 4
    assert NB + 1 <= R4

    def sb(name, shape, dt=F32):
        return Tl(nc.alloc_sbuf_tensor(name, list(shape), dt))

    PSB = [Tl(nc.alloc_psum_tensor(f"ps{i}", [128, 512], F32)) for i in range(8)]
    ident = sb("ident", (128, 128))
    S.dma("sp", ident[:], c_ident, writes=[ident])
    eps_t = sb("eps_t", (128, 1))
    S.op("dve", lambda e: e.memset(eps_t[:], EPS), [], [eps_t])

    def stream(b, t0, n):
        if t0 < CTX:
            return cx[b, t0:t0 + n, :]
        return out[b, t0 - CTX:t0 - CTX + n, :]

    def tiles(nmax, lat_only=False):
        r = []
        if not lat_only:
            for t0 in range(0, CTX, nmax):
                r.append((t0, min(nmax, CTX - t0)))
        for t0 in range(CTX, T, nmax):
            r.append((t0, min(nmax, T - t0)))
        return r

    def seg_bounds(t0):
        return (0, CTX) if t0 < CTX else (CTX, T)

    for b in range(NB):
        for r0 in range(0, SEQ, 512):
            S.dma("sp", out[b, r0:r0 + 512, :], x_in[b, r0:r0 + 512, :])
        S.dma("sp", cx[b], ctx_in[b])

    def col_load(q, dst_tile, dst_ap, vec_ap, n):
        for c0 in range(0, n, 8):
            c1 = min(n, c0 + 8)
            S.dma(q, dst_ap[:, c0:c1], vec_ap[c0 * 128:c1 * 128].rearrange("(c p) -> p c", p=128),
                  writes=[dst_tile], allow_slow_non_contiguous=True)

    def gelu_ops(es_get, src_ap, src_tl, n, out_ap, out_tl):
        xs = es_get()
        t = es_get()
        S.op("act", lambda e: e.activation(out=xs[:, 0:n], in_=src_ap, func=AF.Identity), [src_tl], [xs])
        S.op("dve", lambda e: e.tensor_tensor(out=t[:, 0:n], in0=xs[:, 0:n], in1=xs[:, 0:n], op=ALU.mult), [xs], [t])
        S.op("dve", lambda e: e.tensor_scalar(out=t[:, 0:n], in0=t[:, 0:n], scalar1=0.044715, scalar2=1.0,
                                              op0=ALU.mult, op1=ALU.add), [t], [t])
        S.op("dve", lambda e: e.tensor_tensor(out=t[:, 0:n], in0=t[:, 0:n], in1=xs[:, 0:n], op=ALU.mult), [t, xs], [t])
        S.op("act", lambda e: e.activation(out=t[:, 0:n], in_=t[:, 0:n], func=AF.Sigmoid, scale=GELU_C), [t], [t])
        S.op("dve", lambda e: e.tensor_tensor(out=out_ap, in0=t[:, 0:n], in1=xs[:, 0:n], op=ALU.mult), [t, xs], [out_tl])

    def rstd_ops(ssq_tl, ssq_ap, out_tl, out_ap, inv_n):
        S.op("act", lambda e: e.activation(out=out_ap, in_=ssq_ap, func=AF.Sqrt, bias=eps_t[0:out_ap.shape[0], :],
                                           scale=inv_n), [ssq_tl, eps_t], [out_tl])
        S.op("dve", lambda e: e.reciprocal(out=out_ap, in_=out_ap), [out_tl], [out_tl])

    def norm_mod_T(xt, nj, n, Acol, Bcol, ri, hT, pools):
        junk, stat, xn_pool, psr = pools
        ssq = stat.get()
        for j in range(nj):
            jk = junk.get()
            S.op("act", lambda e, j=j, jk=jk: e.activation(out=jk[:], in_=xt[:, j, :], func=AF.Square,
                                                           accum_out=ssq[:, j:j + 1]), [xt], [jk, ssq])
        rs = stat.get()
        rstd_ops(ssq, ssq[:, 0:nj], rs, rs[:, 0:nj], 1.0 / D)
        xn = xn_pool.get()
        for j in range(nj):
            S.op("act", lambda e, j=j: e.activation(out=xn[:, j, :], in_=xt[:, j, :], func=AF.Identity,
                                                    scale=rs[:, j:j + 1]), [xt, rs], [xn])
        for kc in range(8):
            ps = psr.get()
            for j in range(nj):
                S.op("pe", lambda e, j=j, kc=kc, ps=ps: e.transpose(out=ps[:, j * 128:(j + 1) * 128],
                                                                     in_=xn[:, j, kc * 128:(kc + 1) * 128],
                                                                     identity=ident[:]), [xn, ident], [ps])
            S.op("act", lambda e, kc=kc, ps=ps: e.activation(out=hT[:, kc, 0:n], in_=ps[:, 0:n], func=AF.Identity,
                                                             scale=Acol[:, kc, ri:ri + 1],
                                                             bias=Bcol[:, kc, ri:ri + 1]), [ps, Acol, Bcol], [hT])

    for l in range(L):
        last = (l == L - 1)
        lam_init = 0.8 - 0.6 * math.exp(-0.3 * l)
        S.barrier()
        for r0 in range(0, D, 128):
            S.dma("pool", wb_in[r0:r0 + 128, :], W["w_in"][l, r0:r0 + 128, :])
            S.dma("pool", wb_up[r0:r0 + 128, :], W["w_up"][l, r0:r0 + 128, :])
            S.dma("pool", wb_o[r0:r0 + 128, :], W["w_o"][l, r0:r0 + 128, :])
        for r0 in range(0, D_FF, 128):
            S.dma("pool", wb_dn[r0:r0 + 128, :], W["w_down"][l, r0:r0 + 128, :])
        for bi, nm in enumerate(("w_a_br", "w_b_br", "w_c_br")):
            for r0 in range(0, 512, 128):
                S.dma("pool", wb_br[bi, r0:r0 + 128, :], W[nm][l, r0:r0 + 128, :])

        with contextlib.ExitStack() as LS:
            def lsb(name, shape, dt=F32):
                return Tl(LS.enter_context(nc.sbuf_tensor(f"{name}_{l}", list(shape), dt)))

            Acol1 = lsb("Acol1", (128, 8, R4))
            Bcol1 = lsb("Bcol1", (128, 8, R4))
            Acol2 = lsb("Acol2", (128, 8, R4))
            Bcol2 = lsb("Bcol2", (128, 8, R4))
            lam_t = lsb("lam", (128, 2))
            wsT = lsb("wsT", (128, 4, 128), BF16)
            bsb = lsb("bsb", (128, 512))
            sgng = lsb("sgng", (128, 512))
            gq_c = lsb("gq_c", (128, 1))
            gk_c = lsb("gk_c", (128, 1))
            subg = lsb("subg", (128, 128))
            gdng = lsb("gdng", (128, 512))
            cwq = lsb("cwq", (128, 12, 3))
            cwf = lsb("cwf", (128, 44, 3))
            alog_b = lsb("alog_b", (128, 8))
            dtb_b = lsb("dtb_b", (128, 8))
            blk64 = lsb("blk64", (128, 128))
            rotT = lsb("rotT", (128, 128), BF16)

            with contextlib.ExitStack() as ES:
                def esb(name, shape, dt=F32):
                    return Tl(ES.enter_context(nc.sbuf_tensor(f"{name}_{l}", list(shape), dt)))
                S.dma("sp", blk64[:], c_blk64, writes=[blk64])
                rt32 = esb("rt32", (128, 128))
                S.dma("sp", rt32[:], c_rotT, writes=[rt32])
                S.op("dve", lambda e: e.tensor_copy(out=rotT[:], in_=rt32[:]), [rt32], [rotT])
                S.dma("sp", bsb[:], W["sgu_b"][l].rearrange("g i -> (g i)").partition_broadcast(128), writes=[bsb])
                S.dma("sp", sgng[:], W["sgu_norm_g"][l].rearrange("g i -> (g i)").partition_broadcast(128),
                      writes=[sgng])
                S.dma("sp", subg[:], W["subln_g"][l].partition_broadcast(128), writes=[subg])
                for h in range(4):
                    S.dma("sp", gdng[:, h * 128:(h + 1) * 128], W["gdn_norm_g"][l].partition_broadcast(128),
                          writes=[gdng])
                S.dma("sp", alog_b[:], W["a_log"][l].rearrange("a b -> (a b)").partition_broadcast(128),
                      writes=[alog_b])
                S.dma("sp", dtb_b[:], W["dt_bias"][l].rearrange("a b -> (a b)").partition_broadcast(128),
                      writes=[dtb_b])
                S.op("act", lambda e: e.activation(out=alog_b[:], in_=alog_b[:], func=AF.Exp), [alog_b], [alog_b])
                S.op("dve", lambda e: e.tensor_scalar(out=alog_b[:], in0=alog_b[:], scalar1=-1.0, scalar2=None,
                                                      op0=ALU.mult), [alog_b], [alog_b])
                for half in range(2):
                    S.dma("sp", gq_c[half * 64:(half + 1) * 64, :], W["q_norm_g"][l].rearrange("(p o) -> p o", o=1),
                          writes=[gq_c])
                    S.dma("sp", gk_c[half * 64:(half + 1) * 64, :], W["k_norm_g"][l].rearrange("(p o) -> p o", o=1),
                          writes=[gk_c])
                for k in range(3):
                    col_load("sp", cwq, cwq[:, :, k], W["conv_qkv_w"][l, k], 12)
                    col_load("sp", cwf, cwf[:, :, k], W["conv_ffn_w"][l, k], 44)
                sw = esb("sw", (128, 4, 128))
                S.dma("sp", sw[:], W["sgu_w"][l].rearrange("g i j -> i g j"), writes=[sw])
                ps = PSB[0]
                for g in range(4):
                    S.op("pe", lambda e, g=g: e.transpose(out=ps[:, g * 128:(g + 1) * 128], in_=sw[:, g, :],
                                                          identity=ident[:]), [sw, ident], [ps])
                S.op("dve", lambda e: e.tensor_copy(out=wsT[:].rearrange("p g i -> p (g i)"), in_=ps[:]), [ps], [wsT])
                lv = esb("lv", (128, 4, 64))
                for i, nm in enumerate(("lambda_q1", "lambda_k1", "lambda_q2", "lambda_k2")):
                    S.dma("sp", lv[:, i, :], W[nm][l].partition_broadcast(128), writes=[lv])
                lp = esb("lp", (128, 2, 64))
                S.op("dve", lambda e: e.tensor_tensor(out=lp[:, 0, :], in0=lv[:, 0, :], in1=lv[:, 1, :], op=ALU.mult),
                     [lv], [lp])
                S.op("dve", lambda e: e.tensor_tensor(out=lp[:, 1, :], in0=lv[:, 2, :], in1=lv[:, 3, :], op=ALU.mult),
                     [lv], [lp])
                ls_ = esb("ls", (128, 2))
                S.op("dve", lambda e: e.tensor_reduce(out=ls_[:], in_=lp[:], axis=AX.X, op=ALU.add), [lp], [ls_])
                S.op("act", lambda e: e.activation(out=ls_[:], in_=ls_[:], func=AF.Exp), [ls_], [ls_])
                S.op("dve", lambda e: e.tensor_tensor(out=lam_t[:, 0:1], in0=ls_[:, 1:2], in1=ls_[:, 0:1],
                                                      op=ALU.subtract), [ls_], [lam_t])
                S.op("dve", lambda e: e.tensor_scalar(out=lam_t[:, 0:1], in0=lam_t[:, 0:1], scalar1=-lam_init,
                                                      scalar2=None, op0=ALU.add), [lam_t], [lam_t])
                scT = esb("scT", (128, 8, R4))
                S.op("dve", lambda e: e.memset(scT[:], 0.0), [], [scT])
                for r in range(NB + 1):
                    src = c_in[r] if r < NB else cctx_in
                    S.dma("sp", scT[:, :, r], src.rearrange("(c p) -> p c", p=128), writes=[scT],
                          allow_slow_non_contiguous=True)
                S.op("act", lambda e: e.activation(out=scT[:], in_=scT[:], func=AF.Silu), [scT], [scT])
                bmc = esb("bmc", (128, 48))
                col_load("sp", bmc, bmc[:], W["b_mod"][l], 48)
                g1c = esb("g1c", (128, 8))
                g2c = esb("g2c", (128, 8))
                col_load("sp", g1c, g1c[:], W["norm1_g"][l], 8)
                col_load("sp", g2c, g2c[:], W["norm2_g"][l], 8)
                bmr = esb("bmr", (R4, 6 * D))
                S.dma("sp", bmr[:], W["b_mod"][l].partition_broadcast(R4), writes=[bmr])
                modT = esb("modT", (128, 6, 8, R4))
                mrow = esb("mrow", (R4, 2, D))
                wmp = Rot([esb(f"wm{i}", (128, 8, 512)) for i in range(2)])
                wmv = W["w_mod"][l].rearrange("(kc p) n -> p kc n", p=128)
                for cb in range(12):
                    seg = cb // 2
                    wm = wmp.get()
                    S.dma("sp", wm[:], wmv[:, :, cb * 512:(cb + 1) * 512], writes=[wm])
                    if seg in (2, 5):
                        ps = PSB[(cb % 2) + 1]
                        for kc in range(8):
                            S.op("pe", lambda e, kc=kc, ps=ps, wm=wm: e.matmul(ps[0:R4, :], lhsT=scT[:, kc, :],
                                                                                rhs=wm[:, kc, :], start=(kc == 0),
                                                                                stop=(kc == 7)), [scT, wm], [ps])
                        gi = 0 if seg == 2 else 1
                        hf = cb % 2
                        S.op("dve", lambda e, ps=ps, gi=gi, hf=hf, cb=cb: e.tensor_tensor(
                            out=mrow[:, gi, hf * 512:(hf + 1) * 512], in0=ps[0:R4, :],
                            in1=bmr[:, cb * 512:(cb + 1) * 512], op=ALU.add), [ps, bmr], [mrow])
                    else:
                        ps = PSB[(cb % 2) + 1]
                        for c4 in range(4):
                            for kc in range(8):
                                S.op("pe", lambda e, kc=kc, c4=c4, ps=ps, wm=wm: e.matmul(
                                    ps[:, c4 * R4:(c4 + 1) * R4], lhsT=wm[:, kc, c4 * 128:(c4 + 1) * 128],
                                    rhs=scT[:, kc, :], start=(kc == 0), stop=(kc == 7)), [scT, wm], [ps])
                        for c4 in range(4):
                            fc = cb * 4 + c4
                            S.op("dve", lambda e, ps=ps, c4=c4, fc=fc, seg=seg: e.tensor_scalar(
                                out=modT[:, seg, fc % 8, :], in0=ps[:, c4 * R4:(c4 + 1) * R4],
                                scalar1=bmc[:, fc:fc + 1], scalar2=None, op0=ALU.add), [ps, bmc], [modT])
                for kc in range(8):
                    S.op("dve", lambda e, kc=kc: e.tensor_scalar(out=Acol1[:, kc, :], in0=modT[:, 1, kc, :], scalar1=1.0,
                                                                 scalar2=g1c[:, kc:kc + 1], op0=ALU.add, op1=ALU.mult),
                         [modT, g1c], [Acol1])
                    S.op("dve", lambda e, kc=kc: e.tensor_scalar(out=Acol2[:, kc, :], in0=modT[:, 4, kc, :], scalar1=1.0,
                                                                 scalar2=g2c[:, kc:kc + 1], op0=ALU.add, op1=ALU.mult),
                         [modT, g2c], [Acol2])
                S.op("dve", lambda e: e.tensor_copy(out=Bcol1[:], in_=modT[:, 0]), [modT], [Bcol1])
                S.op("dve", lambda e: e.tensor_copy(out=Bcol2[:], in_=modT[:, 3]), [modT], [Bcol2])
                S.dma("sp", modrow_d.rearrange("g r d -> r g d"), mrow[:], reads=[mrow])
                S.barrier()
            if dbg == "setup":
                break

            for b in range(NB):
                S.barrier()
                with contextlib.ExitStack() as ES:
                    cnt = [0]

                    def esb(name, shape, dt=F32):
                        cnt[0] += 1
                        return Tl(ES.enter_context(nc.sbuf_tensor(f"{name}_{l}_{b}_{cnt[0]}", list(shape), dt)))
                    xt_p = Rot([esb("xt", (128, 4, D)) for _ in range(1)])
                    xn_p = Rot([esb("xn", (128, 4, D)) for _ in range(1)])
                    junk = Rot([esb("junk", (128, D)) for _ in range(2)])
                    stat = Rot([esb("stat", (128, 8)) for _ in range(8)])
                    hT_p = Rot([esb("hT", (128, 8, 512), BF16) for _ in range(2)])
                    wt_p = Rot([esb("wt", (128, 8, 512), BF16) for _ in range(3)])
                    tmp_p = Rot([esb("tmp", (128, 512)) for _ in range(6)])
                    tb_p = Rot([esb("tb", (128, 512), BF16) for _ in range(4)])
                    uT_p = Rot([esb("uT", (128, 4, 512), BF16) for _ in range(2)])
                    ya_p = Rot([esb("ya", (128, 4, 512), BF16) for _ in range(2)])
                    qk_p = Rot([esb("qk", (128, 4, 512), BF16) for _ in range(3)])
                    cs_p = Rot([esb("cs", (128, 2, 512)) for _ in range(2)])
                    vt_p = Rot([esb("vt", (128, 4, 512), BF16) for _ in range(2)])
                    zt_p = Rot([esb("zt", (128, 4, 512)) for _ in range(1)])
                    gp_p = Rot([esb("gp", (128, 4, 512)) for _ in range(2)])
                    bg_p = Rot([esb("bgt", (128, 4, 16)) for _ in range(2)])
                    sm_p = Rot([esb("sm", (128, 16)) for _ in range(6)])
                    psr = Rot(PSB)
                    wv_in = wb_in.rearrange("(kc p) n -> p kc n", p=128)

                    for (t0, n) in tiles(512):
                        nj = n // 128
                        ri = NB if t0 < CTX else b
                        xt = xt_p.get()
                        S.dma("sp", xt[:, 0:nj, :], stream(b, t0, n).rearrange("(j p) d -> p j d", p=128), writes=[xt])
                        hT = hT_p.get()
                        norm_mod_T(xt, nj, n, Acol1, Bcol1, ri, hT, (junk, stat, xn_p, psr))
                        S.dma("pool", hT_d.rearrange("(kc p) t -> p kc t", p=128)[:, :, t0:t0 + n], hT[:, :, 0:n],
                              reads=[hT])
                        cs = cs_p.get()
                        S.dma("sp", cs[:, 0, 0:n], c_cos[:, t0:t0 + n], writes=[cs])
                        S.dma("sp", cs[:, 1, 0:n], c_sin[:, t0:t0 + n], writes=[cs])

                        def fm_group(col0):
                            wt = wt_p.get()
                            S.dma("sp", wt[:], wv_in[:, :, col0:col0 + 512], writes=[wt])
                            for cc in range(4):
                                ps = psr.get()
                                for kc in range(8):
                                    S.op("pe", lambda e, kc=kc, cc=cc, ps=ps, wt=wt: e.matmul(
                                        ps[:, 0:n], lhsT=wt[:, kc, cc * 128:(cc + 1) * 128], rhs=hT[:, kc, 0:n],
                                        start=(kc == 0), stop=(kc == 7)), [wt, hT], [ps])
                                yield cc, ps

                        def tm_group(col0, ncol=512):
                            wt = wt_p.get()
                            S.dma("sp", wt[:, :, 0:ncol], wv_in[:, :, col0:col0 + ncol], writes=[wt])
                            for j in range(nj):
                                ps = psr.get()
                                for kc in range(8):
                                    S.op("pe", lambda e, kc=kc, j=j, ps=ps, wt=wt: e.matmul(
                                        ps[:, 0:ncol], lhsT=hT[:, kc, j * 128:(j + 1) * 128], rhs=wt[:, kc, 0:ncol],
                                        start=(kc == 0), stop=(kc == 7)), [wt, hT], [ps])
                                yield j, ps

                        uT = uT_p.get()
                        for cc, ps in fm_group(O_U):
                            gelu_ops(tmp_p.get, ps[:, 0:n], ps, n, uT[:, cc, 0:n], uT)
                        ya = ya_p.get()
                        for j, ps in tm_group(O_SV):
                            gv = tmp_p.get()
                            gelu_ops(tmp_p.get, ps[:], ps, 512, gv[:], gv)
                            sq = tmp_p.get()
                            S.op("dve", lambda e, gv=gv, sq=sq: e.tensor_tensor(out=sq[:], in0=gv[:], in1=gv[:],
                                                                                op=ALU.mult), [gv], [sq])
                            ssq = stat.get()
                            S.op("dve", lambda e, sq=sq, ssq=ssq: e.tensor_reduce(
                                out=ssq[:, 0:4], in_=sq[:].rearrange("p (g c) -> p g c", g=4), axis=AX.X, op=ALU.add),
                                [sq], [ssq])
                            rs = stat.get()
                            rstd_ops(ssq, ssq[:, 0:4], rs, rs[:, 0:4], 1.0 / 128)
                            S.op("dve", lambda e, gv=gv, rs=rs: e.tensor_tensor(
                                out=gv[:].rearrange("p (g c) -> p g c", g=4),
                                in0=gv[:].rearrange("p (g c) -> p g c", g=4),
                                in1=rs[:, 0:4].unsqueeze(2).to_broadcast([128, 4, 128]), op=ALU.mult), [gv, rs], [gv])
                            vn = tb_p.get()
                            S.op("dve", lambda e, gv=gv, vn=vn: e.tensor_tensor(out=vn[:], in0=gv[:], in1=sgng[:],
                                                                                op=ALU.mult), [gv, sgng], [vn])
                            pm = psr.get()
                            for g in range(4):
                                S.op("pe", lambda e, g=g, pm=pm, vn=vn: e.matmul(
                                    pm[:, g * 128:(g + 1) * 128], lhsT=vn[:, g * 128:(g + 1) * 128], rhs=wsT[:, g, :],
                                    start=True, stop=True), [vn, wsT], [pm])
                            mx = tmp_p.get()
                            S.op("dve", lambda e, pm=pm, mx=mx: e.tensor_tensor(out=mx[:], in0=pm[:], in1=bsb[:],
                                                                                op=ALU.add), [pm, bsb], [mx])
                            S.op("dve", lambda e, mx=mx, j=j: e.tensor_tensor(
                                out=ya[:, :, j * 128:(j + 1) * 128], in0=mx[:].rearrange("p (g i) -> p g i", g=4),
                                in1=uT[:, :, j * 128:(j + 1) * 128], op=ALU.mult), [mx, uT], [ya])
                        S.dma("pool", yT_d[0].rearrange("(c p) t -> p c t", p=128)[:, :, t0:t0 + n], ya[:, :, 0:n],
                              reads=[ya])
                        for (col0, gcol, dst) in ((O_BQ, gq_c, QT_d), (O_BK, gk_c, KT_d)):
                            qk = qk_p.get()
                            for cc, ps in fm_group(col0):
                                sq = tmp_p.get()
                                S.op("act", lambda e, ps=ps, sq=sq: e.activation(out=sq[:, 0:n], in_=ps[:, 0:n],
                                                                                 func=AF.Square), [ps], [sq])
                                p2 = psr.get()
                                S.op("pe", lambda e, p2=p2, sq=sq: e.matmul(p2[:, 0:n], lhsT=blk64[:], rhs=sq[:, 0:n],
                                                                            start=True, stop=True), [blk64, sq], [p2])
                                rs = tmp_p.get()
                                rstd_ops(p2, p2[:, 0:n], rs, rs[:, 0:n], 1.0 / 64)
                                qn = tb_p.get()
                                S.op("dve", lambda e, ps=ps, rs=rs, qn=qn, gcol=gcol: e.scalar_tensor_tensor(
                                    out=qn[:, 0:n], in0=ps[:, 0:n], scalar=gcol[:, 0:1], in1=rs[:, 0:n], op0=ALU.mult,
                                    op1=ALU.mult), [ps, rs, gcol], [qn])
                                p3 = psr.get()
                                S.op("pe", lambda e, p3=p3, qn=qn: e.matmul(p3[:, 0:n], lhsT=rotT[:], rhs=qn[:, 0:n],
                                                                            start=True, stop=True), [rotT, qn], [p3])
                                t1 = tmp_p.get()
                                S.op("dve", lambda e, qn=qn, t1=t1: e.tensor_tensor(out=t1[:, 0:n], in0=qn[:, 0:n],
                                                                                    in1=cs[:, 0, 0:n], op=ALU.mult),
                                     [qn, cs], [t1])
                                t2 = tmp_p.get()
                                S.op("dve", lambda e, p3=p3, t2=t2: e.tensor_tensor(out=t2[:, 0:n], in0=p3[:, 0:n],
                                                                                    in1=cs[:, 1, 0:n], op=ALU.mult),
                                     [p3, cs], [t2])
                                S.op("pool", lambda e, t1=t1, t2=t2, cc=cc, qk=qk: e.tensor_tensor(
                                    out=qk[:, cc, 0:n], in0=t1[:, 0:n], in1=t2[:, 0:n], op=ALU.add), [t1, t2], [qk])
                            S.dma("pool", dst.rearrange("(c p) t -> p c t", p=128)[:, :, t0:t0 + n], qk[:, :, 0:n],
                                  reads=[qk])
                        vt = vt_p.get()
                        for j, ps in tm_group(O_BV):
                            S.op("act", lambda e, ps=ps, j=j: e.activation(out=vt[:, j, :], in_=ps[:], func=AF.Identity),
                                 [ps], [vt])
                        S.dma("pool", V_d[t0:t0 + n, :].rearrange("(j p) c -> p j c", p=128), vt[:, 0:nj, :], reads=[vt])
                        for gi, col0 in enumerate((O_DQ, O_DK, O_DV)):
                            gp = gp_p.get()
                            for cc, ps in fm_group(col0):
                                S.op("act", lambda e, ps=ps, cc=cc, gp=gp: e.activation(out=gp[:, cc, 0:n], in_=ps[:, 0:n],
                                                                                        func=AF.Identity), [ps], [gp])
                            S.dma("pool", gpre_d[gi * 512:(gi + 1) * 512, :].rearrange("(c p) t -> p c t", p=128)[
                                :, :, t0:t0 + n], gp[:, :, 0:n], reads=[gp])
                        zt = zt_p.get()
                        for j, ps in tm_group(O_DZ):
                            S.op("act", lambda e, ps=ps, j=j: e.activation(out=zt[:, j, :], in_=ps[:], func=AF.Silu),
                                 [ps], [zt])
                        S.dma("pool", zs_d[t0:t0 + n, :].rearrange("(j p) c -> p j c", p=128), zt[:, 0:nj, :], reads=[zt])
                        bgt = bg_p.get()
                        for j, ps in tm_group(O_BETA, 16):
                            S.op("act", lambda e, ps=ps, j=j: e.activation(out=bgt[:, j, 0:8], in_=ps[:, 0:8],
                                                                           func=AF.Sigmoid), [ps], [bgt])
                            xa = sm_p.get()
                            S.op("dve", lambda e, ps=ps, xa=xa: e.tensor_tensor(out=xa[:, 0:8], in0=ps[:, 8:16],
                                                                                in1=dtb_b[:], op=ALU.add),
                                 [ps, dtb_b], [xa])
                            ab = sm_p.get()
                            S.op("dve", lambda e, xa=xa, ab=ab: e.tensor_scalar(out=ab[:, 0:8], in0=xa[:, 0:8], scalar1=-1.0,
                                                                                scalar2=None, op0=ALU.mult), [xa], [ab])
                            S.op("dve", lambda e, xa=xa, ab=ab: e.tensor_tensor(out=ab[:, 0:8], in0=ab[:, 0:8],
                                                                                in1=xa[:, 0:8], op=ALU.min), [xa, ab], [ab])
                            S.op("act", lambda e, ab=ab: e.activation(out=ab[:, 0:8], in_=ab[:, 0:8], func=AF.Exp),
                                 [ab], [ab])
                            S.op("act", lambda e, ab=ab: e.activation(out=ab[:, 0:8], in_=ab[:, 0:8], func=AF.Ln,
                                                                      bias=1.0), [ab], [ab])
                            S.op("dve", lambda e, xa=xa, ab=ab: e.scalar_tensor_tensor(
                                out=ab[:, 0:8], in0=xa[:, 0:8], scalar=0.0, in1=ab[:, 0:8], op0=ALU.max, op1=ALU.add),
                                [xa, ab], [ab])
                            S.op("dve", lambda e, ab=ab, j=j: e.tensor_tensor(out=bgt[:, j, 8:16], in0=ab[:, 0:8],
                                                                              in1=alog_b[:], op=ALU.mult),
                                 [ab, alog_b], [bgt])
                        S.dma("pool", bg_d[t0:t0 + n, :].rearrange("(j p) c -> p j c", p=128), bgt[:, 0:nj, :],
                              reads=[bgt])
                    S.barrier()

                if dbg == "p1":
                    break
                phase_attn(nc, S, PSB, ident, eps_t, l, b, NB, CTX, T, last, lam_t, subg, lam_init, QT_d, KT_d, V_d, yT_d)
                if dbg == "attn":
                    break
                phase_gdn(nc, S, PSB, ident, eps_t, l, b, CTX, T, cwq, gdng, c_gdn, gpre_d, gqT_d, gkT_d, gkt_d, gvt_d,
                          zs_d, bg_d, of_d, yT_d, dbg=dbg)
                if dbg is not None and dbg.startswith("g"):
                    break
                phase_merge_ffn(nc, S, PSB, ident, eps_t, l, b, NB, CTX, T, last, stream, tiles, seg_bounds, norm_mod_T,
                                Acol2, Bcol2, cwf, modrow_d, wb_in, wb_br, wb_o, wb_up, wb_dn, hT_d, h2T_d, yT_d)
            if dbg is not None:
                break
    S.barrier()
    return nc, S


def phase_attn(nc, S, PSB, ident, eps_t, l, b, NB, CTX, T, last, lam_t, subg, lam_init, QT_d, KT_d, V_d, yT_d):
    NKT = T // 128
    with contextlib.ExitStack() as ES:
        cnt = [0]

        def esb(name, shape, dt=F32):
            cnt[0] += 1
            return Tl(ES.enter_context(nc.sbuf_tensor(f"a{name}_{l}_{b}_{cnt[0]}", list(shape), dt)))
        KT = esb("KT", (128, 4, T), BF16)
        V = esb("V", (128, NKT, 4, 130), BF16)
        S.op("dve", lambda e: e.memset(V[:, :, :, 128:130], 1.0), [], [V])
        for h in range(4):
            S.dma("sp", KT[:, h, :], KT_d[h * 128:(h + 1) * 128, :], writes=[KT])
        for kt in range(NKT):
            S.dma("sp", V[:, kt, :, 0:128], V_d[kt * 128:(kt + 1) * 128, :].rearrange("p (h c) -> p h c", h=4),
                  writes=[V])
        QT_p = Rot([esb("QT", (128, 4, 512), BF16) for _ in range(2)])
        PT_p = Rot([esb("PT", (128, 512), BF16) for _ in range(4)])
        o_p = Rot([esb("o", (128, 2, 128)) for _ in range(4)])
        r_p = Rot([esb("r", (128, 4)) for _ in range(8)])
        yb_p = Rot([esb("yb", (128, 4, 128)) for _ in range(2)])
        ybT_p = Rot([esb("ybT", (128, 4, 512), BF16) for _ in range(2)])
        jk_p = Rot([esb("jk", (128, 128)) for _ in range(2)])
        acc = PSB[0:4]
        st_p = Rot(PSB[4:7])
        tr_ps = PSB[7]
        qtiles = []
        if not last:
            qtiles.append((0, CTX, 0, CTX // 128))
        for t0 in range(CTX, T, 512):
            qtiles.append((t0, min(512, T - t0), 0, NKT))
        for (t0, n, ka, kb) in qtiles:
            nj = n // 128
            QT = QT_p.get()
            S.dma("sp", QT[:, :, 0:n], QT_d.rearrange("(h p) t -> p h t", p=128)[:, :, t0:t0 + n], writes=[QT])
            ybT = ybT_p.get()
            om = {}
            for h in range(4):
                for m in range(2):
                    pr = slice(m * 64, (m + 1) * 64)
                    for kt in range(ka, kb):
                        st = st_p.get()
                        S.op("pe", lambda e, st=st, h=h, kt=kt, pr=pr: e.matmul(
                            st[:, 0:n], lhsT=KT[pr, h, kt * 128:(kt + 1) * 128], rhs=QT[pr, h, 0:n], start=True,
                            stop=True), [KT, QT], [st])
                        PT = PT_p.get()
                        S.op("act", lambda e, st=st, PT=PT: e.activation(out=PT[:, 0:n], in_=st[:, 0:n], func=AF.Exp,
                                                                         scale=0.125), [st], [PT])
                        for j in range(nj):
                            S.op("pe", lambda e, j=j, PT=PT, kt=kt, h=h: e.matmul(
                                acc[j][:, 0:129], lhsT=PT[:, j * 128:(j + 1) * 128], rhs=V[:, kt, h, 0:129],
                                start=(kt == ka), stop=(kt == kb - 1)), [PT, V], [acc[j]])
                    for j in range(nj):
                        if m == 0:
                            om[j] = o_p.get()
                        r = r_p.get()
                        S.op("dve", lambda e, j=j, r=r: e.reciprocal(out=r[:, 0:1], in_=acc[j][:, 128:129]),
                             [acc[j]], [r])
                        if m == 1:
                            S.op("dve", lambda e, r=r: e.tensor_tensor(out=r[:, 0:1], in0=r[:, 0:1], in1=lam_t[:, 0:1],
                                                                       op=ALU.mult), [r, lam_t], [r])
                        S.op("act", lambda e, j=j, r=r, m=m, o=om[j]: e.activation(
                            out=o[:, m, :], in_=acc[j][:, 0:128], func=AF.Identity, scale=r[:, 0:1]), [acc[j], r], [om[j]])
                for j in range(nj):
                    o = om[j]
                    S.op("dve", lambda e, o=o: e.tensor_tensor(out=o[:, 0, :], in0=o[:, 0, :], in1=o[:, 1, :],
                                                               op=ALU.add), [o], [o])
                    ssq = r_p.get()
                    jk = jk_p.get()
                    S.op("act", lambda e, o=o, jk=jk, ssq=ssq: e.activation(out=jk[:], in_=o[:, 0, :], func=AF.Square,
                                                                            accum_out=ssq[:, 0:1]), [o], [jk, ssq])
                    rs = r_p.get()
                    S.op("act", lambda e, ssq=ssq, rs=rs: e.activation(out=rs[:, 0:1], in_=ssq[:, 0:1], func=AF.Sqrt,
                                                                       bias=eps_t[:, :], scale=1.0 / 128),
                         [ssq, eps_t], [rs])
                    S.op("dve", lambda e, rs=rs: e.reciprocal(out=rs[:, 0:1], in_=rs[:, 0:1]), [rs], [rs])
                    yj = jk_p.get()
                    S.op("dve", lambda e, o=o, rs=rs, yj=yj: e.scalar_tensor_tensor(
                        out=yj[:], in0=o[:, 0, :], scalar=rs[:, 0:1], in1=subg[:], op0=ALU.mult, op1=ALU.mult),
                        [o, rs, subg], [yj])
                    S.op("pe", lambda e, yj=yj: e.transpose(out=tr_ps[:, 0:128], in_=yj[:], identity=ident[:]),
                         [yj, ident], [tr_ps])
                    S.op("act", lambda e, j=j, h=h: e.activation(out=ybT[:, h, j * 128:(j + 1) * 128],
                                                                 in_=tr_ps[:, 0:128], func=AF.Identity,
                                                                 scale=(1.0 - lam_init)), [tr_ps], [ybT])
            S.dma("pool", yT_d[1].rearrange("(c p) t -> p c t", p=128)[:, :, t0:t0 + n], ybT[:, :, 0:n], reads=[ybT])
        S.barrier()


def phase_gdn(nc, S, PSB, ident, eps_t, l, b, CTX, T, cwq, gdng, c_gdn, gpre_d, gqT_d, gkT_d, gkt_d, gvt_d, zs_d, bg_d,
              of_d, yT_d, dbg=None):
    DK = 128 ** -0.5
    with contextlib.ExitStack() as ES:
        cnt = [0]

        def esb(name, shape, dt=F32):
            cnt[0] += 1
            return Tl(ES.enter_context(nc.sbuf_tensor(f"g{name}_{l}_{b}_{cnt[0]}", list(shape), dt)))
        in_p = Rot([esb("in", (128, 514)) for _ in range(3)])
        t_p = Rot([esb("t", (128, 512)) for _ in range(8)])
        tk_p = Rot([esb("tk", (128, 4, 128)) for _ in range(3)])
        ones = esb("ones", (128, 128))
        S.op("dve", lambda e: e.memset(ones[:], 1.0), [], [ones])
        psr = Rot(PSB)
        segs = [(0, CTX), (CTX, T)]
        for fc in range(12):
            kind = fc // 4
            h = fc % 4
            for (s0, s1) in segs:
                for t0 in range(s0, s1, 512):
                    n = min(512, s1 - t0)
                    xin = in_p.get()
                    lo = max(t0 - 1, s0)
                    hi = min(t0 + n + 1, s1)
                    if lo != t0 - 1 or hi != t0 + n + 1:
                        S.op("pool", lambda e, xin=xin: e.memset(xin[:], 0.0), [], [xin])
                    S.dma("sp", xin[:, lo - (t0 - 1):hi - (t0 - 1)], gpre_d[fc * 128:(fc + 1) * 128, lo:hi], writes=[xin])
                    y = t_p.get()
                    S.op("act", lambda e, xin=xin, y=y: e.activation(out=y[:, 0:n], in_=xin[:, 0:n], func=AF.Identity,
                                                                     scale=cwq[:, fc, 0:1]), [xin, cwq], [y])
                    S.op("dve", lambda e, xin=xin, y=y: e.scalar_tensor_tensor(
                        out=y[:, 0:n], in0=xin[:, 1:n + 1], scalar=cwq[:, fc, 1:2], in1=y[:, 0:n], op0=ALU.mult,
                        op1=ALU.add), [xin, cwq, y], [y])
                    S.op("dve", lambda e, xin=xin, y=y: e.scalar_tensor_tensor(
                        out=y[:, 0:n], in0=xin[:, 2:n + 2], scalar=cwq[:, fc, 2:3], in1=y[:, 0:n], op0=ALU.mult,
                        op1=ALU.add), [xin, cwq, y], [y])
                    S.op("act", lambda e, y=y: e.activation(out=y[:, 0:n], in_=y[:, 0:n], func=AF.Silu), [y], [y])
                    if kind < 2:
                        sq = t_p.get()
                        S.op("dve", lambda e, y=y, sq=sq: e.tensor_tensor(out=sq[:, 0:n], in0=y[:, 0:n], in1=y[:, 0:n],
                                                                          op=ALU.mult), [y], [sq])
                        p2 = psr.get()
                        S.op("pe", lambda e, p2=p2, sq=sq: e.matmul(p2[:, 0:n], lhsT=ones[:], rhs=sq[:, 0:n], start=True,
                                                                    stop=True), [ones, sq], [p2])
                        rs = t_p.get()
                        S.op("act", lambda e, p2=p2, rs=rs: e.activation(out=rs[:, 0:n], in_=p2[:, 0:n], func=AF.Sqrt,
                                                                         bias=eps_t[:, :], scale=1.0), [p2, eps_t], [rs])
                        S.op("dve", lambda e, rs=rs: e.reciprocal(out=rs[:, 0:n], in_=rs[:, 0:n]), [rs], [rs])
                        S.op("dve", lambda e, y=y, rs=rs: e.tensor_tensor(out=y[:, 0:n], in0=y[:, 0:n], in1=rs[:, 0:n],
                                                                          op=ALU.mult), [y, rs], [y])
                        dstT = gqT_d if kind == 0 else gkT_d
                        S.dma("pool", dstT[h * 128:(h + 1) * 128, t0:t0 + n], y[:, 0:n], reads=[y])
                    if kind >= 1:
                        nj = n // 128
                        pt = psr.get()
                        for j in range(nj):
                            S.op("pe", lambda e, j=j, pt=pt, y=y: e.transpose(out=pt[:, j * 128:(j + 1) * 128],
                                                                              in_=y[:, j * 128:(j + 1) * 128],
                                                                              identity=ident[:]), [y, ident], [pt])
                        tk = tk_p.get()
                        S.op("act", lambda e, pt=pt, tk=tk: e.activation(out=tk[:].rearrange("p j d -> p (j d)")[:, 0:n],
                                                                         in_=pt[:, 0:n], func=AF.Identity), [pt], [tk])
                        dst = gkt_d if kind == 1 else gvt_d
                        S.dma("pool", dst[t0:t0 + n, h * 128:(h + 1) * 128].rearrange("(j p) d -> p j d", p=128),
                              tk[:, 0:nj, :], reads=[tk])
        S.barrier()
    if dbg == "g1":
        return
    NCH = T // 64
    NCC = CTX // 64
    order_f = list(range(NCH))
    order_b = list(range(NCC - 1, -1, -1)) + list(range(NCH - 1, NCC - 1, -1))
    with contextlib.ExitStack() as ES:
        cnt = [0]

        zsrc = [None]

        def esb(name, shape, dt=F32, zero=False, r=False):
            cnt[0] += 1
            t = Tl(ES.enter_context(nc.sbuf_tensor(f"s{name}_{l}_{b}_{cnt[0]}", list(shape), dt)))
            if zero and r:
                fl = t[:].rearrange("p a b -> p (a b)")
                S.op("dve", lambda e: e.tensor_copy(out=fl.bitcast(mybir.dt.float32r), in_=zsrc[0][:, 0:fl.shape[1]]),
                     [zsrc[0]], [t])
            elif zero:
                S.op("dve", lambda e: e.memset(t[:], 0.0), [], [t])
            return t
        zsrc[0] = esb("zsrc", (128, 1024), zero=True)
        gc = esb("gc", (64, 6, 8, 64))
        S.dma("sp", gc[:], c_gdn, writes=[gc])
        triC, m_incl, m_strict, m_inclT, id8, ones8 = (gc[:, i] for i in range(6))
        triP = esb("triP", (128, 2, 128), zero=True)
        for d in range(2):
            S.op("dve", lambda e, d=d: e.tensor_copy(out=triP[0:64, d, 0:64], in_=triC[:, d * 4, :]), [gc, triP], [triP])
        nones = esb("nones", (128, 128), zero=True)
        S.op("dve", lambda e: e.memset(nones[0:64, :], -1.0), [nones], [nones])
        ones128 = esb("ones128", (128, 128), zero=True)
        S.op("dve", lambda e: e.memset(ones128[0:64, :], 1.0), [ones128], [ones128])
        identr = esb("identr", (128, 64))
        S.op("dve", lambda e: e.tensor_copy(out=identr[:].bitcast(mybir.dt.float32r), in_=ident[:, 0:64]), [ident], [identr])
        St = esb("S", (128, 8, 128), zero=True)
        Sb = esb("Sb", (128, 8, 128), BF16, zero=True)
        NB2 = 3
        qT_p = Rot([esb("qT", (128, 9, 64), zero=True) for _ in range(NB2)])
        kT_p = Rot([esb("kT", (128, 9, 64), zero=True) for _ in range(NB2)])
        kt_p = Rot([esb("kt", (64, 8, 128)) for _ in range(NB2)])
        vt_p = Rot([esb("vt", (64, 8, 128)) for _ in range(NB2)])
        b8_p = Rot([esb("b8", (128, 2, 8), zero=True) for _ in range(NB2)])
        s8_p = Rot([esb("s8", (128, 8)) for _ in range(24)])
        B64 = Rot([esb("B64", (128, 9, 64), zero=True) for _ in range(10)])
        B64r = Rot([esb("B64r", (128, 9, 64), zero=True, r=True) for _ in range(20)])
        B64b = Rot([esb("B64b", (128, 9, 64), BF16, zero=True) for _ in range(4)])
        B128 = Rot([esb("B128", (128, 8, 128), zero=True) for _ in range(3)])
        B128r = Rot([esb("B128r", (128, 8, 128), zero=True, r=True) for _ in range(4)])
        B128b = Rot([esb("B128b", (128, 8, 128), BF16, zero=True) for _ in range(4)])
        W64b = Rot([esb("W64b", (128, 9, 64), BF16, zero=True) for _ in range(4)])
        T512 = Rot([esb("T512", (128, 4, 128)) for _ in range(3)])
        psr = Rot(PSB)

        def v3(ps, np_, a, bb):
            return ps[0:np_, 0:a * bb].rearrange("p (a b) -> p a b", a=a)

        def bc(ap, np_, a, bb):
            return ap.unsqueeze(2).to_broadcast([np_, a, bb])

        def f8(X):
            return X[0:64, 0:8, :]

        F32R = mybir.dt.float32r

        def l2(X, k):
            return X[:, k:k + 2, :].rearrange("p a b -> p (a b)").bitcast(F32R)

        def l2f(X, k):
            return X[:, k:k + 2, :].rearrange("p a b -> p (a b)")

        def rr(ap):
            return ap.bitcast(F32R)

        def slot_gen(s):
            tf = order_f[s] * 64
            tb = order_b[s] * 64
            qT = qT_p.get()
            kT = kT_p.get()
            kt = kt_p.get()
            vt = vt_p.get()
            b8 = b8_p.get()
            for (bo, tt, dcol) in ((0, tf, 0), (4, tb, 4)):
                S.dma("sp", qT[:, bo:bo + 4, :], gqT_d.rearrange("(h p) t -> p h t", p=128)[:, :, tt:tt + 64], writes=[qT])
                S.dma("sp", kT[:, bo:bo + 4, :], gkT_d.rearrange("(h p) t -> p h t", p=128)[:, :, tt:tt + 64], writes=[kT])
                S.dma("sp", kt[:, bo:bo + 4, :], gkt_d[tt:tt + 64, :].rearrange("p (h d) -> p h d", h=4), writes=[kt])
                S.dma("sp", vt[:, bo:bo + 4, :], gvt_d[tt:tt + 64, :].rearrange("p (h d) -> p h d", h=4), writes=[vt])
                S.dma("sp", b8[0:64, 0, bo:bo + 4], bg_d[tt:tt + 64, dcol:dcol + 4], writes=[b8])
                S.dma("sp", b8[0:64, 1, bo:bo + 4], bg_d[tt:tt + 64, 8 + dcol:8 + dcol + 4], writes=[b8])
            beta8 = b8[0:64, 0, :]
            g8 = b8[0:64, 1, :]
            g8p = b8[:, 1, :]
            gTri = B64.get()
            S.op("dve", lambda e, gTri=gTri, g8=g8: e.tensor_tensor(out=f8(gTri), in0=triC, in1=bc(g8, 64, 8, 64),
                                                                    op=ALU.mult), [gc, b8], [gTri])
            psA = psr.get()
            for d in range(2):
                S.op("pe", lambda e, d=d, psA=psA, g8p=g8p: e.matmul(psA[:, d * 4:(d + 1) * 4], lhsT=triP[:, d, :],
                                                                     rhs=g8p[:, d * 4:(d + 1) * 4], start=True, stop=True),
                     [triP, b8], [psA])
            S.op("pe", lambda e, psA=psA, g8p=g8p: e.matmul(psA[:, 8:16], lhsT=ones128[:], rhs=g8p, start=True, stop=True),
                 [ones128, b8], [psA])
            G = s8_p.get()
            Gt = s8_p.get()
            S.op("act", lambda e, G=G, psA=psA: e.activation(out=G[0:64, :], in_=psA[0:64, 0:8], func=AF.Identity),
                 [psA], [G])
            S.op("act", lambda e, Gt=Gt, psA=psA: e.activation(out=Gt[:], in_=psA[:, 8:16], func=AF.Identity),
                 [psA], [Gt])
            eG = s8_p.get()
            S.op("act", lambda e, G=G, eG=eG: e.activation(out=eG[0:64, :], in_=G[0:64, :], func=AF.Exp), [G], [eG])
            gtot = s8_p.get()
            S.op("act", lambda e, Gt=Gt, gtot=gtot: e.activation(out=gtot[:], in_=Gt[:], func=AF.Exp), [Gt], [gtot])
            etl = s8_p.get()
            S.op("dve", lambda e, etl=etl, Gt=Gt, G=G: e.tensor_tensor(out=etl[0:64, :], in0=Gt[0:64, :], in1=G[0:64, :],
                                                                       op=ALU.subtract), [Gt, G], [etl])
            S.op("act", lambda e, etl=etl: e.activation(out=etl[0:64, :], in_=etl[0:64, :], func=AF.Exp), [etl], [etl])
            beG = s8_p.get()
            S.op("dve", lambda e, beG=beG, eG=eG, beta8=beta8: e.tensor_tensor(out=beG[0:64, :], in0=eG[0:64, :],
                                                                               in1=beta8, op=ALU.mult), [eG, b8], [beG])
            eGq = s8_p.get()
            S.op("dve", lambda e, eGq=eGq, eG=eG: e.tensor_scalar(out=eGq[0:64, :], in0=eG[0:64, :], scalar1=DK,
                                                                  scalar2=None, op0=ALU.mult), [eG], [eGq])
            yield "st"
            psD = psr.get()
            for k in range(8):
                S.op("pe", lambda e, k=k, psD=psD, gTri=gTri: e.matmul(psD[:, k * 64:(k + 1) * 64], lhsT=l2f(gTri, k),
                                                                       rhs=ones128[:, 0:64], start=True, stop=False),
                     [gTri, ones128], [psD])
                S.op("pe", lambda e, k=k, psD=psD, gTri=gTri: e.matmul(psD[:, k * 64:(k + 1) * 64], lhsT=nones[:],
                                                                       rhs=gTri[:, k, :], start=False, stop=True),
                     [gTri, nones], [psD])
            seg = B64.get()
            S.op("dve", lambda e, seg=seg, psD=psD: e.tensor_scalar(out=f8(seg), in0=v3(psD, 64, 8, 64), scalar1=0.0,
                                                                    scalar2=None, op0=ALU.min), [psD], [seg])
            S.op("act", lambda e, seg=seg: e.activation(out=f8(seg), in_=f8(seg), func=AF.Exp), [seg], [seg])
            segS = B64.get()
            S.op("pool", lambda e, seg=seg, segS=segS: e.tensor_tensor(out=f8(segS), in0=f8(seg), in1=m_strict,
                                                                       op=ALU.mult), [seg, gc], [segS])
            sgT = B64.get()
            S.op("dve", lambda e, sgT=sgT, psD=psD: e.tensor_scalar(out=f8(sgT), in0=v3(psD, 64, 8, 64), scalar1=-1.0,
                                                                    scalar2=0.0, op0=ALU.mult, op1=ALU.min),
                 [psD], [sgT])
            S.op("act", lambda e, sgT=sgT: e.activation(out=f8(sgT), in_=f8(sgT), func=AF.Exp), [sgT], [sgT])
            S.op("dve", lambda e, sgT=sgT: e.scalar_tensor_tensor(out=f8(sgT), in0=f8(sgT), scalar=DK, in1=m_inclT,
                                                                  op0=ALU.mult, op1=ALU.mult), [sgT, gc], [sgT])
            yield "st"
            psK = psr.get()
            psQ = psr.get()
            for k in range(8):
                S.op("pe", lambda e, k=k, psK=psK, kT=kT: e.matmul(psK[:, k * 64:(k + 1) * 64], lhsT=l2f(kT, k),
                                                                   rhs=kT[:, k, :], start=True, stop=True), [kT], [psK])
            for k in range(8):
                S.op("pe", lambda e, k=k, psQ=psQ, kT=kT, qT=qT: e.matmul(psQ[:, k * 64:(k + 1) * 64], lhsT=l2f(kT, k),
                                                                          rhs=qT[:, k, :], start=True, stop=True),
                     [kT, qT], [psQ])
            A1 = B64.get()
            S.op("dve", lambda e, A1=A1, psK=psK, segS=segS: e.tensor_tensor(out=f8(A1), in0=v3(psK, 64, 8, 64),
                                                                             in1=f8(segS), op=ALU.mult), [psK, segS], [A1])
            A = B64r.get()
            S.op("dve", lambda e, A=A, A1=A1, beta8=beta8: e.tensor_tensor(out=rr(f8(A)), in0=f8(A1),
                                                                           in1=bc(beta8, 64, 8, 64), op=ALU.mult),
                 [A1, b8], [A])
            inT = B64b.get()
            S.op("dve", lambda e, inT=inT, psQ=psQ, sgT=sgT: e.tensor_tensor(out=f8(inT), in0=v3(psQ, 64, 8, 64),
                                                                             in1=f8(sgT), op=ALU.mult), [psQ, sgT], [inT])
            yield "st"
            psT = psr.get()
            for k in range(8):
                S.op("pe", lambda e, k=k, psT=psT, A=A: e.matmul(psT[:, k * 64:(k + 1) * 64], lhsT=l2(A, k),
                                                                 rhs=rr(identr[:]), start=True, stop=True),
                     [A, identr], [psT])
            AT = B64r.get()
            S.op("act", lambda e, AT=AT, psT=psT: e.activation(out=rr(f8(AT)), in_=v3(psT, 64, 8, 64), func=AF.Identity),
                 [psT], [AT])
            TT = B64r.get()
            S.op("dve", lambda e, TT=TT, AT=AT: e.tensor_tensor(out=rr(f8(TT)), in0=id8, in1=f8(AT),
                                                                op=ALU.subtract), [gc, AT], [TT])
            yield "st"
            P, PT_ = A, AT
            for lev in range(1, 6):
                psP = psr.get()
                for k in range(8):
                    S.op("pe", lambda e, k=k, psP=psP, P=P, PT_=PT_: e.matmul(
                        psP[:, k * 64:(k + 1) * 64], lhsT=l2(PT_, k), rhs=rr(P[:, k, :]), start=True, stop=True),
                        [P, PT_], [psP])
                if lev < 5:
                    psPT = psr.get()
                    for k in range(8):
                        S.op("pe", lambda e, k=k, psPT=psPT, P=P, PT_=PT_: e.matmul(
                            psPT[:, k * 64:(k + 1) * 64], lhsT=l2(P, k), rhs=rr(PT_[:, k, :]), start=True, stop=True),
                            [P, PT_], [psPT])
                Pn = B64r.get()
                S.op("act", lambda e, Pn=Pn, psP=psP: e.activation(out=rr(f8(Pn)), in_=v3(psP, 64, 8, 64), func=AF.Identity),
                     [psP], [Pn])
                if lev < 5:
                    PTn = B64r.get()
                    S.op("dve", lambda e, PTn=PTn, psPT=psPT: e.tensor_copy(out=rr(f8(PTn)), in_=v3(psPT, 64, 8, 64)),
                         [psPT], [PTn])
                else:
                    PTn = None
                psZ = psr.get()
                for k in range(8):
                    S.op("pe", lambda e, k=k, psZ=psZ, Pn=Pn, TT=TT: e.matmul(
                        psZ[:, k * 64:(k + 1) * 64], lhsT=l2(Pn, k), rhs=rr(TT[:, k, :]), start=True, stop=True),
                        [Pn, TT], [psZ])
                TTn = B64r.get()
                S.op("dve", lambda e, TTn=TTn, TT=TT, psZ=psZ: e.tensor_tensor(out=rr(f8(TTn)), in0=f8(TT),
                                                                               in1=v3(psZ, 64, 8, 64), op=ALU.add),
                     [TT, psZ], [TTn])
                TT = TTn
                P, PT_ = Pn, PTn
                yield "st"
            vb = B128r.get()
            S.op("pool", lambda e, vb=vb, vt=vt, beta8=beta8: e.tensor_tensor(out=rr(vb[0:64]), in0=vt[:],
                                                                              in1=bc(beta8, 64, 8, 128), op=ALU.mult),
                 [vt, b8], [vb])
            kbg = B128r.get()
            S.op("dve", lambda e, kbg=kbg, kt=kt, beG=beG: e.tensor_tensor(out=rr(kbg[0:64]), in0=kt[:],
                                                                           in1=bc(beG[0:64, :], 64, 8, 128), op=ALU.mult),
                 [kt, beG], [kbg])
            ktl = B128b.get()
            S.op("pool", lambda e, ktl=ktl, kt=kt, etl=etl: e.tensor_tensor(out=ktl[0:64], in0=kt[:],
                                                                            in1=bc(etl[0:64, :], 64, 8, 128),
                                                                            op=ALU.mult), [kt, etl], [ktl])
            qTb = W64b.get()
            S.op("pool", lambda e, qTb=qTb, qT=qT: e.tensor_copy(out=qTb[:, 0:8, :], in_=qT[:, 0:8, :]), [qT], [qTb])
            u = B128.get()
            for hf in range(2):
                psU = psr.get()
                for k4 in range(4):
                    k = hf * 4 + k4
                    S.op("pe", lambda e, k=k, k4=k4, psU=psU, TT=TT, vb=vb: e.matmul(
                        psU[:, k4 * 128:(k4 + 1) * 128], lhsT=l2(TT, k), rhs=rr(vb[:, k, :]), start=True, stop=True),
                        [TT, vb], [psU])
                S.op("act", lambda e, hf=hf, psU=psU, u=u: e.activation(out=u[0:64, hf * 4:(hf + 1) * 4, :],
                                                                        in_=v3(psU, 64, 4, 128), func=AF.Identity),
                     [psU], [u])
            psW = psr.get()
            for k in range(8):
                S.op("pe", lambda e, k=k, psW=psW, kbg=kbg, TT=TT: e.matmul(psW[:, k * 64:(k + 1) * 64], lhsT=rr(kbg[:, k, :]),
                                                                            rhs=rr(TT[:, k, :]), start=True, stop=True),
                     [kbg, TT], [psW])
            wTb = W64b.get()
            S.op("act", lambda e, wTb=wTb, psW=psW: e.activation(out=wTb[:, 0:8, :], in_=v3(psW, 128, 8, 64),
                                                                 func=AF.Identity), [psW], [wTb])
            yield "PRE_DONE"
            vn = B128b.get()
            for hf in range(2):
                hs = slice(hf * 4, (hf + 1) * 4)
                psWS = psr.get()
                for k4 in range(4):
                    k = hf * 4 + k4
                    S.op("pe", lambda e, k=k, k4=k4, psWS=psWS, wTb=wTb: e.matmul(
                        psWS[:, k4 * 128:(k4 + 1) * 128], lhsT=l2f(wTb, k), rhs=Sb[:, k, :], start=True, stop=True),
                        [wTb, Sb], [psWS])
                S.op("dve", lambda e, hs=hs, psWS=psWS, vn=vn, u=u: e.tensor_tensor(
                    out=vn[0:64, hs, :], in0=u[0:64, hs, :], in1=v3(psWS, 64, 4, 128), op=ALU.subtract), [u, psWS], [vn])
                psQS = psr.get()
                for k4 in range(4):
                    k = hf * 4 + k4
                    S.op("pe", lambda e, k=k, k4=k4, psQS=psQS, qTb=qTb: e.matmul(
                        psQS[:, k4 * 128:(k4 + 1) * 128], lhsT=l2f(qTb, k), rhs=Sb[:, k, :], start=True, stop=True),
                        [qTb, Sb], [psQS])
                psIV = psr.get()
                for k4 in range(4):
                    k = hf * 4 + k4
                    S.op("pe", lambda e, k=k, k4=k4, psIV=psIV, inT=inT, vn=vn: e.matmul(
                        psIV[:, k4 * 128:(k4 + 1) * 128], lhsT=l2f(inT, k), rhs=vn[:, k, :], start=True, stop=True),
                        [inT, vn], [psIV])
                o = T512.get()
                S.op("dve", lambda e, o=o, psQS=psQS, eGq=eGq, hs=hs: e.tensor_tensor(
                    out=o[0:64], in0=v3(psQS, 64, 4, 128), in1=bc(eGq[0:64, hs], 64, 4, 128), op=ALU.mult),
                    [psQS, eGq], [o])
                S.op("dve", lambda e, o=o, psIV=psIV: e.tensor_tensor(out=o[0:64], in0=o[0:64], in1=v3(psIV, 64, 4, 128),
                                                                      op=ALU.add), [o, psIV], [o])
                tt = tf if hf == 0 else tb
                S.dma("pool", of_d[hf, tt:tt + 64, :].rearrange("p (h d) -> p h d", h=4), o[0:64], reads=[o])
                psKV = psr.get()
                for k4 in range(4):
                    k = hf * 4 + k4
                    S.op("pe", lambda e, k=k, k4=k4, psKV=psKV, ktl=ktl, vn=vn: e.matmul(
                        psKV[:, k4 * 128:(k4 + 1) * 128], lhsT=ktl[:, k, :], rhs=vn[:, k, :], start=True, stop=True),
                        [ktl, vn], [psKV])
                sd = T512.get()
                S.op("pool", lambda e, sd=sd, hs=hs, gtot=gtot: e.tensor_tensor(
                    out=sd[:], in0=St[:, hs, :], in1=bc(gtot[:, hs], 128, 4, 128), op=ALU.mult), [St, gtot], [sd])
                S.op("dve", lambda e, sd=sd, hs=hs, psKV=psKV: e.tensor_tensor(
                    out=St[:, hs, :], in0=sd[:], in1=v3(psKV, 128, 4, 128), op=ALU.add), [sd, psKV], [St])
                S.op("act", lambda e, hs=hs: e.activation(out=Sb[:, hs, :], in_=St[:, hs, :], func=AF.Identity),
                     [St], [Sb])
        NSL = NCH if dbg not in ("g2pre", "g2inv", "g2one") else 1
        for s0 in range(0, NSL, 2):
            gens = [slot_gen(s) for s in range(s0, min(s0 + 2, NSL))]
            done = [False] * len(gens)
            while not all(done):
                for i, g in enumerate(gens):
                    if not done[i] and next(g) == "PRE_DONE":
                        done[i] = True
            for g in gens:
                for _ in g:
                    pass
        S.barrier()
    if dbg is not None and dbg.startswith("g2"):
        return
    with contextlib.ExitStack() as ES:
        cnt = [0]

        def esb(name, shape, dt=F32):
            cnt[0] += 1
            return Tl(ES.enter_context(nc.sbuf_tensor(f"o{name}_{l}_{b}_{cnt[0]}", list(shape), dt)))
        of_p = Rot([esb("of", (128, 512)) for _ in range(2)])
        ob_p = Rot([esb("ob", (128, 512)) for _ in range(2)])
        z_p = Rot([esb("z", (128, 512)) for _ in range(2)])
        sq_p = Rot([esb("sq", (128, 512)) for _ in range(2)])
        st_p = Rot([esb("st", (128, 4)) for _ in range(4)])
        yT_p = Rot([esb("yT", (128, 4, 128), BF16) for _ in range(2)])
        psr = Rot(PSB)
        for t0 in range(0, T, 128):
            of = of_p.get()
            ob = ob_p.get()
            z = z_p.get()
            S.dma("sp", of[:], of_d[0, t0:t0 + 128, :], writes=[of])
            S.dma("sp", ob[:], of_d[1, t0:t0 + 128, :], writes=[ob])
            S.dma("sp", z[:], zs_d[t0:t0 + 128, :], writes=[z])
            S.op("dve", lambda e, of=of, ob=ob: e.tensor_tensor(out=of[:], in0=of[:], in1=ob[:], op=ALU.add), [of, ob], [of])
            sq = sq_p.get()
            S.op("pool", lambda e, of=of, sq=sq: e.tensor_tensor(out=sq[:], in0=of[:], in1=of[:], op=ALU.mult), [of], [sq])
            ssq = st_p.get()
            S.op("dve", lambda e, sq=sq, ssq=ssq: e.tensor_reduce(out=ssq[:], in_=sq[:].rearrange("p (h d) -> p h d", h=4),
                                                                  axis=AX.X, op=ALU.add), [sq], [ssq])
            rs = st_p.get()
            S.op("act", lambda e, ssq=ssq, rs=rs: e.activation(out=rs[:], in_=ssq[:], func=AF.Sqrt, bias=eps_t[:, :],
                                                               scale=1.0 / 128), [ssq, eps_t], [rs])
            S.op("dve", lambda e, rs=rs: e.reciprocal(out=rs[:], in_=rs[:]), [rs], [rs])
            S.op("dve", lambda e, of=of, rs=rs: e.tensor_tensor(
                out=of[:].rearrange("p (h d) -> p h d", h=4), in0=of[:].rearrange("p (h d) -> p h d", h=4),
                in1=rs[:].unsqueeze(2).to_broadcast([128, 4, 128]), op=ALU.mult), [of, rs], [of])
            S.op("pool", lambda e, of=of: e.tensor_tensor(out=of[:], in0=of[:], in1=gdng[:], op=ALU.mult), [of, gdng], [of])
            S.op("dve", lambda e, of=of, z=z: e.tensor_tensor(out=of[:], in0=of[:], in1=z[:], op=ALU.mult), [of, z], [of])
            pt = psr.get()
            for h in range(4):
                S.op("pe", lambda e, h=h, pt=pt, of=of: e.transpose(out=pt[:, h * 128:(h + 1) * 128],
                                                                    in_=of[:, h * 128:(h + 1) * 128], identity=ident[:]),
                     [of, ident], [pt])
            yT = yT_p.get()
            S.op("act", lambda e, pt=pt, yT=yT: e.activation(out=yT[:].rearrange("p h t -> p (h t)"), in_=pt[:],
                                                             func=AF.Identity), [pt], [yT])
            S.dma("pool", yT_d[2].rearrange("(c p) t -> p c t", p=128)[:, :, t0:t0 + 128], yT[:], reads=[yT])
        S.barrier()


def phase_merge_ffn(nc, S, PSB, ident, eps_t, l, b, NB, CTX, T, last, stream, tiles, seg_bounds, norm_mod_T, Acol2, Bcol2,
                    cwf, modrow_d, wb_in, wb_br, wb_o, wb_up, wb_dn, hT_d, h2T_d, yT_d):
    with contextlib.ExitStack() as ES:
        cnt = [0]

        def esb(name, shape, dt=F32):
            cnt[0] += 1
            return Tl(ES.enter_context(nc.sbuf_tensor(f"m{name}_{l}_{b}_{cnt[0]}", list(shape), dt)))
        wg = esb("wg", (128, 8, 3072), BF16)
        wbr = esb("wbr", (128, 3, 4, D), BF16)
        wo = esb("wo", (128, 8, D), BF16)
        for kc in range(8):
            S.dma("sp", wg[:, kc, :], wb_in[kc * 128:(kc + 1) * 128, O_GATE:O_GATE + 3072], writes=[wg])
            S.dma("sp", wo[:, kc, :], wb_o[kc * 128:(kc + 1) * 128, :], writes=[wo])
        for br in range(3):
            S.dma("sp", wbr[:, br, :, :], wb_br[br].rearrange("(c p) n -> p c n", p=128), writes=[wbr])
        gtb = esb("gtb", (128, 2, D))
        S.dma("sp", gtb[:, 0, :], modrow_d[0, b].partition_broadcast(128), writes=[gtb])
        S.dma("sp", gtb[:, 1, :], modrow_d[0, NB].partition_broadcast(128), writes=[gtb])
        hT_p = Rot([esb("hT", (128, 8, 512), BF16) for _ in range(1)])
        yb_p = Rot([esb("yb", (128, 3, 4, 512), BF16) for _ in range(1)])
        yT_p = Rot([esb("yT", (128, 8, 512), BF16) for _ in range(1)])
        sg_p = Rot([esb("sg", (128, 512)) for _ in range(4)])
        ac_p = Rot([esb("ac", (128, 512)) for _ in range(3)])
        xt_p = Rot([esb("xt", (128, 4, D)) for _ in range(1)])
        xn_p = Rot([esb("xn", (128, 4, D)) for _ in range(1)])
        junk = Rot([esb("junk", (128, D)) for _ in range(1)])
        stat = Rot([esb("stat", (128, 8)) for _ in range(6)])
        h2_p = Rot([esb("h2", (128, 8, 512), BF16) for _ in range(1)])
        psr = Rot(PSB)
        for (t0, n) in tiles(512, lat_only=last):
            nj = n // 128
            ri = NB if t0 < CTX else b
            gi = 1 if t0 < CTX else 0
            hT = hT_p.get()
            S.dma("sp", hT[:, :, 0:n], hT_d.rearrange("(kc p) t -> p kc t", p=128)[:, :, t0:t0 + n], writes=[hT])
            yb = yb_p.get()
            for br in range(3):
                S.dma("sp", yb[:, br, :, 0:n], yT_d[br].rearrange("(c p) t -> p c t", p=128)[:, :, t0:t0 + n], writes=[yb])
            xt = xt_p.get()
            S.dma("sp", xt[:, 0:nj, :], stream(b, t0, n).rearrange("(j p) d -> p j d", p=128), writes=[xt])
            yT = yT_p.get()
            for fc in range(8):
                acc_t = ac_p.get()
                for br in range(3):
                    pg = psr.get()
                    for kc in range(8):
                        S.op("pe", lambda e, kc=kc, pg=pg, br=br, fc=fc: e.matmul(
                            pg[:, 0:n], lhsT=wg[:, kc, br * D + fc * 128:br * D + (fc + 1) * 128], rhs=hT[:, kc, 0:n],
                            start=(kc == 0), stop=(kc == 7)), [wg, hT], [pg])
                    sg = sg_p.get()
                    S.op("act", lambda e, pg=pg, sg=sg: e.activation(out=sg[:, 0:n], in_=pg[:, 0:n], func=AF.Sigmoid),
                         [pg], [sg])
                    pb = psr.get()
                    for kc in range(4):
                        S.op("pe", lambda e, kc=kc, pb=pb, br=br, fc=fc: e.matmul(
                            pb[:, 0:n], lhsT=wbr[:, br, kc, fc * 128:(fc + 1) * 128], rhs=yb[:, br, kc, 0:n],
                            start=(kc == 0), stop=(kc == 3)), [wbr, yb], [pb])
                    if br == 0:
                        S.op("dve", lambda e, sg=sg, pb=pb, acc_t=acc_t: e.tensor_tensor(
                            out=acc_t[:, 0:n], in0=sg[:, 0:n], in1=pb[:, 0:n], op=ALU.mult), [sg, pb], [acc_t])
                    else:
                        S.op("dve", lambda e, sg=sg, pb=pb: e.tensor_tensor(out=sg[:, 0:n], in0=sg[:, 0:n], in1=pb[:, 0:n],
                                                                            op=ALU.mult), [sg, pb], [sg])
                        if br == 1:
                            S.op("pool", lambda e, sg=sg, acc_t=acc_t: e.tensor_tensor(
                                out=acc_t[:, 0:n], in0=acc_t[:, 0:n], in1=sg[:, 0:n], op=ALU.add), [sg, acc_t], [acc_t])
                        else:
                            S.op("pool", lambda e, sg=sg, acc_t=acc_t, fc=fc: e.tensor_tensor(
                                out=yT[:, fc, 0:n], in0=acc_t[:, 0:n], in1=sg[:, 0:n], op=ALU.add), [sg, acc_t], [yT])
            for j in range(nj):
                for hf in range(2):
                    po = psr.get()
                    for kc in range(8):
                        S.op("pe", lambda e, kc=kc, po=po, j=j, hf=hf: e.matmul(
                            po[:], lhsT=yT[:, kc, j * 128:(j + 1) * 128], rhs=wo[:, kc, hf * 512:(hf + 1) * 512],
                            start=(kc == 0), stop=(kc == 7)), [yT, wo], [po])
                    tm = sg_p.get()
                    S.op("dve", lambda e, po=po, tm=tm, hf=hf: e.tensor_tensor(
                        out=tm[:], in0=po[:], in1=gtb[:, gi, hf * 512:(hf + 1) * 512], op=ALU.mult), [po, gtb], [tm])
                    S.op("pool", lambda e, tm=tm, j=j, hf=hf: e.tensor_tensor(
                        out=xt[:, j, hf * 512:(hf + 1) * 512], in0=xt[:, j, hf * 512:(hf + 1) * 512], in1=tm[:],
                        op=ALU.add), [tm, xt], [xt])
            S.dma("pool", stream(b, t0, n).rearrange("(j p) d -> p j d", p=128), xt[:, 0:nj, :], reads=[xt])
            h2 = h2_p.get()
            norm_mod_T(xt, nj, n, Acol2, Bcol2, ri, h2, (junk, stat, xn_p, psr))
            S.dma("pool", h2T_d.rearrange("(kc p) t -> p kc t", p=128)[:, :, t0:t0 + n], h2[:, :, 0:n], reads=[h2])
        S.barrier()
    NT = 256
    with contextlib.ExitStack() as ES:
        cnt = [0]

        def esb(name, shape, dt=F32):
            cnt[0] += 1
            return Tl(ES.enter_context(nc.sbuf_tensor(f"f{name}_{l}_{b}_{cnt[0]}", list(shape), dt)))
        wu = esb("wu", (128, 8, 2 * D_FF), BF16)
        wd = esb("wd", (128, 22, D), BF16)
        for kc in range(8):
            S.dma("sp", wu[:, kc, :], wb_up[kc * 128:(kc + 1) * 128, :], writes=[wu])
        for c0 in range(0, 22, 2):
            S.dma("sp", wd[:, c0:c0 + 2, :], wb_dn[c0 * 128:(c0 + 2) * 128, :].rearrange("(c p) n -> p c n", p=128),
                  writes=[wd])
        gtb = esb("gtb", (128, 2, D))
        S.dma("sp", gtb[:, 0, :], modrow_d[1, b].partition_broadcast(128), writes=[gtb])
        S.dma("sp", gtb[:, 1, :], modrow_d[1, NB].partition_broadcast(128), writes=[gtb])
        h2_p = Rot([esb("h2", (128, 8, NT + 2), BF16) for _ in range(2)])
        aT_p = Rot([esb("aT", (128, 22, NT), BF16) for _ in range(1)])
        cg_p = Rot([esb("cg", (128, NT)) for _ in range(4)])
        xt_p = Rot([esb("xt", (128, 2, D)) for _ in range(1)])
        tm_p = Rot([esb("tm", (128, 512)) for _ in range(3)])
        psr = Rot(PSB)
        for (t0, n) in tiles(NT, lat_only=last):
            nj = n // 128
            gi = 1 if t0 < CTX else 0
            s0, s1 = seg_bounds(t0)
            lo = max(t0 - 1, s0)
            hi = min(t0 + n + 1, s1)
            h2 = h2_p.get()
            if lo != t0 - 1 or hi != t0 + n + 1:
                S.op("pool", lambda e, h2=h2: e.memset(h2[:], 0.0), [], [h2])
            S.dma("sp", h2[:, :, lo - (t0 - 1):hi - (t0 - 1)], h2T_d.rearrange("(kc p) t -> p kc t", p=128)[:, :, lo:hi],
                  writes=[h2])
            xt = xt_p.get()
            S.dma("sp", xt[:, 0:nj, :], stream(b, t0, n).rearrange("(j p) d -> p j d", p=128), writes=[xt])
            aT = aT_p.get()
            for cc in range(22):
                cgv = []
                for gv in range(2):
                    col = gv * D_FF + cc * 128
                    wi = gv * 22 + cc
                    pu = psr.get()
                    for kc in range(8):
                        S.op("pe", lambda e, kc=kc, pu=pu, col=col: e.matmul(
                            pu[:, 0:n + 2], lhsT=wu[:, kc, col:col + 128], rhs=h2[:, kc, 0:n + 2], start=(kc == 0),
                            stop=(kc == 7)), [wu, h2], [pu])
                    cg = cg_p.get()
                    S.op("act", lambda e, pu=pu, cg=cg, wi=wi: e.activation(out=cg[:, 0:n], in_=pu[:, 0:n],
                                                                            func=AF.Identity, scale=cwf[:, wi, 0:1]),
                         [pu, cwf], [cg])
                    S.op("dve", lambda e, pu=pu, cg=cg, wi=wi: e.scalar_tensor_tensor(
                        out=cg[:, 0:n], in0=pu[:, 1:n + 1], scalar=cwf[:, wi, 1:2], in1=cg[:, 0:n], op0=ALU.mult,
                        op1=ALU.add), [pu, cwf, cg], [cg])
                    S.op("dve", lambda e, pu=pu, cg=cg, wi=wi: e.scalar_tensor_tensor(
                        out=cg[:, 0:n], in0=pu[:, 2:n + 2], scalar=cwf[:, wi, 2:3], in1=cg[:, 0:n], op0=ALU.mult,
                        op1=ALU.add), [pu, cwf, cg], [cg])
                    cgv.append(cg)
                S.op("act", lambda e, cg=cgv[0]: e.activation(out=cg[:, 0:n], in_=cg[:, 0:n], func=AF.Silu),
                     [cgv[0]], [cgv[0]])
                S.op("pool", lambda e, cc=cc, a=cgv[0], v=cgv[1]: e.tensor_tensor(out=aT[:, cc, 0:n], in0=a[:, 0:n],
                                                                                  in1=v[:, 0:n], op=ALU.mult),
                     [cgv[0], cgv[1]], [aT])
            for j in range(nj):
                for hf in range(2):
                    po = psr.get()
                    for cc in range(22):
                        S.op("pe", lambda e, cc=cc, po=po, j=j, hf=hf: e.matmul(
                            po[:], lhsT=aT[:, cc, j * 128:(j + 1) * 128], rhs=wd[:, cc, hf * 512:(hf + 1) * 512],
                            start=(cc == 0), stop=(cc == 21)), [aT, wd], [po])
                    tm = tm_p.get()
                    S.op("dve", lambda e, po=po, tm=tm, hf=hf: e.tensor_tensor(
                        out=tm[:], in0=po[:], in1=gtb[:, gi, hf * 512:(hf + 1) * 512], op=ALU.mult), [po, gtb], [tm])
                    S.op("dve", lambda e, tm=tm, j=j, hf=hf: e.tensor_tensor(
                        out=xt[:, j, hf * 512:(hf + 1) * 512], in0=xt[:, j, hf * 512:(hf + 1) * 512], in1=tm[:],
                        op=ALU.add), [tm, xt], [xt])
            S.dma("pool", stream(b, t0, n).rearrange("(j p) d -> p j d", p=128), xt[:, 0:nj, :], reads=[xt])
        S.barrier()


_CACHE = {}


def run(inputs, n_cores, NB, SEQ, CTX, DEPTH):
    key = (NB, SEQ, CTX, DEPTH)
    if key not in _CACHE:
        _CACHE[key] = build(NB, SEQ, CTX, DEPTH)[0]
    nc = _CACHE[key]
    consts = host_consts(SEQ, CTX)
    shared = {k: np.ascontiguousarray(np.asarray(v, dtype=np.float32)) for k, v in inputs.items()
              if k not in ("x", "c", "ctx")}
    in_maps = []
    for i in range(n_cores):
        m = dict(shared)
        m.update(consts)
        for k in ("x", "c", "ctx"):
            m[k] = np.ascontiguousarray(np.asarray(inputs[k][i * NB:(i + 1) * NB], dtype=np.float32))
        in_maps.append(m)
    res = run_bass_kernel_spmd(nc, in_maps, core_ids=list(range(n_cores)))
    return np.concatenate([np.asarray(r["out"]) for r in res.results], axis=0).astype(np.float32)


def kernel(**inputs):
    return run(inputs, 8, 2, 4096, 256, 2)
```

```python
import math
import contextlib
import numpy as np
import concourse.bass as bass
import concourse.mybir as mybir
from concourse.bass_utils import run_bass_kernel_spmd

F32 = mybir.dt.float32
BF16 = mybir.dt.bfloat16
AF = mybir.ActivationFunctionType
ALU = mybir.AluOpType
AX = mybir.AxisListType

D = 1024
GRID_W = 64
A_W = 512
B_H = 4
C_H = 4
D_FF = 2816
EPS = 1e-6
D_IN = 7696
O_U, O_SV, O_BQ, O_BK, O_BV, O_DQ, O_DK, O_DV, O_DZ, O_BETA, O_GATE = (
    0, 512, 1024, 1536, 2048, 2560, 3072, 3584, 4096, 4608, 4624)
EPOCH = 30000
GELU_C = 2.0 * math.sqrt(2.0 / math.pi)


class Tl:
    __slots__ = ("t", "lw", "rd")

    def __init__(self, t):
        self.t = t
        self.lw = None
        self.rd = {}

    def __getitem__(self, idx):
        return self.t[idx]


class Sched:
    def __init__(self, nc, n_dma_slots=8):
        self.nc = nc
        self.eng = {"pe": nc.tensor, "act": nc.scalar, "dve": nc.vector, "pool": nc.gpsimd, "sp": nc.sync}
        self.cnt = {e: 0 for e in self.eng}
        self.sems = {e: [] for e in self.eng}
        self.seen = {e: {} for e in self.eng}
        self.dq = {}
        self.n_dma_slots = n_dma_slots
        self.ninst = 0

    def _esem(self, e, idx):
        ep = (idx - 1) // EPOCH
        while len(self.sems[e]) <= ep:
            self.sems[e].append(self.nc.alloc_semaphore(f"s_{e}_{len(self.sems[e])}"))
        return self.sems[e][ep], (idx - 1) % EPOCH + 1

    def _wait(self, e, tok):
        if tok is None:
            return
        if tok[0] == "eng":
            _, f, idx = tok
            if f == e and e == "pe":
                return
            sem, val = self._esem(f, idx)
            key = (f, (idx - 1) // EPOCH)
        else:
            _, sem, val, key = tok
        if self.seen[e].get(key, 0) >= val:
            return
        self.seen[e][key] = val
        self.eng[e].wait_ge(sem, val)
        self.ninst += 1

    def _deps(self, e, reads, writes):
        for t in reads:
            self._wait(e, t.lw)
        for t in writes:
            self._wait(e, t.lw)
            for tok in list(t.rd.values()):
                self._wait(e, tok)

    def _mark(self, tok, rkey, reads, writes):
        for t in reads:
            t.rd[rkey] = tok
        for t in writes:
            t.lw = tok
            t.rd = {}

    def op(self, e, fn, reads=(), writes=()):
        if e == "pool":
            e = "dve"
        self._deps(e, reads, writes)
        inst = fn(self.eng[e])
        self.cnt[e] += 1
        idx = self.cnt[e]
        sem, _ = self._esem(e, idx)
        inst.then_inc(sem, 1)
        self.ninst += 1
        self._mark(("eng", e, idx), e, reads, writes)

    def dma(self, q, out, in_, reads=(), writes=(), **kw):
        if q not in self.dq:
            self.dq[q] = {"slots": [[self.nc.alloc_semaphore(f"d_{q}_{i}"), 0]
                                    for i in range(self.n_dma_slots)], "i": 0}
        d = self.dq[q]
        si = d["i"] % self.n_dma_slots
        d["i"] += 1
        slot = d["slots"][si]
        self._deps(q, reads, writes)
        key = ("dma", q, si)
        if slot[1] > 0:
            self._wait(q, ("dma", slot[0], slot[1], key))
        inst = self.eng[q].dma_start(out=out, in_=in_, **kw)
        slot[1] += 16
        inst.then_inc(slot[0], 16)
        self.ninst += 1
        self._mark(("dma", slot[0], slot[1], key), key, reads, writes)

    def barrier(self):
        toks = []
        for f in self.eng:
            if self.cnt[f] > 0:
                toks.append(("eng", f, self.cnt[f]))
        for q, d in self.dq.items():
            for si, slot in enumerate(d["slots"]):
                if slot[1] > 0:
                    toks.append(("dma", slot[0], slot[1], ("dma", q, si)))
        for e in self.eng:
            for tok in toks:
                if tok[0] == "eng" and tok[1] == e:
                    continue
                self._wait(e, tok)


class Rot:
    def __init__(self, tiles):
        self.tiles = tiles
        self.i = 0

    def get(self):
        t = self.tiles[self.i % len(self.tiles)]
        self.i += 1
        return t


def host_consts(SEQ, CTX):
    T = CTX + SEQ
    c = {}
    c["c_ident"] = np.eye(128, dtype=np.float32)
    p = np.arange(128)
    c["c_blk64"] = (p[:, None] // 64 == p[None, :] // 64).astype(np.float32)
    R = np.zeros((64, 64), np.float32)
    for base in (0, 32):
        for i in range(16):
            R[base + i, base + 16 + i] = -1.0
            R[base + 16 + i, base + i] = 1.0
    R2 = np.zeros((128, 128), np.float32)
    R2[:64, :64] = R
    R2[64:, 64:] = R
    c["c_rotT"] = np.ascontiguousarray(R2.T)
    n_freq = 16
    inv_freq = (np.float32(10000.0) ** (-np.arange(n_freq, dtype=np.float32) / np.float32(n_freq))).astype(np.float32)
    rows = SEQ // GRID_W
    row = np.repeat(np.arange(rows, dtype=np.float32), GRID_W)
    col = np.tile(np.arange(GRID_W, dtype=np.float32), rows)
    ang_r = (row[:, None] * inv_freq).astype(np.float32)
    ang_c = (col[:, None] * inv_freq).astype(np.float32)
    cos64 = np.concatenate([np.cos(ang_r), np.cos(ang_r), np.cos(ang_c), np.cos(ang_c)], axis=1).astype(np.float32)
    sin64 = np.concatenate([np.sin(ang_r), np.sin(ang_r), np.sin(ang_c), np.sin(ang_c)], axis=1).astype(np.float32)
    cos = np.ones((128, T), np.float32)
    sin = np.zeros((128, T), np.float32)
    cos[:, CTX:] = np.concatenate([cos64, cos64], axis=1).T
    sin[:, CTX:] = np.concatenate([sin64, sin64], axis=1).T
    c["c_cos"] = cos
    c["c_sin"] = sin
    i = np.arange(64)
    lo = (i[:, None] >= i[None, :]).astype(np.float32)
    up = (i[:, None] <= i[None, :]).astype(np.float32)
    slo = (i[:, None] > i[None, :]).astype(np.float32)
    sup = (i[:, None] < i[None, :]).astype(np.float32)

    def blk8(f, b):
        return np.ascontiguousarray(np.stack([f] * 4 + [b] * 4, axis=1))
    g = np.zeros((64, 6, 8, 64), np.float32)
    g[:, 0] = blk8(up, lo)
    g[:, 1] = blk8(lo, up)
    g[:, 2] = blk8(slo, sup)
    g[:, 3] = blk8(up, lo)
    g[:, 4] = blk8(np.eye(64, dtype=np.float32), np.eye(64, dtype=np.float32))
    g[:, 5] = 1.0
    c["c_gdn"] = g
    return c


def build(NB, SEQ, CTX, DEPTH, dbg=None):
    T = CTX + SEQ
    nc = bass.Bass("TRN2", target_bir_lowering=False)
    S = Sched(nc)

    def din(name, shape):
        return nc.dram_tensor(name, list(shape), F32, kind="ExternalInput").ap()

    def dscr(name, shape, dt):
        return nc.dram_tensor(name, list(shape), dt, kind="Internal").ap()

    L = DEPTH
    x_in = din("x", (NB, SEQ, D))
    c_in = din("c", (NB, D))
    ctx_in = din("ctx", (NB, CTX, D))
    cctx_in = din("c_ctx", (D,))
    W = {}
    for name, shape in [("w_mod", (L, D, 6 * D)), ("b_mod", (L, 6 * D)), ("norm1_g", (L, D)), ("w_in", (L, D, D_IN)),
                        ("sgu_norm_g", (L, 4, 128)), ("sgu_w", (L, 4, 128, 128)), ("sgu_b", (L, 4, 128)),
                        ("w_a_br", (L, 512, D)), ("q_norm_g", (L, 64)), ("k_norm_g", (L, 64)),
                        ("lambda_q1", (L, 64)), ("lambda_k1", (L, 64)), ("lambda_q2", (L, 64)),
                        ("lambda_k2", (L, 64)), ("subln_g", (L, 128)), ("w_b_br", (L, 512, D)),
                        ("conv_qkv_w", (L, 3, 1536)), ("a_log", (L, 2, 4)), ("dt_bias", (L, 2, 4)),
                        ("gdn_norm_g", (L, 128)), ("w_c_br", (L, 512, D)), ("w_o", (L, D, D)),
                        ("norm2_g", (L, D)), ("w_up", (L, D, 2 * D_FF)), ("conv_ffn_w", (L, 3, 2 * D_FF)),
                        ("w_down", (L, D_FF, D))]:
        W[name] = din(name, shape)
    c_ident = din("c_ident", (128, 128))
    c_blk64 = din("c_blk64", (128, 128))
    c_rotT = din("c_rotT", (128, 128))
    c_cos = din("c_cos", (128, T))
    c_sin = din("c_sin", (128, T))
    c_gdn = din("c_gdn", (64, 6, 8, 64))
    out = nc.dram_tensor("out", [NB, SEQ, D], F32, kind="ExternalOutput").ap()

    cx = dscr("cx", (NB, CTX, D), F32)
    wb_in = dscr("wb_in", (D, D_IN), BF16)
    wb_br = dscr("wb_br", (3, 512, D), BF16)
    wb_o = dscr("wb_o", (D, D), BF16)
    wb_up = dscr("wb_up", (D, 2 * D_FF), BF16)
    wb_dn = dscr("wb_dn", (D_FF, D), BF16)
    modrow_d = dscr("modrow", (2, 4, D), F32)
    hT_d = dscr("hT", (D, T), BF16)
    h2T_d = dscr("h2T", (D, T), BF16)
    yT_d = dscr("yT", (3, 512, T), BF16)
    QT_d = dscr("QT", (512, T), BF16)
    KT_d = dscr("KT", (512, T), BF16)
    V_d = dscr("V", (T, 512), BF16)
    gpre_d = dscr("gpre", (1536, T), F32)
    gqT_d = dscr("gqT", (512, T), F32)
    gkT_d = dscr("gkT", (512, T), F32)
    gkt_d = dscr("gkt", (T, 512), F32)
    gvt_d = dscr("gvt", (T, 512), F32)
    zs_d = dscr("zs", (T, 512), F32)
    bg_d = dscr("bg", (T, 16), F32)
    of_d = dscr("of", (2, T, 512), F32)

    R4 = 4
    assert NB + 1 <= R4

    def sb(name, shape, dt=F32):
        return Tl(nc.alloc_sbuf_tensor(name, list(shape), dt))

    PSB = [Tl(nc.alloc_psum_tensor(f"ps{i}", [128, 512], F32)) for i in range(8)]
    ident = sb("ident", (128, 128))
    S.dma("sp", ident[:], c_ident, writes=[ident])
    eps_t = sb("eps_t", (128, 1))
    S.op("dve", lambda e: e.memset(eps_t[:], EPS), [], [eps_t])

    def stream(b, t0, n):
        if t0 < CTX:
            return cx[b, t0:t0 + n, :]
        return out[b, t0 - CTX:t0 - CTX + n, :]

    def tiles(nmax, lat_only=False):
        r = []
        if not lat_only:
            for t0 in range(0, CTX, nmax):
                r.append((t0, min(nmax, CTX - t0)))
        for t0 in range(CTX, T, nmax):
            r.append((t0, min(nmax, T - t0)))
        return r

    def seg_bounds(t0):
        return (0, CTX) if t0 < CTX else (CTX, T)

    for b in range(NB):
        for r0 in range(0, SEQ, 512):
            S.dma("sp", out[b, r0:r0 + 512, :], x_in[b, r0:r0 + 512, :])
        S.dma("sp", cx[b], ctx_in[b])

    def col_load(q, dst_tile, dst_ap, vec_ap, n):
        for c0 in range(0, n, 8):
            c1 = min(n, c0 + 8)
            S.dma(q, dst_ap[:, c0:c1], vec_ap[c0 * 128:c1 * 128].rearrange("(c p) -> p c", p=128),
                  writes=[dst_tile], allow_slow_non_contiguous=True)

    def gelu_ops(es_get, src_ap, src_tl, n, out_ap, out_tl):
        xs = es_get()
        t = es_get()
        S.op("act", lambda e: e.activation(out=xs[:, 0:n], in_=src_ap, func=AF.Identity), [src_tl], [xs])
        S.op("dve", lambda e: e.tensor_tensor(out=t[:, 0:n], in0=xs[:, 0:n], in1=xs[:, 0:n], op=ALU.mult), [xs], [t])
        S.op("dve", lambda e: e.tensor_scalar(out=t[:, 0:n], in0=t[:, 0:n], scalar1=0.044715, scalar2=1.0,
                                              op0=ALU.mult, op1=ALU.add), [t], [t])
        S.op("dve", lambda e: e.tensor_tensor(out=t[:, 0:n], in0=t[:, 0:n], in1=xs[:, 0:n], op=ALU.mult), [t, xs], [t])
        S.op("act", lambda e: e.activation(out=t[:, 0:n], in_=t[:, 0:n], func=AF.Sigmoid, scale=GELU_C), [t], [t])
        S.op("dve", lambda e: e.tensor_tensor(out=out_ap, in0=t[:, 0:n], in1=xs[:, 0:n], op=ALU.mult), [t, xs], [out_tl])

    def rstd_ops(ssq_tl, ssq_ap, out_tl, out_ap, inv_n):
        S.op("act", lambda e: e.activation(out=out_ap, in_=ssq_ap, func=AF.Sqrt, bias=eps_t[0:out_ap.shape[0], :],
                                           scale=inv_n), [ssq_tl, eps_t], [out_tl])
        S.op("dve", lambda e: e.reciprocal(out=out_ap, in_=out_ap), [out_tl], [out_tl])

    def norm_mod_T(xt, nj, n, Acol, Bcol, ri, hT, pools):
        junk, stat, xn_pool, psr = pools
        ssq = stat.get()
        for j in range(nj):
            jk = junk.get()
            S.op("act", lambda e, j=j, jk=jk: e.activation(out=jk[:], in_=xt[:, j, :], func=AF.Square,
                                                           accum_out=ssq[:, j:j + 1]), [xt], [jk, ssq])
        rs = stat.get()
        rstd_ops(ssq, ssq[:, 0:nj], rs, rs[:, 0:nj], 1.0 / D)
        xn = xn_pool.get()
        for j in range(nj):
            S.op("act", lambda e, j=j: e.activation(out=xn[:, j, :], in_=xt[:, j, :], func=AF.Identity,
                                                    scale=rs[:, j:j + 1]), [xt, rs], [xn])
        for kc in range(8):
            ps = psr.get()
            for j in range(nj):
                S.op("pe", lambda e, j=j, kc=kc, ps=ps: e.transpose(out=ps[:, j * 128:(j + 1) * 128],
                                                                     in_=xn[:, j, kc * 128:(kc + 1) * 128],
                                                                     identity=ident[:]), [xn, ident], [ps])
            S.op("act", lambda e, kc=kc, ps=ps: e.activation(out=hT[:, kc, 0:n], in_=ps[:, 0:n], func=AF.Identity,
                                                             scale=Acol[:, kc, ri:ri + 1],
                                                             bias=Bcol[:, kc, ri:ri + 1]), [ps, Acol, Bcol], [hT])

    for l in range(L):
        last = (l == L - 1)
        lam_init = 0.8 - 0.6 * math.exp(-0.3 * l)
        S.barrier()
        for r0 in range(0, D, 128):
            S.dma("pool", wb_in[r0:r0 + 128, :], W["w_in"][l, r0:r0 + 128, :])
            S.dma("pool", wb_up[r0:r0 + 128, :], W["w_up"][l, r0:r0 + 128, :])
            S.dma("pool", wb_o[r0:r0 + 128, :], W["w_o"][l, r0:r0 + 128, :])
        for r0 in range(0, D_FF, 128):
            S.dma("pool", wb_dn[r0:r0 + 128, :], W["w_down"][l, r0:r0 + 128, :])
        for bi, nm in enumerate(("w_a_br", "w_b_br", "w_c_br")):
            for r0 in range(0, 512, 128):
                S.dma("pool", wb_br[bi, r0:r0 + 128, :], W[nm][l, r0:r0 + 128, :])

        with contextlib.ExitStack() as LS:
            def lsb(name, shape, dt=F32):
                return Tl(LS.enter_context(nc.sbuf_tensor(f"{name}_{l}", list(shape), dt)))

            Acol1 = lsb("Acol1", (128, 8, R4))
            Bcol1 = lsb("Bcol1", (128, 8, R4))
            Acol2 = lsb("Acol2", (128, 8, R4))
            Bcol2 = lsb("Bcol2", (128, 8, R4))
            lam_t = lsb("lam", (128, 2))
            wsT = lsb("wsT", (128, 4, 128), BF16)
            bsb = lsb("bsb", (128, 512))
            sgng = lsb("sgng", (128, 512))
            gq_c = lsb("gq_c", (128, 1))
            gk_c = lsb("gk_c", (128, 1))
            subg = lsb("subg", (128, 128))
            gdng = lsb("gdng", (128, 512))
            cwq = lsb("cwq", (128, 12, 3))
            cwf = lsb("cwf", (128, 44, 3))
            alog_b = lsb("alog_b", (128, 8))
            dtb_b = lsb("dtb_b", (128, 8))
            blk64 = lsb("blk64", (128, 128))
            rotT = lsb("rotT", (128, 128), BF16)

            with contextlib.ExitStack() as ES:
                def esb(name, shape, dt=F32):
                    return Tl(ES.enter_context(nc.sbuf_tensor(f"{name}_{l}", list(shape), dt)))
                S.dma("sp", blk64[:], c_blk64, writes=[blk64])
                rt32 = esb("rt32", (128, 128))
                S.dma("sp", rt32[:], c_rotT, writes=[rt32])
                S.op("dve", lambda e: e.tensor_copy(out=rotT[:], in_=rt32[:]), [rt32], [rotT])
                S.dma("sp", bsb[:], W["sgu_b"][l].rearrange("g i -> (g i)").partition_broadcast(128), writes=[bsb])
                S.dma("sp", sgng[:], W["sgu_norm_g"][l].rearrange("g i -> (g i)").partition_broadcast(128),
                      writes=[sgng])
                S.dma("sp", subg[:], W["subln_g"][l].partition_broadcast(128), writes=[subg])
                for h in range(4):
                    S.dma("sp", gdng[:, h * 128:(h + 1) * 128], W["gdn_norm_g"][l].partition_broadcast(128),
                          writes=[gdng])
                S.dma("sp", alog_b[:], W["a_log"][l].rearrange("a b -> (a b)").partition_broadcast(128),
                      writes=[alog_b])
                S.dma("sp", dtb_b[:], W["dt_bias"][l].rearrange("a b -> (a b)").partition_broadcast(128),
                      writes=[dtb_b])
                S.op("act", lambda e: e.activation(out=alog_b[:], in_=alog_b[:], func=AF.Exp), [alog_b], [alog_b])
                S.op("dve", lambda e: e.tensor_scalar(out=alog_b[:], in0=alog_b[:], scalar1=-1.0, scalar2=None,
                                                      op0=ALU.mult), [alog_b], [alog_b])
                for half in range(2):
                    S.dma("sp", gq_c[half * 64:(half + 1) * 64, :], W["q_norm_g"][l].rearrange("(p o) -> p o", o=1),
                          writes=[gq_c])
                    S.dma("sp", gk_c[half * 64:(half + 1) * 64, :], W["k_norm_g"][l].rearrange("(p o) -> p o", o=1),
                          writes=[gk_c])
                for k in range(3):
                    col_load("sp", cwq, cwq[:, :, k], W["conv_qkv_w"][l, k], 12)
                    col_load("sp", cwf, cwf[:, :, k], W["conv_ffn_w"][l, k], 44)
                sw = esb("sw", (128, 4, 128))
                S.dma("sp", sw[:], W["sgu_w"][l].rearrange("g i j -> i g j"), writes=[sw])
                ps = PSB[0]
                for g in range(4):
                    S.op("pe", lambda e, g=g: e.transpose(out=ps[:, g * 128:(g + 1) * 128], in_=sw[:, g, :],
                                                          identity=ident[:]), [sw, ident], [ps])
                S.op("dve", lambda e: e.tensor_copy(out=wsT[:].rearrange("p g i -> p (g i)"), in_=ps[:]), [ps], [wsT])
                lv = esb("lv", (128, 4, 64))
                for i, nm in enumerate(("lambda_q1", "lambda_k1", "lambda_q2", "lambda_k2")):
                    S.dma("sp", lv[:, i, :], W[nm][l].partition_broadcast(128), writes=[lv])
                lp = esb("lp", (128, 2, 64))
                S.op("dve", lambda e: e.tensor_tensor(out=lp[:, 0, :], in0=lv[:, 0, :], in1=lv[:, 1, :], op=ALU.mult),
                     [lv], [lp])
                S.op("dve", lambda e: e.tensor_tensor(out=lp[:, 1, :], in0=lv[:, 2, :], in1=lv[:, 3, :], op=ALU.mult),
                     [lv], [lp])
                ls_ = esb("ls", (128, 2))
                S.op("dve", lambda e: e.tensor_reduce(out=ls_[:], in_=lp[:], axis=AX.X, op=ALU.add), [lp], [ls_])
                S.op("act", lambda e: e.activation(out=ls_[:], in_=ls_[:], func=AF.Exp), [ls_], [ls_])
                S.op("dve", lambda e: e.tensor_tensor(out=lam_t[:, 0:1], in0=ls_[:, 1:2], in1=ls_[:, 0:1],
                                                      op=ALU.subtract), [ls_], [lam_t])
                S.op("dve", lambda e: e.tensor_scalar(out=lam_t[:, 0:1], in0=lam_t[:, 0:1], scalar1=-lam_init,
                                                      scalar2=None, op0=ALU.add), [lam_t], [lam_t])
                scT = esb("scT", (128, 8, R4))
                S.op("dve", lambda e: e.memset(scT[:], 0.0), [], [scT])
                for r in range(NB + 1):
                    src = c_in[r] if r < NB else cctx_in
                    S.dma("sp", scT[:, :, r], src.rearrange("(c p) -> p c", p=128), writes=[scT],
                          allow_slow_non_contiguous=True)
                S.op("act", lambda e: e.activation(out=scT[:], in_=scT[:], func=AF.Silu), [scT], [scT])
                bmc = esb("bmc", (128, 48))
                col_load("sp", bmc, bmc[:], W["b_mod"][l], 48)
                g1c = esb("g1c", (128, 8))
                g2c = esb("g2c", (128, 8))
                col_load("sp", g1c, g1c[:], W["norm1_g"][l], 8)
                col_load("sp", g2c, g2c[:], W["norm2_g"][l], 8)
                bmr = esb("bmr", (R4, 6 * D))
                S.dma("sp", bmr[:], W["b_mod"][l].partition_broadcast(R4), writes=[bmr])
                modT = esb("modT", (128, 6, 8, R4))
                mrow = esb("mrow", (R4, 2, D))
                wmp = Rot([esb(f"wm{i}", (128, 8, 512)) for i in range(2)])
                wmv = W["w_mod"][l].rearrange("(kc p) n -> p kc n", p=128)
                for cb in range(12):
                    seg = cb // 2
                    wm = wmp.get()
                    S.dma("sp", wm[:], wmv[:, :, cb * 512:(cb + 1) * 512], writes=[wm])
                    if seg in (2, 5):
                        ps = PSB[(cb % 2) + 1]
                        for kc in range(8):
                            S.op("pe", lambda e, kc=kc, ps=ps, wm=wm: e.matmul(ps[0:R4, :], lhsT=scT[:, kc, :],
                                                                                rhs=wm[:, kc, :], start=(kc == 0),
                                                                                stop=(kc == 7)), [scT, wm], [ps])
                        gi = 0 if seg == 2 else 1
                        hf = cb % 2
                        S.op("dve", lambda e, ps=ps, gi=gi, hf=hf, cb=cb: e.tensor_tensor(
                            out=mrow[:, gi, hf * 512:(hf + 1) * 512], in0=ps[0:R4, :],
                            in1=bmr[:, cb * 512:(cb + 1) * 512], op=ALU.add), [ps, bmr], [mrow])
                    else:
                        ps = PSB[(cb % 2) + 1]
                        for c4 in range(4):
                            for kc in range(8):
                                S.op("pe", lambda e, kc=kc, c4=c4, ps=ps, wm=wm: e.matmul(
                                    ps[:, c4 * R4:(c4 + 1) * R4], lhsT=wm[:, kc, c4 * 128:(c4 + 1) * 128],
                                    rhs=scT[:, kc, :], start=(kc == 0), stop=(kc == 7)), [scT, wm], [ps])
                        for c4 in range(4):
                            fc = cb * 4 + c4
                            S.op("dve", lambda e, ps=ps, c4=c4, fc=fc, seg=seg: e.tensor_scalar(
                                out=modT[:, seg, fc % 8, :], in0=ps[:, c4 * R4:(c4 + 1) * R4],
                                scalar1=bmc[:, fc:fc + 1], scalar2=None, op0=ALU.add), [ps, bmc], [modT])
                for kc in range(8):
                    S.op("dve", lambda e, kc=kc: e.tensor_scalar(out=Acol1[:, kc, :], in0=modT[:, 1, kc, :], scalar1=1.0,
                                                                 scalar2=g1c[:, kc:kc + 1], op0=ALU.add, op1=ALU.mult),
                         [modT, g1c], [Acol1])
                    S.op("dve", lambda e, kc=kc: e.tensor_scalar(out=Acol2[:, kc, :], in0=modT[:, 4, kc, :], scalar1=1.0,
                                                                 scalar2=g2c[:, kc:kc + 1], op0=ALU.add, op1=ALU.mult),
                         [modT, g2c], [Acol2])
                S.op("dve", lambda e: e.tensor_copy(out=Bcol1[:], in_=modT[:, 0]), [modT], [Bcol1])
                S.op("dve", lambda e: e.tensor_copy(out=Bcol2[:], in_=modT[:, 3]), [modT], [Bcol2])
                S.dma("sp", modrow_d.rearrange("g r d -> r g d"), mrow[:], reads=[mrow])
                S.barrier()
            if dbg == "setup":
                break

            for b in range(NB):
                S.barrier()
                with contextlib.ExitStack() as ES:
                    cnt = [0]

                    def esb(name, shape, dt=F32):
                        cnt[0] += 1
                        return Tl(ES.enter_context(nc.sbuf_tensor(f"{name}_{l}_{b}_{cnt[0]}", list(shape), dt)))
                    xt_p = Rot([esb("xt", (128, 4, D)) for _ in range(1)])
                    xn_p = Rot([esb("xn", (128, 4, D)) for _ in range(1)])
                    junk = Rot([esb("junk", (128, D)) for _ in range(2)])
                    stat = Rot([esb("stat", (128, 8)) for _ in range(8)])
                    hT_p = Rot([esb("hT", (128, 8, 512), BF16) for _ in range(2)])
                    wt_p = Rot([esb("wt", (128, 8, 512), BF16) for _ in range(3)])
                    tmp_p = Rot([esb("tmp", (128, 512)) for _ in range(6)])
                    tb_p = Rot([esb("tb", (128, 512), BF16) for _ in range(4)])
                    uT_p = Rot([esb("uT", (128, 4, 512), BF16) for _ in range(2)])
                    ya_p = Rot([esb("ya", (128, 4, 512), BF16) for _ in range(2)])
                    qk_p = Rot([esb("qk", (128, 4, 512), BF16) for _ in range(3)])
                    cs_p = Rot([esb("cs", (128, 2, 512)) for _ in range(2)])
                    vt_p = Rot([esb("vt", (128, 4, 512), BF16) for _ in range(2)])
                    zt_p = Rot([esb("zt", (128, 4, 512)) for _ in range(1)])
                    gp_p = Rot([esb("gp", (128, 4, 512)) for _ in range(2)])
                    bg_p = Rot([esb("bgt", (128, 4, 16)) for _ in range(2)])
                    sm_p = Rot([esb("sm", (128, 16)) for _ in range(6)])
                    psr = Rot(PSB)
                    wv_in = wb_in.rearrange("(kc p) n -> p kc n", p=128)

                    for (t0, n) in tiles(512):
                        nj = n // 128
                        ri = NB if t0 < CTX else b
                        xt = xt_p.get()
                        S.dma("sp", xt[:, 0:nj, :], stream(b, t0, n).rearrange("(j p) d -> p j d", p=128), writes=[xt])
                        hT = hT_p.get()
                        norm_mod_T(xt, nj, n, Acol1, Bcol1, ri, hT, (junk, stat, xn_p, psr))
                        S.dma("pool", hT_d.rearrange("(kc p) t -> p kc t", p=128)[:, :, t0:t0 + n], hT[:, :, 0:n],
                              reads=[hT])
                        cs = cs_p.get()
                        S.dma("sp", cs[:, 0, 0:n], c_cos[:, t0:t0 + n], writes=[cs])
                        S.dma("sp", cs[:, 1, 0:n], c_sin[:, t0:t0 + n], writes=[cs])

                        def fm_group(col0):
                            wt = wt_p.get()
                            S.dma("sp", wt[:], wv_in[:, :, col0:col0 + 512], writes=[wt])
                            for cc in range(4):
                                ps = psr.get()
                                for kc in range(8):
                                    S.op("pe", lambda e, kc=kc, cc=cc, ps=ps, wt=wt: e.matmul(
                                        ps[:, 0:n], lhsT=wt[:, kc, cc * 128:(cc + 1) * 128], rhs=hT[:, kc, 0:n],
                                        start=(kc == 0), stop=(kc == 7)), [wt, hT], [ps])
                                yield cc, ps

                        def tm_group(col0, ncol=512):
                            wt = wt_p.get()
                            S.dma("sp", wt[:, :, 0:ncol], wv_in[:, :, col0:col0 + ncol], writes=[wt])
                            for j in range(nj):
                                ps = psr.get()
                                for kc in range(8):
                                    S.op("pe", lambda e, kc=kc, j=j, ps=ps, wt=wt: e.matmul(
                                        ps[:, 0:ncol], lhsT=hT[:, kc, j * 128:(j + 1) * 128], rhs=wt[:, kc, 0:ncol],
                                        start=(kc == 0), stop=(kc == 7)), [wt, hT], [ps])
                                yield j, ps

                        uT = uT_p.get()
                        for cc, ps in fm_group(O_U):
                            gelu_ops(tmp_p.get, ps[:, 0:n], ps, n, uT[:, cc, 0:n], uT)
                        ya = ya_p.get()
                        for j, ps in tm_group(O_SV):
                            gv = tmp_p.get()
                            gelu_ops(tmp_p.get, ps[:], ps, 512, gv[:], gv)
                            sq = tmp_p.get()
                            S.op("dve", lambda e, gv=gv, sq=sq: e.tensor_tensor(out=sq[:], in0=gv[:], in1=gv[:],
                                                                                op=ALU.mult), [gv], [sq])
                            ssq = stat.get()
                            S.op("dve", lambda e, sq=sq, ssq=ssq: e.tensor_reduce(
                                out=ssq[:, 0:4], in_=sq[:].rearrange("p (g c) -> p g c", g=4), axis=AX.X, op=ALU.add),
                                [sq], [ssq])
                            rs = stat.get()
                            rstd_ops(ssq, ssq[:, 0:4], rs, rs[:, 0:4], 1.0 / 128)
                            S.op("dve", lambda e, gv=gv, rs=rs: e.tensor_tensor(
                                out=gv[:].rearrange("p (g c) -> p g c", g=4),
                                in0=gv[:].rearrange("p (g c) -> p g c", g=4),
                                in1=rs[:, 0:4].unsqueeze(2).to_broadcast([128, 4, 128]), op=ALU.mult), [gv, rs], [gv])
                            vn = tb_p.get()
                            S.op("dve", lambda e, gv=gv, vn=vn: e.tensor_tensor(out=vn[:], in0=gv[:], in1=sgng[:],
                                                                                op=ALU.mult), [gv, sgng], [vn])
                            pm = psr.get()
                            for g in range(4):
                                S.op("pe", lambda e, g=g, pm=pm, vn=vn: e.matmul(
                                    pm[:, g * 128:(g + 1) * 128], lhsT=vn[:, g * 128:(g + 1) * 128], rhs=wsT[:, g, :],
                                    start=True, stop=True), [vn, wsT], [pm])
                            mx = tmp_p.get()
                            S.op("dve", lambda e, pm=pm, mx=mx: e.tensor_tensor(out=mx[:], in0=pm[:], in1=bsb[:],
                                                                                op=ALU.add), [pm, bsb], [mx])
                            S.op("dve", lambda e, mx=mx, j=j: e.tensor_tensor(
                                out=ya[:, :, j * 128:(j + 1) * 128], in0=mx[:].rearrange("p (g i) -> p g i", g=4),
                                in1=uT[:, :, j * 128:(j + 1) * 128], op=ALU.mult), [mx, uT], [ya])
                        S.dma("pool", yT_d[0].rearrange("(c p) t -> p c t", p=128)[:, :, t0:t0 + n], ya[:, :, 0:n],
                              reads=[ya])
                        for (col0, gcol, dst) in ((O_BQ, gq_c, QT_d), (O_BK, gk_c, KT_d)):
                            qk = qk_p.get()
                            for cc, ps in fm_group(col0):
                                sq = tmp_p.get()
                                S.op("act", lambda e, ps=ps, sq=sq: e.activation(out=sq[:, 0:n], in_=ps[:, 0:n],
                                                                                 func=AF.Square), [ps], [sq])
                                p2 = psr.get()
                                S.op("pe", lambda e, p2=p2, sq=sq: e.matmul(p2[:, 0:n], lhsT=blk64[:], rhs=sq[:, 0:n],
                                                                            start=True, stop=True), [blk64, sq], [p2])
                                rs = tmp_p.get()
                                rstd_ops(p2, p2[:, 0:n], rs, rs[:, 0:n], 1.0 / 64)
                                qn = tb_p.get()
                                S.op("dve", lambda e, ps=ps, rs=rs, qn=qn, gcol=gcol: e.scalar_tensor_tensor(
                                    out=qn[:, 0:n], in0=ps[:, 0:n], scalar=gcol[:, 0:1], in1=rs[:, 0:n], op0=ALU.mult,
                                    op1=ALU.mult), [ps, rs, gcol], [qn])
                                p3 = psr.get()
                                S.op("pe", lambda e, p3=p3, qn=qn: e.matmul(p3[:, 0:n], lhsT=rotT[:], rhs=qn[:, 0:n],
                                                                            start=True, stop=True), [rotT, qn], [p3])
                                t1 = tmp_p.get()
                                S.op("dve", lambda e, qn=qn, t1=t1: e.tensor_tensor(out=t1[:, 0:n], in0=qn[:, 0:n],
                                                                                    in1=cs[:, 0, 0:n], op=ALU.mult),
                                     [qn, cs], [t1])
                                t2 = tmp_p.get()
                                S.op("dve", lambda e, p3=p3, t2=t2: e.tensor_tensor(out=t2[:, 0:n], in0=p3[:, 0:n],
                                                                                    in1=cs[:, 1, 0:n], op=ALU.mult),
                                     [p3, cs], [t2])
                                S.op("pool", lambda e, t1=t1, t2=t2, cc=cc, qk=qk: e.tensor_tensor(
                                    out=qk[:, cc, 0:n], in0=t1[:, 0:n], in1=t2[:, 0:n], op=ALU.add), [t1, t2], [qk])
                            S.dma("pool", dst.rearrange("(c p) t -> p c t", p=128)[:, :, t0:t0 + n], qk[:, :, 0:n],
                                  reads=[qk])
                        vt = vt_p.get()
                        for j, ps in tm_group(O_BV):
                            S.op("act", lambda e, ps=ps, j=j: e.activation(out=vt[:, j, :], in_=ps[:], func=AF.Identity),
                                 [ps], [vt])
                        S.dma("pool", V_d[t0:t0 + n, :].rearrange("(j p) c -> p j c", p=128), vt[:, 0:nj, :], reads=[vt])
                        for gi, col0 in enumerate((O_DQ, O_DK, O_DV)):
                            gp = gp_p.get()
                            for cc, ps in fm_group(col0):
                                S.op("act", lambda e, ps=ps, cc=cc, gp=gp: e.activation(out=gp[:, cc, 0:n], in_=ps[:, 0:n],
                                                                                        func=AF.Identity), [ps], [gp])
                            S.dma("pool", gpre_d[gi * 512:(gi + 1) * 512, :].rearrange("(c p) t -> p c t", p=128)[
                                :, :, t0:t0 + n], gp[:, :, 0:n], reads=[gp])
                        zt = zt_p.get()
                        for j, ps in tm_group(O_DZ):
                            S.op("act", lambda e, ps=ps, j=j: e.activation(out=zt[:, j, :], in_=ps[:], func=AF.Silu),
                                 [ps], [zt])
                        S.dma("pool", zs_d[t0:t0 + n, :].rearrange("(j p) c -> p j c", p=128), zt[:, 0:nj, :], reads=[zt])
                        bgt = bg_p.get()
                        for j, ps in tm_group(O_BETA, 16):
                            S.op("act", lambda e, ps=ps, j=j: e.activation(out=bgt[:, j, 0:8], in_=ps[:, 0:8],
                                                                           func=AF.Sigmoid), [ps], [bgt])
                            xa = sm_p.get()
                            S.op("dve", lambda e, ps=ps, xa=xa: e.tensor_tensor(out=xa[:, 0:8], in0=ps[:, 8:16],
                                                                                in1=dtb_b[:], op=ALU.add),
                                 [ps, dtb_b], [xa])
                            ab = sm_p.get()
                            S.op("dve", lambda e, xa=xa, ab=ab: e.tensor_scalar(out=ab[:, 0:8], in0=xa[:, 0:8], scalar1=-1.0,
                                                                                scalar2=None, op0=ALU.mult), [xa], [ab])
                            S.op("dve", lambda e, xa=xa, ab=ab: e.tensor_tensor(out=ab[:, 0:8], in0=ab[:, 0:8],
                                                                                in1=xa[:, 0:8], op=ALU.min), [xa, ab], [ab])
                            S.op("act", lambda e, ab=ab: e.activation(out=ab[:, 0:8], in_=ab[:, 0:8], func=AF.Exp),
                                 [ab], [ab])
                            S.op("act", lambda e, ab=ab: e.activation(out=ab[:, 0:8], in_=ab[:, 0:8], func=AF.Ln,
                                                                      bias=1.0), [ab], [ab])
                            S.op("dve", lambda e, xa=xa, ab=ab: e.scalar_tensor_tensor(
                                out=ab[:, 0:8], in0=xa[:, 0:8], scalar=0.0, in1=ab[:, 0:8], op0=ALU.max, op1=ALU.add),
                                [xa, ab], [ab])
                            S.op("dve", lambda e, ab=ab, j=j: e.tensor_tensor(out=bgt[:, j, 8:16], in0=ab[:, 0:8],
                                                                              in1=alog_b[:], op=ALU.mult),
                                 [ab, alog_b], [bgt])
                        S.dma("pool", bg_d[t0:t0 + n, :].rearrange("(j p) c -> p j c", p=128), bgt[:, 0:nj, :],
                              reads=[bgt])
                    S.barrier()

                if dbg == "p1":
                    break
                phase_attn(nc, S, PSB, ident, eps_t, l, b, NB, CTX, T, last, lam_t, subg, lam_init, QT_d, KT_d, V_d, yT_d)
                if dbg == "attn":
                    break
                phase_gdn(nc, S, PSB, ident, eps_t, l, b, CTX, T, cwq, gdng, c_gdn, gpre_d, gqT_d, gkT_d, gkt_d, gvt_d,
                          zs_d, bg_d, of_d, yT_d, dbg=dbg)
                if dbg is not None and dbg.startswith("g"):
                    break
                phase_merge_ffn(nc, S, PSB, ident, eps_t, l, b, NB, CTX, T, last, stream, tiles, seg_bounds, norm_mod_T,
                                Acol2, Bcol2, cwf, modrow_d, wb_in, wb_br, wb_o, wb_up, wb_dn, hT_d, h2T_d, yT_d)
            if dbg is not None:
                break
    S.barrier()
    return nc, S


def phase_attn(nc, S, PSB, ident, eps_t, l, b, NB, CTX, T, last, lam_t, subg, lam_init, QT_d, KT_d, V_d, yT_d):
    NKT = T // 128
    with contextlib.ExitStack() as ES:
        cnt = [0]

        def esb(name, shape, dt=F32):
            cnt[0] += 1
            return Tl(ES.enter_context(nc.sbuf_tensor(f"a{name}_{l}_{b}_{cnt[0]}", list(shape), dt)))
        KT = esb("KT", (128, 4, T), BF16)
        V = esb("V", (128, NKT, 4, 130), BF16)
        S.op("dve", lambda e: e.memset(V[:, :, :, 128:130], 1.0), [], [V])
        for h in range(4):
            S.dma("sp", KT[:, h, :], KT_d[h * 128:(h + 1) * 128, :], writes=[KT])
        for kt in range(NKT):
            S.dma("sp", V[:, kt, :, 0:128], V_d[kt * 128:(kt + 1) * 128, :].rearrange("p (h c) -> p h c", h=4),
                  writes=[V])
        QT_p = Rot([esb("QT", (128, 4, 512), BF16) for _ in range(2)])
        PT_p = Rot([esb("PT", (128, 512), BF16) for _ in range(4)])
        o_p = Rot([esb("o", (128, 2, 128)) for _ in range(4)])
        r_p = Rot([esb("r", (128, 4)) for _ in range(8)])
        yb_p = Rot([esb("yb", (128, 4, 128)) for _ in range(2)])
        ybT_p = Rot([esb("ybT", (128, 4, 512), BF16) for _ in range(2)])
        jk_p = Rot([esb("jk", (128, 128)) for _ in range(2)])
        acc = PSB[0:4]
        st_p = Rot(PSB[4:7])
        tr_ps = PSB[7]
        qtiles = []
        if not last:
            qtiles.append((0, CTX, 0, CTX // 128))
        for t0 in range(CTX, T, 512):
            qtiles.append((t0, min(512, T - t0), 0, NKT))
        for (t0, n, ka, kb) in qtiles:
            nj = n // 128
            QT = QT_p.get()
            S.dma("sp", QT[:, :, 0:n], QT_d.rearrange("(h p) t -> p h t", p=128)[:, :, t0:t0 + n], writes=[QT])
            ybT = ybT_p.get()
            om = {}
            for h in range(4):
                for m in range(2):
                    pr = slice(m * 64, (m + 1) * 64)
                    for kt in range(ka, kb):
                        st = st_p.get()
                        S.op("pe", lambda e, st=st, h=h, kt=kt, pr=pr: e.matmul(
                            st[:, 0:n], lhsT=KT[pr, h, kt * 128:(kt + 1) * 128], rhs=QT[pr, h, 0:n], start=True,
                            stop=True), [KT, QT], [st])
                        PT = PT_p.get()
                        S.op("act", lambda e, st=st, PT=PT: e.activation(out=PT[:, 0:n], in_=st[:, 0:n], func=AF.Exp,
                                                                         scale=0.125), [st], [PT])
                        for j in range(nj):
                            S.op("pe", lambda e, j=j, PT=PT, kt=kt, h=h: e.matmul(
                                acc[j][:, 0:129], lhsT=PT[:, j * 128:(j + 1) * 128], rhs=V[:, kt, h, 0:129],
                                start=(kt == ka), stop=(kt == kb - 1)), [PT, V], [acc[j]])
                    for j in range(nj):
                        if m == 0:
                            om[j] = o_p.get()
                        r = r_p.get()
                        S.op("dve", lambda e, j=j, r=r: e.reciprocal(out=r[:, 0:1], in_=acc[j][:, 128:129]),
                             [acc[j]], [r])
                        if m == 1:
                            S.op("dve", lambda e, r=r: e.tensor_tensor(out=r[:, 0:1], in0=r[:, 0:1], in1=lam_t[:, 0:1],
                                                                       op=ALU.mult), [r, lam_t], [r])
                        S.op("act", lambda e, j=j, r=r, m=m, o=om[j]: e.activation(
                            out=o[:, m, :], in_=acc[j][:, 0:128], func=AF.Identity, scale=r[:, 0:1]), [acc[j], r], [om[j]])
                for j in range(nj):
                    o = om[j]
                    S.op("dve", lambda e, o=o: e.tensor_tensor(out=o[:, 0, :], in0=o[:, 0, :], in1=o[:, 1, :],
                                                               op=ALU.add), [o], [o])
                    ssq = r_p.get()
                    jk = jk_p.get()
                    S.op("act", lambda e, o=o, jk=jk, ssq=ssq: e.activation(out=jk[:], in_=o[:, 0, :], func=AF.Square,
                                                                            accum_out=ssq[:, 0:1]), [o], [jk, ssq])
                    rs = r_p.get()
                    S.op("act", lambda e, ssq=ssq, rs=rs: e.activation(out=rs[:, 0:1], in_=ssq[:, 0:1], func=AF.Sqrt,
                                                                       bias=eps_t[:, :], scale=1.0 / 128),
                         [ssq, eps_t], [rs])
                    S.op("dve", lambda e, rs=rs: e.reciprocal(out=rs[:, 0:1], in_=rs[:, 0:1]), [rs], [rs])
                    yj = jk_p.get()
                    S.op("dve", lambda e, o=o, rs=rs, yj=yj: e.scalar_tensor_tensor(
                        out=yj[:], in0=o[:, 0, :], scalar=rs[:, 0:1], in1=subg[:], op0=ALU.mult, op1=ALU.mult),
                        [o, rs, subg], [yj])
                    S.op("pe", lambda e, yj=yj: e.transpose(out=tr_ps[:, 0:128], in_=yj[:], identity=ident[:]),
                         [yj, ident], [tr_ps])
                    S.op("act", lambda e, j=j, h=h: e.activation(out=ybT[:, h, j * 128:(j + 1) * 128],
                                                                 in_=tr_ps[:, 0:128], func=AF.Identity,
                                                                 scale=(1.0 - lam_init)), [tr_ps], [ybT])
            S.dma("pool", yT_d[1].rearrange("(c p) t -> p c t", p=128)[:, :, t0:t0 + n], ybT[:, :, 0:n], reads=[ybT])
        S.barrier()


def phase_gdn(nc, S, PSB, ident, eps_t, l, b, CTX, T, cwq, gdng, c_gdn, gpre_d, gqT_d, gkT_d, gkt_d, gvt_d, zs_d, bg_d,
              of_d, yT_d, dbg=None):
    DK = 128 ** -0.5
    with contextlib.ExitStack() as ES:
        cnt = [0]

        def esb(name, shape, dt=F32):
            cnt[0] += 1
            return Tl(ES.enter_context(nc.sbuf_tensor(f"g{name}_{l}_{b}_{cnt[0]}", list(shape), dt)))
        in_p = Rot([esb("in", (128, 514)) for _ in range(6)])
        t_p = Rot([esb("t", (128, 512)) for _ in range(14)])
        tk_p = Rot([esb("tk", (128, 4, 128)) for _ in range(5)])
        ones = esb("ones", (128, 128))
        S.op("dve", lambda e: e.memset(ones[:], 1.0), [], [ones])
        psr = Rot(PSB)
        segs = [(0, CTX), (CTX, T)]
        def g1_gen(fc, s0, s1, t0):
            kind = fc // 4
            h = fc % 4
            n = min(512, s1 - t0)
            xin = in_p.get()
            lo = max(t0 - 1, s0)
            hi = min(t0 + n + 1, s1)
            if lo != t0 - 1 or hi != t0 + n + 1:
                S.op("pool", lambda e, xin=xin: e.memset(xin[:], 0.0), [], [xin])
            S.dma("sp", xin[:, lo - (t0 - 1):hi - (t0 - 1)], gpre_d[fc * 128:(fc + 1) * 128, lo:hi], writes=[xin])
            y = t_p.get()
            S.op("act", lambda e, xin=xin, y=y: e.activation(out=y[:, 0:n], in_=xin[:, 0:n], func=AF.Identity,
                                                             scale=cwq[:, fc, 0:1]), [xin, cwq], [y])
            S.op("dve", lambda e, xin=xin, y=y: e.scalar_tensor_tensor(
                out=y[:, 0:n], in0=xin[:, 1:n + 1], scalar=cwq[:, fc, 1:2], in1=y[:, 0:n], op0=ALU.mult,
                op1=ALU.add), [xin, cwq, y], [y])
            S.op("dve", lambda e, xin=xin, y=y: e.scalar_tensor_tensor(
                out=y[:, 0:n], in0=xin[:, 2:n + 2], scalar=cwq[:, fc, 2:3], in1=y[:, 0:n], op0=ALU.mult,
                op1=ALU.add), [xin, cwq, y], [y])
            S.op("act", lambda e, y=y: e.activation(out=y[:, 0:n], in_=y[:, 0:n], func=AF.Silu), [y], [y])
            yield
            if kind < 2:
                sq = t_p.get()
                S.op("dve", lambda e, y=y, sq=sq: e.tensor_tensor(out=sq[:, 0:n], in0=y[:, 0:n], in1=y[:, 0:n],
                                                                  op=ALU.mult), [y], [sq])
                p2 = psr.get()
                S.op("pe", lambda e, p2=p2, sq=sq: e.matmul(p2[:, 0:n], lhsT=ones[:], rhs=sq[:, 0:n], start=True,
                                                            stop=True), [ones, sq], [p2])
                rs = t_p.get()
                S.op("act", lambda e, p2=p2, rs=rs: e.activation(out=rs[:, 0:n], in_=p2[:, 0:n], func=AF.Sqrt,
                                                                 bias=eps_t[:, :], scale=1.0), [p2, eps_t], [rs])
                yield
                S.op("dve", lambda e, rs=rs: e.reciprocal(out=rs[:, 0:n], in_=rs[:, 0:n]), [rs], [rs])
                S.op("dve", lambda e, y=y, rs=rs: e.tensor_tensor(out=y[:, 0:n], in0=y[:, 0:n], in1=rs[:, 0:n],
                                                                  op=ALU.mult), [y, rs], [y])
                dstT = gqT_d if kind == 0 else gkT_d
                S.dma("pool", dstT[h * 128:(h + 1) * 128, t0:t0 + n], y[:, 0:n], reads=[y])
            yield
            if kind >= 1:
                nj = n // 128
                pt = psr.get()
                for j in range(nj):
                    S.op("pe", lambda e, j=j, pt=pt, y=y: e.transpose(out=pt[:, j * 128:(j + 1) * 128],
                                                                      in_=y[:, j * 128:(j + 1) * 128],
                                                                      identity=ident[:]), [y, ident], [pt])
                tk = tk_p.get()
                S.op("act", lambda e, pt=pt, tk=tk: e.activation(out=tk[:].rearrange("p j d -> p (j d)")[:, 0:n],
                                                                 in_=pt[:, 0:n], func=AF.Identity), [pt], [tk])
                dst = gkt_d if kind == 1 else gvt_d
                S.dma("pool", dst[t0:t0 + n, h * 128:(h + 1) * 128].rearrange("(j p) d -> p j d", p=128),
                      tk[:, 0:nj, :], reads=[tk])

        items = [(fc, s0, s1, t0) for fc in range(12) for (s0, s1) in segs for t0 in range(s0, s1, 512)]
        it = iter(items)
        active = []
        while True:
            while len(active) < 3:
                nxt = next(it, None)
                if nxt is None:
                    break
                active.append(g1_gen(*nxt))
            if not active:
                break
            for g in list(active):
                try:
                    next(g)
                except StopIteration:
                    active.remove(g)
        S.barrier()
    if dbg == "g1":
        return
    NCH = T // 64
    NCC = CTX // 64
    order_f = list(range(NCH))
    order_b = list(range(NCC - 1, -1, -1)) + list(range(NCH - 1, NCC - 1, -1))
    with contextlib.ExitStack() as ES:
        cnt = [0]

        zsrc = [None]

        def esb(name, shape, dt=F32, zero=False, r=False):
            cnt[0] += 1
            t = Tl(ES.enter_context(nc.sbuf_tensor(f"s{name}_{l}_{b}_{cnt[0]}", list(shape), dt)))
            if zero and r:
                fl = t[:].rearrange("p a b -> p (a b)")
                S.op("dve", lambda e: e.tensor_copy(out=fl.bitcast(mybir.dt.float32r), in_=zsrc[0][:, 0:fl.shape[1]]),
                     [zsrc[0]], [t])
            elif zero:
                S.op("dve", lambda e: e.memset(t[:], 0.0), [], [t])
            return t
        zsrc[0] = esb("zsrc", (128, 1024), zero=True)
        gc = esb("gc", (64, 6, 8, 64))
        S.dma("sp", gc[:], c_gdn, writes=[gc])
        triC, m_incl, m_strict, m_inclT, id8, ones8 = (gc[:, i] for i in range(6))
        triP = esb("triP", (128, 2, 128), zero=True)
        for d in range(2):
            S.op("dve", lambda e, d=d: e.tensor_copy(out=triP[0:64, d, 0:64], in_=triC[:, d * 4, :]), [gc, triP], [triP])
        nones = esb("nones", (128, 128), zero=True)
        S.op("dve", lambda e: e.memset(nones[0:64, :], -1.0), [nones], [nones])
        ones128 = esb("ones128", (128, 128), zero=True)
        S.op("dve", lambda e: e.memset(ones128[0:64, :], 1.0), [ones128], [ones128])
        identr = esb("identr", (128, 64))
        S.op("dve", lambda e: e.tensor_copy(out=identr[:].bitcast(mybir.dt.float32r), in_=ident[:, 0:64]), [ident], [identr])
        St = esb("S", (128, 8, 128), zero=True)
        Sb = esb("Sb", (128, 8, 128), BF16, zero=True)
        NB2 = 3
        qT_p = Rot([esb("qT", (128, 9, 64), zero=True) for _ in range(NB2)])
        kT_p = Rot([esb("kT", (128, 9, 64), zero=True) for _ in range(NB2)])
        kt_p = Rot([esb("kt", (64, 8, 128)) for _ in range(NB2)])
        vt_p = Rot([esb("vt", (64, 8, 128)) for _ in range(NB2)])
        b8_p = Rot([esb("b8", (128, 2, 8), zero=True) for _ in range(NB2)])
        s8_p = Rot([esb("s8", (128, 8)) for _ in range(24)])
        B64 = Rot([esb("B64", (128, 9, 64), zero=True) for _ in range(10)])
        B64r = Rot([esb("B64r", (128, 9, 64), zero=True, r=True) for _ in range(20)])
        B64b = Rot([esb("B64b", (128, 9, 64), BF16, zero=True) for _ in range(4)])
        B128 = Rot([esb("B128", (128, 8, 128), zero=True) for _ in range(3)])
        B128r = Rot([esb("B128r", (128, 8, 128), zero=True, r=True) for _ in range(4)])
        B128b = Rot([esb("B128b", (128, 8, 128), BF16, zero=True) for _ in range(4)])
        W64b = Rot([esb("W64b", (128, 9, 64), BF16, zero=True) for _ in range(4)])
        T512 = Rot([esb("T512", (128, 4, 128)) for _ in range(3)])
        psr = Rot(PSB)

        def v3(ps, np_, a, bb):
            return ps[0:np_, 0:a * bb].rearrange("p (a b) -> p a b", a=a)

        def bc(ap, np_, a, bb):
            return ap.unsqueeze(2).to_broadcast([np_, a, bb])

        def f8(X):
            return X[0:64, 0:8, :]

        F32R = mybir.dt.float32r

        def l2(X, k):
            return X[:, k:k + 2, :].rearrange("p a b -> p (a b)").bitcast(F32R)

        def l2f(X, k):
            return X[:, k:k + 2, :].rearrange("p a b -> p (a b)")

        def rr(ap):
            return ap.bitcast(F32R)

        def slot_gen(s):
            tf = order_f[s] * 64
            tb = order_b[s] * 64
            qT = qT_p.get()
            kT = kT_p.get()
            kt = kt_p.get()
            vt = vt_p.get()
            b8 = b8_p.get()
            for (bo, tt, dcol) in ((0, tf, 0), (4, tb, 4)):
                S.dma("sp", qT[:, bo:bo + 4, :], gqT_d.rearrange("(h p) t -> p h t", p=128)[:, :, tt:tt + 64], writes=[qT])
                S.dma("sp", kT[:, bo:bo + 4, :], gkT_d.rearrange("(h p) t -> p h t", p=128)[:, :, tt:tt + 64], writes=[kT])
                S.dma("sp", kt[:, bo:bo + 4, :], gkt_d[tt:tt + 64, :].rearrange("p (h d) -> p h d", h=4), writes=[kt])
                S.dma("sp", vt[:, bo:bo + 4, :], gvt_d[tt:tt + 64, :].rearrange("p (h d) -> p h d", h=4), writes=[vt])
                S.dma("sp", b8[0:64, 0, bo:bo + 4], bg_d[tt:tt + 64, dcol:dcol + 4], writes=[b8])
                S.dma("sp", b8[0:64, 1, bo:bo + 4], bg_d[tt:tt + 64, 8 + dcol:8 + dcol + 4], writes=[b8])
            beta8 = b8[0:64, 0, :]
            g8 = b8[0:64, 1, :]
            g8p = b8[:, 1, :]
            gTri = B64.get()
            S.op("dve", lambda e, gTri=gTri, g8=g8: e.tensor_tensor(out=f8(gTri), in0=triC, in1=bc(g8, 64, 8, 64),
                                                                    op=ALU.mult), [gc, b8], [gTri])
            psA = psr.get()
            for d in range(2):
                S.op("pe", lambda e, d=d, psA=psA, g8p=g8p: e.matmul(psA[:, d * 4:(d + 1) * 4], lhsT=triP[:, d, :],
                                                                     rhs=g8p[:, d * 4:(d + 1) * 4], start=True, stop=True),
                     [triP, b8], [psA])
            S.op("pe", lambda e, psA=psA, g8p=g8p: e.matmul(psA[:, 8:16], lhsT=ones128[:], rhs=g8p, start=True, stop=True),
                 [ones128, b8], [psA])
            G = s8_p.get()
            Gt = s8_p.get()
            S.op("act", lambda e, G=G, psA=psA: e.activation(out=G[0:64, :], in_=psA[0:64, 0:8], func=AF.Identity),
                 [psA], [G])
            S.op("act", lambda e, Gt=Gt, psA=psA: e.activation(out=Gt[:], in_=psA[:, 8:16], func=AF.Identity),
                 [psA], [Gt])
            eG = s8_p.get()
            S.op("act", lambda e, G=G, eG=eG: e.activation(out=eG[0:64, :], in_=G[0:64, :], func=AF.Exp), [G], [eG])
            gtot = s8_p.get()
            S.op("act", lambda e, Gt=Gt, gtot=gtot: e.activation(out=gtot[:], in_=Gt[:], func=AF.Exp), [Gt], [gtot])
            etl = s8_p.get()
            S.op("dve", lambda e, etl=etl, Gt=Gt, G=G: e.tensor_tensor(out=etl[0:64, :], in0=Gt[0:64, :], in1=G[0:64, :],
                                                                       op=ALU.subtract), [Gt, G], [etl])
            S.op("act", lambda e, etl=etl: e.activation(out=etl[0:64, :], in_=etl[0:64, :], func=AF.Exp), [etl], [etl])
            beG = s8_p.get()
            S.op("dve", lambda e, beG=beG, eG=eG, beta8=beta8: e.tensor_tensor(out=beG[0:64, :], in0=eG[0:64, :],
                                                                               in1=beta8, op=ALU.mult), [eG, b8], [beG])
            eGq = s8_p.get()
            S.op("dve", lambda e, eGq=eGq, eG=eG: e.tensor_scalar(out=eGq[0:64, :], in0=eG[0:64, :], scalar1=DK,
                                                                  scalar2=None, op0=ALU.mult), [eG], [eGq])
            yield "st"
            psD = psr.get()
            for k in range(8):
                S.op("pe", lambda e, k=k, psD=psD, gTri=gTri: e.matmul(psD[:, k * 64:(k + 1) * 64], lhsT=l2f(gTri, k),
                                                                       rhs=ones128[:, 0:64], start=True, stop=False),
                     [gTri, ones128], [psD])
                S.op("pe", lambda e, k=k, psD=psD, gTri=gTri: e.matmul(psD[:, k * 64:(k + 1) * 64], lhsT=nones[:],
                                                                       rhs=gTri[:, k, :], start=False, stop=True),
                     [gTri, nones], [psD])
            seg = B64.get()
            S.op("dve", lambda e, seg=seg, psD=psD: e.tensor_scalar(out=f8(seg), in0=v3(psD, 64, 8, 64), scalar1=0.0,
                                                                    scalar2=None, op0=ALU.min), [psD], [seg])
            S.op("act", lambda e, seg=seg: e.activation(out=f8(seg), in_=f8(seg), func=AF.Exp), [seg], [seg])
            segS = B64.get()
            S.op("pool", lambda e, seg=seg, segS=segS: e.tensor_tensor(out=f8(segS), in0=f8(seg), in1=m_strict,
                                                                       op=ALU.mult), [seg, gc], [segS])
            sgT = B64.get()
            S.op("dve", lambda e, sgT=sgT, psD=psD: e.tensor_scalar(out=f8(sgT), in0=v3(psD, 64, 8, 64), scalar1=-1.0,
                                                                    scalar2=0.0, op0=ALU.mult, op1=ALU.min),
                 [psD], [sgT])
            S.op("act", lambda e, sgT=sgT: e.activation(out=f8(sgT), in_=f8(sgT), func=AF.Exp), [sgT], [sgT])
            S.op("dve", lambda e, sgT=sgT: e.scalar_tensor_tensor(out=f8(sgT), in0=f8(sgT), scalar=DK, in1=m_inclT,
                                                                  op0=ALU.mult, op1=ALU.mult), [sgT, gc], [sgT])
            yield "st"
            psK = psr.get()
            psQ = psr.get()
            for k in range(8):
                S.op("pe", lambda e, k=k, psK=psK, kT=kT: e.matmul(psK[:, k * 64:(k + 1) * 64], lhsT=l2f(kT, k),
                                                                   rhs=kT[:, k, :], start=True, stop=True), [kT], [psK])
            for k in range(8):
                S.op("pe", lambda e, k=k, psQ=psQ, kT=kT, qT=qT: e.matmul(psQ[:, k * 64:(k + 1) * 64], lhsT=l2f(kT, k),
                                                                          rhs=qT[:, k, :], start=True, stop=True),
                     [kT, qT], [psQ])
            A1 = B64.get()
            S.op("dve", lambda e, A1=A1, psK=psK, segS=segS: e.tensor_tensor(out=f8(A1), in0=v3(psK, 64, 8, 64),
                                                                             in1=f8(segS), op=ALU.mult), [psK, segS], [A1])
            A = B64r.get()
            S.op("dve", lambda e, A=A, A1=A1, beta8=beta8: e.tensor_tensor(out=rr(f8(A)), in0=f8(A1),
                                                                           in1=bc(beta8, 64, 8, 64), op=ALU.mult),
                 [A1, b8], [A])
            inT = B64b.get()
            S.op("dve", lambda e, inT=inT, psQ=psQ, sgT=sgT: e.tensor_tensor(out=f8(inT), in0=v3(psQ, 64, 8, 64),
                                                                             in1=f8(sgT), op=ALU.mult), [psQ, sgT], [inT])
            yield "st"
            psT = psr.get()
            for k in range(8):
                S.op("pe", lambda e, k=k, psT=psT, A=A: e.matmul(psT[:, k * 64:(k + 1) * 64], lhsT=l2(A, k),
                                                                 rhs=rr(identr[:]), start=True, stop=True),
                     [A, identr], [psT])
            AT = B64r.get()
            S.op("act", lambda e, AT=AT, psT=psT: e.activation(out=rr(f8(AT)), in_=v3(psT, 64, 8, 64), func=AF.Identity),
                 [psT], [AT])
            TT = B64r.get()
            S.op("dve", lambda e, TT=TT, AT=AT: e.tensor_tensor(out=rr(f8(TT)), in0=id8, in1=f8(AT),
                                                                op=ALU.subtract), [gc, AT], [TT])
            yield "st"
            P, PT_ = A, AT
            for lev in range(1, 6):
                psP = psr.get()
                for k in range(8):
                    S.op("pe", lambda e, k=k, psP=psP, P=P, PT_=PT_: e.matmul(
                        psP[:, k * 64:(k + 1) * 64], lhsT=l2(PT_, k), rhs=rr(P[:, k, :]), start=True, stop=True),
                        [P, PT_], [psP])
                if lev < 5:
                    psPT = psr.get()
                    for k in range(8):
                        S.op("pe", lambda e, k=k, psPT=psPT, P=P, PT_=PT_: e.matmul(
                            psPT[:, k * 64:(k + 1) * 64], lhsT=l2(P, k), rhs=rr(PT_[:, k, :]), start=True, stop=True),
                            [P, PT_], [psPT])
                Pn = B64r.get()
                S.op("act", lambda e, Pn=Pn, psP=psP: e.activation(out=rr(f8(Pn)), in_=v3(psP, 64, 8, 64), func=AF.Identity),
                     [psP], [Pn])
                if lev < 5:
                    PTn = B64r.get()
                    S.op("dve", lambda e, PTn=PTn, psPT=psPT: e.tensor_copy(out=rr(f8(PTn)), in_=v3(psPT, 64, 8, 64)),
                         [psPT], [PTn])
                else:
                    PTn = None
                psZ = psr.get()
                for k in range(8):
                    S.op("pe", lambda e, k=k, psZ=psZ, Pn=Pn, TT=TT: e.matmul(
                        psZ[:, k * 64:(k + 1) * 64], lhsT=l2(Pn, k), rhs=rr(TT[:, k, :]), start=True, stop=True),
                        [Pn, TT], [psZ])
                TTn = B64r.get()
                S.op("dve", lambda e, TTn=TTn, TT=TT, psZ=psZ: e.tensor_tensor(out=rr(f8(TTn)), in0=f8(TT),
                                                                               in1=v3(psZ, 64, 8, 64), op=ALU.add),
                     [TT, psZ], [TTn])
                TT = TTn
                P, PT_ = Pn, PTn
                yield "st"
            vb = B128r.get()
            S.op("pool", lambda e, vb=vb, vt=vt, beta8=beta8: e.tensor_tensor(out=rr(vb[0:64]), in0=vt[:],
                                                                              in1=bc(beta8, 64, 8, 128), op=ALU.mult),
                 [vt, b8], [vb])
            kbg = B128r.get()
            S.op("dve", lambda e, kbg=kbg, kt=kt, beG=beG: e.tensor_tensor(out=rr(kbg[0:64]), in0=kt[:],
                                                                           in1=bc(beG[0:64, :], 64, 8, 128), op=ALU.mult),
                 [kt, beG], [kbg])
            ktl = B128b.get()
            S.op("pool", lambda e, ktl=ktl, kt=kt, etl=etl: e.tensor_tensor(out=ktl[0:64], in0=kt[:],
                                                                            in1=bc(etl[0:64, :], 64, 8, 128),
                                                                            op=ALU.mult), [kt, etl], [ktl])
            qTb = W64b.get()
            S.op("pool", lambda e, qTb=qTb, qT=qT: e.tensor_copy(out=qTb[:, 0:8, :], in_=qT[:, 0:8, :]), [qT], [qTb])
            u = B128.get()
            for hf in range(2):
                psU = psr.get()
                for k4 in range(4):
                    k = hf * 4 + k4
                    S.op("pe", lambda e, k=k, k4=k4, psU=psU, TT=TT, vb=vb: e.matmul(
                        psU[:, k4 * 128:(k4 + 1) * 128], lhsT=l2(TT, k), rhs=rr(vb[:, k, :]), start=True, stop=True),
                        [TT, vb], [psU])
                S.op("act", lambda e, hf=hf, psU=psU, u=u: e.activation(out=u[0:64, hf * 4:(hf + 1) * 4, :],
                                                                        in_=v3(psU, 64, 4, 128), func=AF.Identity),
                     [psU], [u])
            psW = psr.get()
            for k in range(8):
                S.op("pe", lambda e, k=k, psW=psW, kbg=kbg, TT=TT: e.matmul(psW[:, k * 64:(k + 1) * 64], lhsT=rr(kbg[:, k, :]),
                                                                            rhs=rr(TT[:, k, :]), start=True, stop=True),
                     [kbg, TT], [psW])
            wTb = W64b.get()
            S.op("act", lambda e, wTb=wTb, psW=psW: e.activation(out=wTb[:, 0:8, :], in_=v3(psW, 128, 8, 64),
                                                                 func=AF.Identity), [psW], [wTb])
            yield "PRE_DONE"
            vn = B128b.get()
            for hf in range(2):
                hs = slice(hf * 4, (hf + 1) * 4)
                psWS = psr.get()
                for k4 in range(4):
                    k = hf * 4 + k4
                    S.op("pe", lambda e, k=k, k4=k4, psWS=psWS, wTb=wTb: e.matmul(
                        psWS[:, k4 * 128:(k4 + 1) * 128], lhsT=l2f(wTb, k), rhs=Sb[:, k, :], start=True, stop=True),
                        [wTb, Sb], [psWS])
                S.op("dve", lambda e, hs=hs, psWS=psWS, vn=vn, u=u: e.tensor_tensor(
                    out=vn[0:64, hs, :], in0=u[0:64, hs, :], in1=v3(psWS, 64, 4, 128), op=ALU.subtract), [u, psWS], [vn])
                psQS = psr.get()
                for k4 in range(4):
                    k = hf * 4 + k4
                    S.op("pe", lambda e, k=k, k4=k4, psQS=psQS, qTb=qTb: e.matmul(
                        psQS[:, k4 * 128:(k4 + 1) * 128], lhsT=l2f(qTb, k), rhs=Sb[:, k, :], start=True, stop=True),
                        [qTb, Sb], [psQS])
                psIV = psr.get()
                for k4 in range(4):
                    k = hf * 4 + k4
                    S.op("pe", lambda e, k=k, k4=k4, psIV=psIV, inT=inT, vn=vn: e.matmul(
                        psIV[:, k4 * 128:(k4 + 1) * 128], lhsT=l2f(inT, k), rhs=vn[:, k, :], start=True, stop=True),
                        [inT, vn], [psIV])
                o = T512.get()
                S.op("dve", lambda e, o=o, psQS=psQS, eGq=eGq, hs=hs: e.tensor_tensor(
                    out=o[0:64], in0=v3(psQS, 64, 4, 128), in1=bc(eGq[0:64, hs], 64, 4, 128), op=ALU.mult),
                    [psQS, eGq], [o])
                S.op("dve", lambda e, o=o, psIV=psIV: e.tensor_tensor(out=o[0:64], in0=o[0:64], in1=v3(psIV, 64, 4, 128),
                                                                      op=ALU.add), [o, psIV], [o])
                tt = tf if hf == 0 else tb
                S.dma("pool", of_d[hf, tt:tt + 64, :].rearrange("p (h d) -> p h d", h=4), o[0:64], reads=[o])
                psKV = psr.get()
                for k4 in range(4):
                    k = hf * 4 + k4
                    S.op("pe", lambda e, k=k, k4=k4, psKV=psKV, ktl=ktl, vn=vn: e.matmul(
                        psKV[:, k4 * 128:(k4 + 1) * 128], lhsT=ktl[:, k, :], rhs=vn[:, k, :], start=True, stop=True),
                        [ktl, vn], [psKV])
                sd = T512.get()
                S.op("pool", lambda e, sd=sd, hs=hs, gtot=gtot: e.tensor_tensor(
                    out=sd[:], in0=St[:, hs, :], in1=bc(gtot[:, hs], 128, 4, 128), op=ALU.mult), [St, gtot], [sd])
                S.op("dve", lambda e, sd=sd, hs=hs, psKV=psKV: e.tensor_tensor(
                    out=St[:, hs, :], in0=sd[:], in1=v3(psKV, 128, 4, 128), op=ALU.add), [sd, psKV], [St])
                S.op("act", lambda e, hs=hs: e.activation(out=Sb[:, hs, :], in_=St[:, hs, :], func=AF.Identity),
                     [St], [Sb])
        NSL = NCH if dbg not in ("g2pre", "g2inv", "g2one") else 1
        for s0 in range(0, NSL, 2):
            gens = [slot_gen(s) for s in range(s0, min(s0 + 2, NSL))]
            done = [False] * len(gens)
            while not all(done):
                for i, g in enumerate(gens):
                    if not done[i] and next(g) == "PRE_DONE":
                        done[i] = True
            for g in gens:
                for _ in g:
                    pass
        S.barrier()
    if dbg is not None and dbg.startswith("g2"):
        return
    with contextlib.ExitStack() as ES:
        cnt = [0]

        def esb(name, shape, dt=F32):
            cnt[0] += 1
            return Tl(ES.enter_context(nc.sbuf_tensor(f"o{name}_{l}_{b}_{cnt[0]}", list(shape), dt)))
        of_p = Rot([esb("of", (128, 512)) for _ in range(2)])
        ob_p = Rot([esb("ob", (128, 512)) for _ in range(2)])
        z_p = Rot([esb("z", (128, 512)) for _ in range(2)])
        sq_p = Rot([esb("sq", (128, 512)) for _ in range(2)])
        st_p = Rot([esb("st", (128, 4)) for _ in range(4)])
        yT_p = Rot([esb("yT", (128, 4, 128), BF16) for _ in range(2)])
        psr = Rot(PSB)
        for t0 in range(0, T, 128):
            of = of_p.get()
            ob = ob_p.get()
            z = z_p.get()
            S.dma("sp", of[:], of_d[0, t0:t0 + 128, :], writes=[of])
            S.dma("sp", ob[:], of_d[1, t0:t0 + 128, :], writes=[ob])
            S.dma("sp", z[:], zs_d[t0:t0 + 128, :], writes=[z])
            S.op("dve", lambda e, of=of, ob=ob: e.tensor_tensor(out=of[:], in0=of[:], in1=ob[:], op=ALU.add), [of, ob], [of])
            sq = sq_p.get()
            S.op("pool", lambda e, of=of, sq=sq: e.tensor_tensor(out=sq[:], in0=of[:], in1=of[:], op=ALU.mult), [of], [sq])
            ssq = st_p.get()
            S.op("dve", lambda e, sq=sq, ssq=ssq: e.tensor_reduce(out=ssq[:], in_=sq[:].rearrange("p (h d) -> p h d", h=4),
                                                                  axis=AX.X, op=ALU.add), [sq], [ssq])
            rs = st_p.get()
            S.op("act", lambda e, ssq=ssq, rs=rs: e.activation(out=rs[:], in_=ssq[:], func=AF.Sqrt, bias=eps_t[:, :],
                                                               scale=1.0 / 128), [ssq, eps_t], [rs])
            S.op("dve", lambda e, rs=rs: e.reciprocal(out=rs[:], in_=rs[:]), [rs], [rs])
            S.op("dve", lambda e, of=of, rs=rs: e.tensor_tensor(
                out=of[:].rearrange("p (h d) -> p h d", h=4), in0=of[:].rearrange("p (h d) -> p h d", h=4),
                in1=rs[:].unsqueeze(2).to_broadcast([128, 4, 128]), op=ALU.mult), [of, rs], [of])
            S.op("pool", lambda e, of=of: e.tensor_tensor(out=of[:], in0=of[:], in1=gdng[:], op=ALU.mult), [of, gdng], [of])
            S.op("dve", lambda e, of=of, z=z: e.tensor_tensor(out=of[:], in0=of[:], in1=z[:], op=ALU.mult), [of, z], [of])
            pt = psr.get()
            for h in range(4):
                S.op("pe", lambda e, h=h, pt=pt, of=of: e.transpose(out=pt[:, h * 128:(h + 1) * 128],
                                                                    in_=of[:, h * 128:(h + 1) * 128], identity=ident[:]),
                     [of, ident], [pt])
            yT = yT_p.get()
            S.op("act", lambda e, pt=pt, yT=yT: e.activation(out=yT[:].rearrange("p h t -> p (h t)"), in_=pt[:],
                                                             func=AF.Identity), [pt], [yT])
            S.dma("pool", yT_d[2].rearrange("(c p) t -> p c t", p=128)[:, :, t0:t0 + 128], yT[:], reads=[yT])
        S.barrier()


def phase_merge_ffn(nc, S, PSB, ident, eps_t, l, b, NB, CTX, T, last, stream, tiles, seg_bounds, norm_mod_T, Acol2, Bcol2,
                    cwf, modrow_d, wb_in, wb_br, wb_o, wb_up, wb_dn, hT_d, h2T_d, yT_d):
    with contextlib.ExitStack() as ES:
        cnt = [0]

        def esb(name, shape, dt=F32):
            cnt[0] += 1
            return Tl(ES.enter_context(nc.sbuf_tensor(f"m{name}_{l}_{b}_{cnt[0]}", list(shape), dt)))
        wg = esb("wg", (128, 8, 3072), BF16)
        wbr = esb("wbr", (128, 3, 4, D), BF16)
        wo = esb("wo", (128, 8, D), BF16)
        for kc in range(8):
            S.dma("sp", wg[:, kc, :], wb_in[kc * 128:(kc + 1) * 128, O_GATE:O_GATE + 3072], writes=[wg])
            S.dma("sp", wo[:, kc, :], wb_o[kc * 128:(kc + 1) * 128, :], writes=[wo])
        for br in range(3):
            S.dma("sp", wbr[:, br, :, :], wb_br[br].rearrange("(c p) n -> p c n", p=128), writes=[wbr])
        gtb = esb("gtb", (128, 2, D))
        S.dma("sp", gtb[:, 0, :], modrow_d[0, b].partition_broadcast(128), writes=[gtb])
        S.dma("sp", gtb[:, 1, :], modrow_d[0, NB].partition_broadcast(128), writes=[gtb])
        hT_p = Rot([esb("hT", (128, 8, 512), BF16) for _ in range(1)])
        yb_p = Rot([esb("yb", (128, 3, 4, 512), BF16) for _ in range(1)])
        yT_p = Rot([esb("yT", (128, 8, 512), BF16) for _ in range(1)])
        sg_p = Rot([esb("sg", (128, 512)) for _ in range(4)])
        ac_p = Rot([esb("ac", (128, 512)) for _ in range(3)])
        xt_p = Rot([esb("xt", (128, 4, D)) for _ in range(1)])
        xn_p = Rot([esb("xn", (128, 4, D)) for _ in range(1)])
        junk = Rot([esb("junk", (128, D)) for _ in range(1)])
        stat = Rot([esb("stat", (128, 8)) for _ in range(6)])
        h2_p = Rot([esb("h2", (128, 8, 512), BF16) for _ in range(1)])
        psr = Rot(PSB)
        for (t0, n) in tiles(512, lat_only=last):
            nj = n // 128
            ri = NB if t0 < CTX else b
            gi = 1 if t0 < CTX else 0
            hT = hT_p.get()
            S.dma("sp", hT[:, :, 0:n], hT_d.rearrange("(kc p) t -> p kc t", p=128)[:, :, t0:t0 + n], writes=[hT])
            yb = yb_p.get()
            for br in range(3):
                S.dma("sp", yb[:, br, :, 0:n], yT_d[br].rearrange("(c p) t -> p c t", p=128)[:, :, t0:t0 + n], writes=[yb])
            xt = xt_p.get()
            S.dma("sp", xt[:, 0:nj, :], stream(b, t0, n).rearrange("(j p) d -> p j d", p=128), writes=[xt])
            yT = yT_p.get()
            for fc in range(8):
                acc_t = ac_p.get()
                for br in range(3):
                    pg = psr.get()
                    for kc in range(8):
                        S.op("pe", lambda e, kc=kc, pg=pg, br=br, fc=fc: e.matmul(
                            pg[:, 0:n], lhsT=wg[:, kc, br * D + fc * 128:br * D + (fc + 1) * 128], rhs=hT[:, kc, 0:n],
                            start=(kc == 0), stop=(kc == 7)), [wg, hT], [pg])
                    sg = sg_p.get()
                    S.op("act", lambda e, pg=pg, sg=sg: e.activation(out=sg[:, 0:n], in_=pg[:, 0:n], func=AF.Sigmoid),
                         [pg], [sg])
                    pb = psr.get()
                    for kc in range(4):
                        S.op("pe", lambda e, kc=kc, pb=pb, br=br, fc=fc: e.matmul(
                            pb[:, 0:n], lhsT=wbr[:, br, kc, fc * 128:(fc + 1) * 128], rhs=yb[:, br, kc, 0:n],
                            start=(kc == 0), stop=(kc == 3)), [wbr, yb], [pb])
                    if br == 0:
                        S.op("dve", lambda e, sg=sg, pb=pb, acc_t=acc_t: e.tensor_tensor(
                            out=acc_t[:, 0:n], in0=sg[:, 0:n], in1=pb[:, 0:n], op=ALU.mult), [sg, pb], [acc_t])
                    else:
                        S.op("dve", lambda e, sg=sg, pb=pb: e.tensor_tensor(out=sg[:, 0:n], in0=sg[:, 0:n], in1=pb[:, 0:n],
                                                                            op=ALU.mult), [sg, pb], [sg])
                        if br == 1:
                            S.op("pool", lambda e, sg=sg, acc_t=acc_t: e.tensor_tensor(
                                out=acc_t[:, 0:n], in0=acc_t[:, 0:n], in1=sg[:, 0:n], op=ALU.add), [sg, acc_t], [acc_t])
                        else:
                            S.op("pool", lambda e, sg=sg, acc_t=acc_t, fc=fc: e.tensor_tensor(
                                out=yT[:, fc, 0:n], in0=acc_t[:, 0:n], in1=sg[:, 0:n], op=ALU.add), [sg, acc_t], [yT])
            for j in range(nj):
                for hf in range(2):
                    po = psr.get()
                    for kc in range(8):
                        S.op("pe", lambda e, kc=kc, po=po, j=j, hf=hf: e.matmul(
                            po[:], lhsT=yT[:, kc, j * 128:(j + 1) * 128], rhs=wo[:, kc, hf * 512:(hf + 1) * 512],
                            start=(kc == 0), stop=(kc == 7)), [yT, wo], [po])
                    tm = sg_p.get()
                    S.op("dve", lambda e, po=po, tm=tm, hf=hf: e.tensor_tensor(
                        out=tm[:], in0=po[:], in1=gtb[:, gi, hf * 512:(hf + 1) * 512], op=ALU.mult), [po, gtb], [tm])
                    S.op("pool", lambda e, tm=tm, j=j, hf=hf: e.tensor_tensor(
                        out=xt[:, j, hf * 512:(hf + 1) * 512], in0=xt[:, j, hf * 512:(hf + 1) * 512], in1=tm[:],
                        op=ALU.add), [tm, xt], [xt])
            S.dma("pool", stream(b, t0, n).rearrange("(j p) d -> p j d", p=128), xt[:, 0:nj, :], reads=[xt])
            h2 = h2_p.get()
            norm_mod_T(xt, nj, n, Acol2, Bcol2, ri, h2, (junk, stat, xn_p, psr))
            S.dma("pool", h2T_d.rearrange("(kc p) t -> p kc t", p=128)[:, :, t0:t0 + n], h2[:, :, 0:n], reads=[h2])
        S.barrier()
    NT = 256
    with contextlib.ExitStack() as ES:
        cnt = [0]

        def esb(name, shape, dt=F32):
            cnt[0] += 1
            return Tl(ES.enter_context(nc.sbuf_tensor(f"f{name}_{l}_{b}_{cnt[0]}", list(shape), dt)))
        wu = esb("wu", (128, 8, 2 * D_FF), BF16)
        wd = esb("wd", (128, 22, D), BF16)
        for kc in range(8):
            S.dma("sp", wu[:, kc, :], wb_up[kc * 128:(kc + 1) * 128, :], writes=[wu])
        for c0 in range(0, 22, 2):
            S.dma("sp", wd[:, c0:c0 + 2, :], wb_dn[c0 * 128:(c0 + 2) * 128, :].rearrange("(c p) n -> p c n", p=128),
                  writes=[wd])
        gtb = esb("gtb", (128, 2, D))
        S.dma("sp", gtb[:, 0, :], modrow_d[1, b].partition_broadcast(128), writes=[gtb])
        S.dma("sp", gtb[:, 1, :], modrow_d[1, NB].partition_broadcast(128), writes=[gtb])
        h2_p = Rot([esb("h2", (128, 8, NT + 2), BF16) for _ in range(2)])
        aT_p = Rot([esb("aT", (128, 22, NT), BF16) for _ in range(1)])
        cg_p = Rot([esb("cg", (128, NT)) for _ in range(4)])
        xt_p = Rot([esb("xt", (128, 2, D)) for _ in range(1)])
        tm_p = Rot([esb("tm", (128, 512)) for _ in range(3)])
        psr = Rot(PSB)
        for (t0, n) in tiles(NT, lat_only=last):
            nj = n // 128
            gi = 1 if t0 < CTX else 0
            s0, s1 = seg_bounds(t0)
            lo = max(t0 - 1, s0)
            hi = min(t0 + n + 1, s1)
            h2 = h2_p.get()
            if lo != t0 - 1 or hi != t0 + n + 1:
                S.op("pool", lambda e, h2=h2: e.memset(h2[:], 0.0), [], [h2])
            S.dma("sp", h2[:, :, lo - (t0 - 1):hi - (t0 - 1)], h2T_d.rearrange("(kc p) t -> p kc t", p=128)[:, :, lo:hi],
                  writes=[h2])
            xt = xt_p.get()
            S.dma("sp", xt[:, 0:nj, :], stream(b, t0, n).rearrange("(j p) d -> p j d", p=128), writes=[xt])
            aT = aT_p.get()
            for cc in range(22):
                cgv = []
                for gv in range(2):
                    col = gv * D_FF + cc * 128
                    wi = gv * 22 + cc
                    pu = psr.get()
                    for kc in range(8):
                        S.op("pe", lambda e, kc=kc, pu=pu, col=col: e.matmul(
                            pu[:, 0:n + 2], lhsT=wu[:, kc, col:col + 128], rhs=h2[:, kc, 0:n + 2], start=(kc == 0),
                            stop=(kc == 7)), [wu, h2], [pu])
                    cg = cg_p.get()
                    S.op("act", lambda e, pu=pu, cg=cg, wi=wi: e.activation(out=cg[:, 0:n], in_=pu[:, 0:n],
                                                                            func=AF.Identity, scale=cwf[:, wi, 0:1]),
                         [pu, cwf], [cg])
                    S.op("dve", lambda e, pu=pu, cg=cg, wi=wi: e.scalar_tensor_tensor(
                        out=cg[:, 0:n], in0=pu[:, 1:n + 1], scalar=cwf[:, wi, 1:2], in1=cg[:, 0:n], op0=ALU.mult,
                        op1=ALU.add), [pu, cwf, cg], [cg])
                    S.op("dve", lambda e, pu=pu, cg=cg, wi=wi: e.scalar_tensor_tensor(
                        out=cg[:, 0:n], in0=pu[:, 2:n + 2], scalar=cwf[:, wi, 2:3], in1=cg[:, 0:n], op0=ALU.mult,
                        op1=ALU.add), [pu, cwf, cg], [cg])
                    cgv.append(cg)
                S.op("act", lambda e, cg=cgv[0]: e.activation(out=cg[:, 0:n], in_=cg[:, 0:n], func=AF.Silu),
                     [cgv[0]], [cgv[0]])
                S.op("pool", lambda e, cc=cc, a=cgv[0], v=cgv[1]: e.tensor_tensor(out=aT[:, cc, 0:n], in0=a[:, 0:n],
                                                                                  in1=v[:, 0:n], op=ALU.mult),
                     [cgv[0], cgv[1]], [aT])
            for j in range(nj):
                for hf in range(2):
                    po = psr.get()
                    for cc in range(22):
                        S.op("pe", lambda e, cc=cc, po=po, j=j, hf=hf: e.matmul(
                            po[:], lhsT=aT[:, cc, j * 128:(j + 1) * 128], rhs=wd[:, cc, hf * 512:(hf + 1) * 512],
                            start=(cc == 0), stop=(cc == 21)), [aT, wd], [po])
                    tm = tm_p.get()
                    S.op("dve", lambda e, po=po, tm=tm, hf=hf: e.tensor_tensor(
                        out=tm[:], in0=po[:], in1=gtb[:, gi, hf * 512:(hf + 1) * 512], op=ALU.mult), [po, gtb], [tm])
                    S.op("dve", lambda e, tm=tm, j=j, hf=hf: e.tensor_tensor(
                        out=xt[:, j, hf * 512:(hf + 1) * 512], in0=xt[:, j, hf * 512:(hf + 1) * 512], in1=tm[:],
                        op=ALU.add), [tm, xt], [xt])
            S.dma("pool", stream(b, t0, n).rearrange("(j p) d -> p j d", p=128), xt[:, 0:nj, :], reads=[xt])
        S.barrier()


_CACHE = {}


def run(inputs, n_cores, NB, SEQ, CTX, DEPTH):
    key = (NB, SEQ, CTX, DEPTH)
    if key not in _CACHE:
        _CACHE[key] = build(NB, SEQ, CTX, DEPTH)[0]
    nc = _CACHE[key]
    consts = host_consts(SEQ, CTX)
    shared = {k: np.ascontiguousarray(np.asarray(v, dtype=np.float32)) for k, v in inputs.items()
              if k not in ("x", "c", "ctx")}
    in_maps = []
    for i in range(n_cores):
        m = dict(shared)
        m.update(consts)
        for k in ("x", "c", "ctx"):
            m[k] = np.ascontiguousarray(np.asarray(inputs[k][i * NB:(i + 1) * NB], dtype=np.float32))
        in_maps.append(m)
    res = run_bass_kernel_spmd(nc, in_maps, core_ids=list(range(n_cores)))
    return np.concatenate([np.asarray(r["out"]) for r in res.results], axis=0).astype(np.float32)


def kernel(**inputs):
    return run(inputs, 8, 2, 4096, 256, 2)
```
